# Optimizing a Trainium2 kernel written in Bass

```python
import jax, jax.numpy as jnp
from jax import lax
import numpy as np

D_MODEL = 1024
BATCH = 2
SEQ = 8192
DEPTH = 2

N_HEADS = 8
HEAD_DIM = 128
ATTN_WIDTH = N_HEADS * HEAD_DIM
ROT_DIM = HEAD_DIM // 4
ROPE_THETA = 500000.0
MOBA_BLOCK = 256
MOBA_TOPK = 3
Q_CHUNK = 32
CONV_WIDTH = D_MODEL
CONV_KERNEL = 31
N_BRANCH = 2
SPLIT_SIZES = (ATTN_WIDTH, ATTN_WIDTH, ATTN_WIDTH, ATTN_WIDTH,
               2 * CONV_WIDTH, CONV_WIDTH, N_BRANCH * D_MODEL)
N_IN = sum(SPLIT_SIZES)
SPLIT_POINTS = tuple(int(v) for v in np.cumsum(SPLIT_SIZES)[:-1])
EPS = 1e-6

kernel_name = "moba_conformer_gated_hybrid"


def rms_norm(t, g):
    tf = t.astype(jnp.float32)
    y = tf * lax.rsqrt(jnp.mean(tf * tf, axis=-1, keepdims=True) + EPS)
    return (y * g.astype(jnp.float32)).astype(t.dtype)


def layer_norm(t, g, b):
    tf = t.astype(jnp.float32)
    mu = jnp.mean(tf, axis=-1, keepdims=True)
    var = jnp.mean(jnp.square(tf - mu), axis=-1, keepdims=True)
    y = (tf - mu) * lax.rsqrt(var + EPS)
    return (y * g.astype(jnp.float32) + b.astype(jnp.float32)).astype(t.dtype)


def rope_tables(seq):
    inv_freq = ROPE_THETA ** (-jnp.arange(0, ROT_DIM, 2, dtype=jnp.float32) / ROT_DIM)
    ang = jnp.arange(seq, dtype=jnp.float32)[:, None] * inv_freq[None, :]
    return jnp.cos(ang), jnp.sin(ang)


def partial_rope(t, cos, sin):
    tf = t.astype(jnp.float32)
    half = ROT_DIM // 2
    t1, t2, rest = tf[..., :half], tf[..., half:ROT_DIM], tf[..., ROT_DIM:]
    out = jnp.concatenate([t1 * cos - t2 * sin, t2 * cos + t1 * sin, rest], axis=-1)
    return out.astype(t.dtype)


def moba_attention(q, k, v):
    B, H, S, D = q.shape
    nb = -(-S // MOBA_BLOCK)
    pad = nb * MOBA_BLOCK - S
    kp = jnp.pad(k, ((0, 0), (0, 0), (0, pad), (0, 0)))
    vp = jnp.pad(v, ((0, 0), (0, 0), (0, pad), (0, 0)))
    kb = kp.reshape(B, H, nb, MOBA_BLOCK, D)
    vb = vp.reshape(B, H, nb, MOBA_BLOCK, D)
    kmean = jnp.mean(kb.astype(jnp.float32), axis=3)
    topk = min(MOBA_TOPK, nb)
    n_chunks = S // Q_CHUNK
    scale = 1.0 / float(np.sqrt(D))
    neg = jnp.finfo(jnp.float32).min
    bi = jnp.arange(B)[:, None, None, None]
    hi = jnp.arange(H)[None, :, None, None]
    qc = q.reshape(B, H, n_chunks, Q_CHUNK, D).transpose(2, 0, 1, 3, 4)

    def chunk(args):
        qi, c = args
        start = c * Q_CHUNK
        blk = start // MOBA_BLOCK
        qpos = start + jnp.arange(Q_CHUNK)
        qf = qi.astype(jnp.float32)
        gate = jnp.einsum('bhqd,bhnd->bhqn', qf, kmean)
        gate = jnp.where(jnp.arange(nb) < blk, gate, neg)
        _, gidx = lax.top_k(gate, topk)
        valid = gidx < blk
        ksel = kb[bi, hi, gidx].astype(jnp.float32)
        vsel = vb[bi, hi, gidx].astype(jnp.float32)
        kown = lax.dynamic_index_in_dim(kb, blk, axis=2, keepdims=False).astype(jnp.float32)
        vown = lax.dynamic_index_in_dim(vb, blk, axis=2, keepdims=False).astype(jnp.float32)
        qs = qf * scale
        l_sel = jnp.einsum('bhqd,bhqtkd->bhqtk', qs, ksel)
        l_sel = jnp.where(valid[..., None], l_sel, neg).reshape(B, H, Q_CHUNK, topk * MOBA_BLOCK)
        kpos = blk * MOBA_BLOCK + jnp.arange(MOBA_BLOCK)
        l_own = jnp.einsum('bhqd,bhkd->bhqk', qs, kown)
        l_own = jnp.where(kpos[None, :] <= qpos[:, None], l_own, neg)
        p = jax.nn.softmax(jnp.concatenate([l_sel, l_own], axis=-1), axis=-1)
        p_sel = p[..., :topk * MOBA_BLOCK].reshape(B, H, Q_CHUNK, topk, MOBA_BLOCK)
        p_own = p[..., topk * MOBA_BLOCK:]
        o = jnp.einsum('bhqtk,bhqtkd->bhqd', p_sel, vsel) + jnp.einsum('bhqk,bhkd->bhqd', p_own, vown)
        return o.astype(q.dtype)

    out = lax.map(chunk, (qc, jnp.arange(n_chunks)))
    return out.transpose(1, 2, 0, 3, 4).reshape(B, H, S, D)


def causal_depthwise_conv(t, w, b):
    C = t.shape[-1]
    y = lax.conv_general_dilated(
        t, w.reshape(CONV_KERNEL, 1, C).astype(t.dtype),
        window_strides=(1,), padding=[(CONV_KERNEL - 1, 0)],
        dimension_numbers=('NWC', 'WIO', 'NWC'), feature_group_count=C)
    return y + b.astype(t.dtype)


def hybrid_layer(x, norm_g, w_in, b_gate, q_norm_g, k_norm_g, conv_w, conv_b,
                 cn_g, cn_b, w_attn_proj, w_conv_proj, w_out, cos, sin):
    B, S, _ = x.shape
    h = rms_norm(x, norm_g)
    proj = h @ w_in
    q, k, v, z_a, u, z_c, gl = jnp.split(proj, SPLIT_POINTS, axis=-1)

    def heads(t):
        return t.reshape(B, S, N_HEADS, HEAD_DIM).transpose(0, 2, 1, 3)

    qh = partial_rope(rms_norm(heads(q), q_norm_g), cos, sin)
    kh = partial_rope(rms_norm(heads(k), k_norm_g), cos, sin)
    o = moba_attention(qh, kh, heads(v)).transpose(0, 2, 1, 3).reshape(B, S, ATTN_WIDTH)
    y_a = (o * jax.nn.silu(z_a)) @ w_attn_proj

    ua, ub = jnp.split(u, 2, axis=-1)
    c = causal_depthwise_conv(ua * jax.nn.sigmoid(ub), conv_w, conv_b)
    c = jax.nn.silu(layer_norm(c, cn_g, cn_b))
    y_c = (c * jax.nn.silu(z_c)) @ w_conv_proj

    g_a, g_c = jnp.split(jax.nn.sigmoid(gl + b_gate), 2, axis=-1)
    return x + (g_a * y_a + g_c * y_c) @ w_out


def setup_inputs(seed: int = 0) -> dict:
    key = jax.random.key(seed)
    ks = jax.random.split(key, 13)
    L = DEPTH
    f32 = jnp.float32
    nrm = lambda k, shape, s: jax.random.normal(k, shape, f32) * s
    return {
        "x": jax.random.normal(ks[0], (BATCH, SEQ, D_MODEL), f32),
        "norm_g": 1.0 + nrm(ks[1], (L, D_MODEL), 0.02),
        "w_in": nrm(ks[2], (L, D_MODEL, N_IN), D_MODEL ** -0.5),
        "b_gate": nrm(ks[3], (L, N_BRANCH * D_MODEL), 0.02),
        "q_norm_g": 1.0 + nrm(ks[4], (L, HEAD_DIM), 0.02),
        "k_norm_g": 1.0 + nrm(ks[5], (L, HEAD_DIM), 0.02),
        "conv_w": nrm(ks[6], (L, CONV_KERNEL, CONV_WIDTH), CONV_KERNEL ** -0.5),
        "conv_b": nrm(ks[7], (L, CONV_WIDTH), 0.02),
        "cn_g": 1.0 + nrm(ks[8], (L, CONV_WIDTH), 0.02),
        "cn_b": nrm(ks[9], (L, CONV_WIDTH), 0.02),
        "w_attn_proj": nrm(ks[10], (L, ATTN_WIDTH, D_MODEL), ATTN_WIDTH ** -0.5),
        "w_conv_proj": nrm(ks[11], (L, CONV_WIDTH, D_MODEL), CONV_WIDTH ** -0.5),
        "w_out": nrm(ks[12], (L, D_MODEL, D_MODEL), D_MODEL ** -0.5),
    }


def reference(x, norm_g, w_in, b_gate, q_norm_g, k_norm_g, conv_w, conv_b,
              cn_g, cn_b, w_attn_proj, w_conv_proj, w_out):
    cos, sin = rope_tables(x.shape[1])
    for l in range(DEPTH):
        x = hybrid_layer(x, norm_g[l], w_in[l], b_gate[l], q_norm_g[l], k_norm_g[l],
                         conv_w[l], conv_b[l], cn_g[l], cn_b[l], w_attn_proj[l],
                         w_conv_proj[l], w_out[l], cos, sin)
    return x
```

```python
import numpy as np
import ml_dtypes
from contextlib import ExitStack
import concourse.bass as bass
import concourse.mybir as mybir
from concourse.bass_utils import run_bass_kernel_spmd

F32 = mybir.dt.float32
BF16 = mybir.dt.bfloat16
AF = mybir.ActivationFunctionType
ALU = mybir.AluOpType
AX = mybir.AxisListType
NPBF = ml_dtypes.bfloat16

D = 1024
SEQ = 8192
NB = 32
BLK = 256
NH = 8
HD = 128
NIN = 9216
TOK = 2048
NT = 16
EPS = 1e-6
NEG = -30000.0
SCALE = 1.0 / float(np.sqrt(HD))
CK = 31
GW = 288

C_Q, C_K, C_V, C_ZA, C_UA, C_UB, C_ZC, C_GA, C_GC = 0, 1024, 2048, 3072, 4096, 5120, 6144, 7168, 8192


class _Op:
    __slots__ = ("eng", "fn", "deps", "dma", "need", "ord", "dsem", "dval", "nobar")


class Prog:
    CE = ("pe", "act", "dve", "pool")
    ALL = ("sp", "pool", "act", "dve", "pe")
    NDS = 8

    def __init__(self, nc):
        self.nc = nc
        self.ops = []
        self.lastw = {}
        self.readers = {}
        self.persist = set()
        self.since_bar = []

    def add(self, eng, fn, reads=(), writes=(), dma=False, nobar=False):
        op = _Op()
        op.eng, op.fn, op.dma, op.need, op.ord, op.dsem, op.dval, op.nobar = eng, fn, dma, False, 0, None, 0, nobar
        deps = {}
        for b in reads:
            w = self.lastw.get(b)
            if w is not None:
                deps[id(w)] = w
        for b in writes:
            w = self.lastw.get(b)
            if w is not None:
                deps[id(w)] = w
            for r in self.readers.get(b, ()):
                deps[id(r)] = r
        for b in writes:
            self.lastw[b] = op
            self.readers[b] = []
        for b in reads:
            self.readers.setdefault(b, []).append(op)
        op.deps = [d for d in deps.values()
                   if d is not op and not (d.eng == "pe" and eng == "pe" and not d.dma and not dma)]
        for d in op.deps:
            d.need = True
        self.ops.append(op)
        if not nobar:
            self.since_bar.append(op)
        return op

    def barrier(self):
        last = {}
        dmas = []
        for op in self.since_bar:
            if op.dma:
                dmas.append(op)
            elif op.fn is not None:
                last[op.eng] = op
        deps = list(last.values()) + dmas
        for d in deps:
            d.need = True
        for e in self.ALL:
            op = _Op()
            op.eng, op.fn, op.dma, op.need, op.ord, op.dsem, op.dval, op.nobar = e, None, False, False, 0, None, 0, False
            op.deps = list(deps)
            self.ops.append(op)
        self.since_bar = []
        self.lastw = {k: v for k, v in self.lastw.items() if k in self.persist}
        self.readers = {k: v for k, v in self.readers.items() if k in self.persist}

    def emit(self, es):
        nc = self.nc
        sems = {e: es.enter_context(nc.semaphore("c_" + e)) for e in self.CE}
        dsems = {q: [es.enter_context(nc.semaphore("d_%s%d" % (q, j))) for j in range(self.NDS)]
                 for q in ("sp", "pool")}
        cnt = {e: 0 for e in self.CE}
        dcnt = {q: 0 for q in dsems}
        duse = {q: [0] * self.NDS for q in dsems}
        prev_on_sem = {}
        for op in self.ops:
            if op.dma:
                q = op.eng
                j = dcnt[q] % self.NDS
                dcnt[q] += 1
                duse[q][j] += 1
                op.dsem = dsems[q][j]
                op.dval = 16 * duse[q][j]
            elif op.fn is not None and op.need:
                cnt[op.eng] += 1
                op.ord = cnt[op.eng]
        block = es.enter_context(nc.Block())
        bname = {"sp": "sync", "pool": "gpsimd", "act": "scalar", "dve": "vector", "pe": "tensor"}
        ninst = [0]
        for e in self.ALL:
            ops_e = [op for op in self.ops if op.eng == e]

            def body(E, ops_e=ops_e, e=e):
                waited = {}
                for op in ops_e:
                    w = {}
                    for d in op.deps:
                        if d.dma:
                            s, v = d.dsem, d.dval
                        else:
                            s, v = sems[d.eng], d.ord
                        k = id(s)
                        if k not in w or w[k][1] < v:
                            w[k] = (s, v)
                    if op.dma and op.dval > 16:
                        k = id(op.dsem)
                        v = op.dval - 16
                        if k not in w or w[k][1] < v:
                            w[k] = (op.dsem, v)
                    for k, (s, v) in w.items():
                        if waited.get(k, 0) < v:
                            E.wait_ge(s, v)
                            waited[k] = v
                            ninst[0] += 1
                    if op.fn is not None:
                        ins = op.fn(E)
                        ninst[0] += 1
                        if op.dma:
                            ins.then_inc(op.dsem, 16)
                        elif op.need:
                            ins.then_inc(sems[e], 1)

            getattr(block, bname[e])(body)
        self.ninst = ninst[0]


def _ap(t, off, dims, parts=128):
    ps = 1
    for s in list(t.shape)[1:]:
        ps *= int(s)
    return bass.AP(t, off, [[ps, parts]] + [list(d) for d in dims])


def build(mode, upto=99):
    nc = bass.Bass("TRN2", target_bir_lowering=False)
    es = ExitStack()
    P = Prog(nc)

    def din(name, shape, dt=F32):
        return nc.dram_tensor(name, list(shape), dt, kind="ExternalInput").ap()

    def dout(name, shape, dt=F32):
        return nc.dram_tensor(name, list(shape), dt, kind="ExternalOutput").ap()

    uniq = [0]

    def sb(name, shape, dt=F32, stack=None):
        uniq[0] += 1
        return (stack or es).enter_context(nc.sbuf_tensor("%s_%d" % (name, uniq[0]), list(shape), dt))

    x_d = din("x", [TOK, D])
    win_d = din("w_in", [D, NIN])
    ng_d = din("ng", [128, 8])
    cs_d = din("cs", [128, NT, 32])
    qkg_d = din("qkg", [128, 2, 128])
    win_v = win_d.rearrange("(k p) c -> p k c", p=128)

    if mode == "A":
        kt_o = dout("kt_o", [128, NH, TOK], BF16)
        v_o = dout("v_o", [128, NT, D], BF16)
        km_o = dout("km_o", [128, 64])
        gt_o = dout("gt_o", [128, 8, 256])
    else:
        kt_all = din("kt_all", [NH, 128, SEQ], BF16)
        v_all = din("v_all", [NH, 128, 64, 128], BF16)
        kt_own = din("kt_own", [NH, 128, TOK], BF16)
        v_own = din("v_own", [NH, 128, NT, HD], BF16)
        km_all = din("km_all", [128, NH, NB])
        halo_d = din("halo", [128, 8, 256])
        gmask_d = din("gmask", [128, 8, NB])
        tri_d = din("tri", [128, 2, 256])
        bg_d = din("bgate", [128, 16])
        cw_d = din("convw", [128, 8, CK])
        cv_d = din("cvec", [128, 3, 8])
        wap_d = din("w_ap", [D, D])
        wcp_d = din("w_cp", [D, D])
        wo_d = din("w_o", [D, D])
        y_d = dout("y", [TOK, D])
        if upto < 99:
            dbgA = dout("dbgA", [128, 8 * 8 * GW], BF16)
            dbgG = dout("dbgG", [128, 8, TOK], BF16)
        gsc = nc.dram_tensor("gsc", [128, 8, TOK], BF16, kind="Internal").ap()
        wap_v = wap_d.rearrange("(k p) c -> p k c", p=128)
        wcp_v = wcp_d.rearrange("(k p) c -> p k c", p=128)
        wo_v = wo_d.rearrange("(k p) c -> p k c", p=128)

    pb = [es.enter_context(nc.psum_tensor("pb%d" % i, [128, 512], F32)) for i in range(8)]
    pbf = [pb[i][:] for i in range(8)]
    pbh = [pb[i][:].bitcast(BF16) for i in range(8)]

    identf = sb("identf", [128, 128], F32)
    ident = sb("ident", [128, 128], BF16)
    ng = sb("ng", [128, 8])
    cs = sb("cs", [128, NT, 32])
    qkg = sb("qkg", [128, 2, 128])
    NST, NBF = 2, 2
    wst = [sb("wst%d" % i, [128, 8, 512], F32) for i in range(NST)]
    wbf = [sb("wbf%d" % i, [128, 8, 512], BF16) for i in range(NBF)]
    bufA = sb("bufA", [128, 8 * 8 * GW], BF16)
    for i in range(NST):
        P.persist.add(("wst", i))
    for i in range(NBF):
        P.persist.add(("wbf", i))

    def qT(h, t0, n):
        return _ap(bufA, h * TOK + t0, [[1, n]])

    P.add("pool", lambda E: E.memset(identf[:], 0.0), writes=[("identf",)])
    P.add("pool", lambda E: E.affine_select(out=identf[:], in_=identf[:], compare_op=ALU.not_equal, fill=1.0,
                                            base=0, pattern=[[-1, 128]], channel_multiplier=1),
          reads=[("identf",)], writes=[("identf",)])
    P.add("pool", lambda E: E.tensor_copy(out=ident[:], in_=identf[:]), reads=[("identf",)], writes=[("ident",)])
    P.add("sp", lambda E: E.dma_start(out=ng[:], in_=ng_d), writes=[("ng",)], dma=True)
    P.add("sp", lambda E: E.dma_start(out=cs[:], in_=cs_d), writes=[("cs",)], dma=True)
    P.add("sp", lambda E: E.dma_start(out=qkg[:], in_=qkg_d), writes=[("qkg",)], dma=True)
    epsc = sb("epsc", [128, 1])
    P.add("pool", lambda E: E.memset(epsc[:], float(EPS)), writes=[("epsc",)])

    if mode == "A":
        wlist = [("in", C_K), ("in", C_K + 512), ("in", C_V), ("in", C_V + 512),
                 ("in", C_UA), ("in", C_UB), ("in", C_UA + 512), ("in", C_UB + 512)]
    else:
        wlist = [("in", C_UA), ("in", C_UB), ("in", C_UA + 512), ("in", C_UB + 512),
                 ("in", C_ZC), ("in", C_ZC + 512), ("in", C_GC), ("in", C_GC + 512),
                 ("cp", 0), ("cp", 512), ("in", C_Q), ("in", C_Q + 512),
                 ("in", C_ZA), ("in", C_ZA + 512), ("in", C_GA), ("in", C_GA + 512),
                 ("ap", 0), ("ap", 512), ("o", 0), ("o", 512)]
    wstate = {"emitted": 0, "next": 0}

    def w_emit(j):
        kind, c0 = wlist[j]
        src = {"in": win_v, "cp": None, "ap": None, "o": None}
        if kind == "in":
            s_ap = win_v[:, :, c0:c0 + 512]
        elif kind == "cp":
            s_ap = wcp_v[:, :, c0:c0 + 512]
        elif kind == "ap":
            s_ap = wap_v[:, :, c0:c0 + 512]
        else:
            s_ap = wo_v[:, :, c0:c0 + 512]
        st, bf = j % NST, j % NBF
        P.add("sp", lambda E: E.dma_start(out=wst[st][:], in_=s_ap), writes=[("wst", st)], dma=True, nobar=True)
        if kind == "in":
            ngb = _ap(ng, 0, [[1, 8], [0, 512]])
            P.add("pool", lambda E: E.tensor_tensor(out=wbf[bf][:], in0=wst[st][:], in1=ngb, op=ALU.mult),
                  reads=[("wst", st), ("ng",)], writes=[("wbf", bf)], nobar=True)
        else:
            P.add("pool", lambda E: E.tensor_copy(out=wbf[bf][:], in_=wst[st][:]),
                  reads=[("wst", st)], writes=[("wbf", bf)], nobar=True)

    def w_next(hold=False):
        j = wstate["next"]
        wstate["next"] += 1
        while wstate["emitted"] < min(len(wlist), j + (1 if hold else 2)):
            w_emit(wstate["emitted"])
            wstate["emitted"] += 1
        return wbf[j % NBF], ("wbf", j % NBF)

    def stage_hT(hT, st):
        xt = [sb("xt%d" % i, [128, D], F32, st) for i in range(2)]
        sqj = sb("sqj", [128, D], BF16, st)
        xn = [sb("xn%d" % i, [128, D], BF16, st) for i in range(2)]
        ss = sb("ss", [128, NT], F32, st)
        rs_ = sb("rs_", [128, NT], F32, st)
        for tt in range(NT):
            b = tt % 2
            P.add("sp", lambda E, tt=tt, b=b: E.dma_start(out=xt[b][:], in_=x_d[tt * 128:(tt + 1) * 128, :]),
                  writes=[("xt", b)], dma=True)
            P.add("act", lambda E, tt=tt, b=b: E.activation(out=sqj[:], in_=xt[b][:], func=AF.Square,
                                                            accum_out=ss[:, tt:tt + 1]),
                  reads=[("xt", b)], writes=[("sqj",), ("ss", tt)])
            P.add("act", lambda E, tt=tt: E.activation(out=rs_[:, tt:tt + 1], in_=ss[:, tt:tt + 1], func=AF.Sqrt,
                                                       bias=epsc[:, 0:1], scale=1.0 / D),
                  reads=[("ss", tt), ("epsc",)], writes=[("rs_", tt)])
            P.add("dve", lambda E, tt=tt: E.reciprocal(out=rs_[:, tt:tt + 1], in_=rs_[:, tt:tt + 1]),
                  reads=[("rs_", tt)], writes=[("rs_", tt)])
            P.add("dve", lambda E, tt=tt, b=b: E.tensor_scalar(out=xn[b][:], in0=xt[b][:], scalar1=rs_[:, tt:tt + 1],
                                                               scalar2=None, op0=ALU.mult),
                  reads=[("xt", b), ("rs_", tt)], writes=[("xn", b)])
            bank = 4 + b
            for c in range(8):
                P.add("pe", lambda E, c=c, b=b, bank=bank: E.transpose(out=pbh[bank][:, c * 128:(c + 1) * 128],
                                                                       in_=xn[b][:, c * 128:(c + 1) * 128],
                                                                       identity=ident[:]),
                      reads=[("xn", b), ("ident",)], writes=[("pb", bank)])
            P.add("act", lambda E, tt=tt, bank=bank: E.activation(
                out=_ap(hT, tt * 128, [[TOK, 8], [1, 128]]),
                in_=pbh[bank].rearrange("p (c t) -> p c t", c=8), func=AF.Copy),
                reads=[("pb", bank)], writes=[("hT", tt)])

    def stage_qk(hT, st, which, dst_fn, dst_key):
        sqk = sb("sqk", [128, 512], F32, st)
        ssk = sb("ssk", [128, 4], F32, st)
        rk = sb("rk", [128, 4], F32, st)
        kn = sb("kn", [128, 512], F32, st)
        kb = [sb("kb%d" % i, [128, 512], BF16, st) for i in range(2)]
        rt = [sb("rt%d" % i, [128, 4, 16], F32, st) for i in range(4)]
        gvec = _ap(qkg, which * 128, [[0, 4], [1, 128]])
        it = 0
        for kg in range(2):
            wt, wkey = w_next()
            for tt in range(NT):
                bank = 1 + it % 3
                for kc in range(8):
                    P.add("pe", lambda E, kc=kc, tt=tt, bank=bank, wt=wt: E.matmul(
                        pbf[bank], lhsT=hT[:, kc, tt * 128:(tt + 1) * 128], rhs=wt[:, kc, :],
                        start=(kc == 0), stop=(kc == 7)),
                        reads=[("hT", tt), wkey], writes=[("pb", bank)])
                ps3 = pbf[bank].rearrange("p (h d) -> p h d", h=4)
                P.add("act", lambda E, bank=bank: E.activation(out=sqk[:], in_=pbf[bank], func=AF.Square),
                      reads=[("pb", bank)], writes=[("sqk",)])
                P.add("dve", lambda E: E.tensor_reduce(out=ssk[:], in_=sqk[:].rearrange("p (h d) -> p h d", h=4),
                                                       axis=AX.X, op=ALU.add),
                      reads=[("sqk",)], writes=[("ssk",)])
                P.add("act", lambda E: E.activation(out=rk[:], in_=ssk[:], func=AF.Sqrt, bias=epsc[:, 0:1], scale=1.0 / HD),
                      reads=[("ssk",), ("epsc",)], writes=[("rk",)])
                P.add("dve", lambda E: E.reciprocal(out=rk[:], in_=rk[:]), reads=[("rk",)], writes=[("rk",)])
                kn3 = kn[:].rearrange("p (h d) -> p h d", h=4)
                P.add("dve", lambda E, ps3=ps3, kn3=kn3: E.tensor_tensor(
                    out=kn3, in0=ps3, in1=_ap(rk, 0, [[1, 4], [0, 128]]), op=ALU.mult),
                    reads=[("pb", bank), ("rk",)], writes=[("kn",)])
                P.add("dve", lambda E, kn3=kn3: E.tensor_tensor(out=kn3, in0=kn3, in1=gvec, op=ALU.mult),
                      reads=[("kn",), ("qkg",)], writes=[("kn",)])
                kbb = kb[it % 2]
                kbk = ("kb", it % 2)
                P.add("act", lambda E, kbb=kbb: E.activation(out=kbb[:], in_=kn[:], func=AF.Copy),
                      reads=[("kn",)], writes=[kbk])
                t1 = _ap(kn, 0, [[128, 4], [1, 16]])
                t2 = _ap(kn, 16, [[128, 4], [1, 16]])
                cosb = _ap(cs, tt * 32, [[0, 4], [1, 16]])
                sinb = _ap(cs, tt * 32 + 16, [[0, 4], [1, 16]])
                o1 = _ap(kbb, 0, [[128, 4], [1, 16]])
                o2 = _ap(kbb, 16, [[128, 4], [1, 16]])
                P.add("dve", lambda E, t1=t1, cosb=cosb: E.tensor_tensor(out=rt[0][:], in0=t1, in1=cosb, op=ALU.mult),
                      reads=[("kn",), ("cs",)], writes=[("rt", 0)])
                P.add("dve", lambda E, t2=t2, sinb=sinb: E.tensor_tensor(out=rt[1][:], in0=t2, in1=sinb, op=ALU.mult),
                      reads=[("kn",), ("cs",)], writes=[("rt", 1)])
                P.add("dve", lambda E, t2=t2, cosb=cosb: E.tensor_tensor(out=rt[2][:], in0=t2, in1=cosb, op=ALU.mult),
                      reads=[("kn",), ("cs",)], writes=[("rt", 2)])
                P.add("dve", lambda E, t1=t1, sinb=sinb: E.tensor_tensor(out=rt[3][:], in0=t1, in1=sinb, op=ALU.mult),
                      reads=[("kn",), ("cs",)], writes=[("rt", 3)])
                P.add("dve", lambda E, o1=o1: E.tensor_tensor(out=o1, in0=rt[0][:], in1=rt[1][:], op=ALU.subtract),
                      reads=[("rt", 0), ("rt", 1), kbk], writes=[kbk])
                P.add("dve", lambda E, o2=o2: E.tensor_tensor(out=o2, in0=rt[2][:], in1=rt[3][:], op=ALU.add),
                      reads=[("rt", 2), ("rt", 3), kbk], writes=[kbk])
                tb = 5 + it % 2
                for h in range(4):
                    P.add("pe", lambda E, h=h, tb=tb, kbb=kbb: E.transpose(
                        out=pbh[tb][:, h * 128:(h + 1) * 128], in_=kbb[:, h * 128:(h + 1) * 128], identity=ident[:]),
                        reads=[kbk, ("ident",)], writes=[("pb", tb)])
                P.add("act", lambda E, tb=tb, kg=kg, tt=tt: E.activation(
                    out=dst_fn(kg * 4, tt), in_=pbh[tb][:, 0:512].rearrange("p (h t) -> p h t", h=4), func=AF.Copy),
                    reads=[("pb", tb)], writes=[(dst_key, kg * 4 + h, tt) for h in range(4)])
                it += 1

    if mode == "A":
        hT = sb("hT", [128, 8, TOK], BF16)
        ktT = sb("ktT", [128, NH, TOK], BF16)
        vown = sb("vown", [128, NT, D], BF16)
        st1 = ExitStack()
        stage_hT(hT, st1)
        P.barrier()
        st1.close()
        st2 = ExitStack()
        stage_qk(hT, st2, 1, lambda h0, tt: _ap(ktT, h0 * TOK + tt * 128, [[TOK, 4], [1, 128]]), "kt")
        it = 0
        for vg in range(2):
            wt, wkey = w_next()
            for tt in range(NT):
                bank = 1 + it % 3
                for kc in range(8):
                    P.add("pe", lambda E, kc=kc, tt=tt, bank=bank, wt=wt: E.matmul(
                        pbf[bank], lhsT=hT[:, kc, tt * 128:(tt + 1) * 128], rhs=wt[:, kc, :],
                        start=(kc == 0), stop=(kc == 7)),
                        reads=[("hT", tt), wkey], writes=[("pb", bank)])
                P.add("act", lambda E, bank=bank, tt=tt, vg=vg: E.activation(
                    out=vown[:, tt, vg * 512:(vg + 1) * 512], in_=pbf[bank], func=AF.Copy),
                    reads=[("pb", bank)], writes=[("vown", tt, vg)])
                it += 1
        kms = sb("kms", [128, 64], F32, st2)
        P.add("dve", lambda E: E.tensor_reduce(out=kms[:], in_=ktT[:].rearrange("p h (i t) -> p (h i) t", t=BLK),
                                               axis=AX.X, op=ALU.add),
              reads=[("kt", h, tt) for h in range(NH) for tt in range(NT)], writes=[("kms",)])
        P.add("dve", lambda E: E.tensor_scalar(out=kms[:], in0=kms[:], scalar1=1.0 / BLK, scalar2=None, op0=ALU.mult),
              reads=[("kms",)], writes=[("kms",)])
        P.add("pool", lambda E: E.dma_start(out=km_o, in_=kms[:]), reads=[("kms",)], writes=[("o_km",)], dma=True)
        P.add("pool", lambda E: E.dma_start(out=kt_o, in_=ktT[:]),
              reads=[("kt", h, tt) for h in range(NH) for tt in range(NT)], writes=[("o_kt",)], dma=True)
        P.add("pool", lambda E: E.dma_start(out=v_o, in_=vown[:]),
              reads=[("vown", tt, vg) for tt in range(NT) for vg in range(2)], writes=[("o_v",)], dma=True)
        sg = sb("sg", [128, 256], F32, st2)
        gtl = sb("gtl", [128, 8, 256], F32, st2)
        for g in range(2):
            wa, wakey = w_next()
            wb, wbkey = w_next(hold=True)
            for cc in range(4):
                c = 4 * g + cc
                for (bank, wt, wkey) in ((1, wa, wakey), (2, wb, wbkey)):
                    for kc in range(8):
                        P.add("pe", lambda E, kc=kc, cc=cc, bank=bank, wt=wt: E.matmul(
                            pbf[bank][:, 0:256], lhsT=wt[:, kc, cc * 128:(cc + 1) * 128],
                            rhs=_ap(hT, kc * TOK + 224, [[BLK, 8], [1, 32]]),
                            start=(kc == 0), stop=(kc == 7)),
                            reads=[("hT", tt) for tt in range(NT)] + [wkey], writes=[("pb", bank)])
                P.add("act", lambda E: E.activation(out=sg[:], in_=pbf[2][:, 0:256], func=AF.Sigmoid),
                      reads=[("pb", 2)], writes=[("sg",)])
                P.add("dve", lambda E, c=c: E.tensor_tensor(out=gtl[:, c, :], in0=pbf[1][:, 0:256], in1=sg[:], op=ALU.mult),
                      reads=[("pb", 1), ("sg",)], writes=[("gtl", c)])
        P.add("pool", lambda E: E.dma_start(out=gt_o, in_=gtl[:]), reads=[("gtl", c) for c in range(8)],
              writes=[("o_gt",)], dma=True)
        P.add("sp", None, reads=[("o_km",), ("o_kt",), ("o_v",), ("o_gt",)])
        P.barrier()
        P.emit(es)
        st2.close()
        es.close()
        return nc, P

    gmask = sb("gmask", [128, 8, NB])
    tri = sb("tri", [128, 2, 256])
    bgate = sb("bgate", [128, 16])
    cwT = sb("cwT", [128, 8, CK])
    cvec = sb("cvec", [128, 3, 8])
    kmf = sb("kmf", [128, NH, NB])
    kmb = sb("kmb", [128, NH, NB], BF16)
    ones_s = sb("ones_s", [128, 128], BF16)
    for (t_, d_, k_) in ((gmask, gmask_d, "gmask"), (tri, tri_d, "tri"), (bgate, bg_d, "bgate"),
                         (cwT, cw_d, "cwT"), (cvec, cv_d, "cvec"), (kmf, km_all, "kmf")):
        P.add("sp", lambda E, t_=t_, d_=d_: E.dma_start(out=t_[:], in_=d_), writes=[(k_,)], dma=True)
    P.add("dve", lambda E: E.tensor_copy(out=kmb[:], in_=kmf[:]), reads=[("kmf",)], writes=[("kmb",)])
    P.add("pool", lambda E: E.memset(ones_s[:], 1.0 / D), writes=[("ones_s",)])
    P.barrier()

    def gT(c, blk0, nblk, off, n):
        return _ap(bufA, c * 8 * GW + blk0 * GW + off, [[GW, nblk], [1, n]])

    def finish(gc=None):
        P.add("pool", lambda E: E.dma_start(out=dbgA, in_=bufA[:]), writes=[("o_dbgA",)], dma=True)
        if gc is not None:
            P.add("pool", lambda E: E.dma_start(out=dbgG, in_=gc[:]), writes=[("o_dbgG",)], dma=True)
        P.add("sp", None, reads=[("o_dbgA",), ("o_dbgG",)])
        P.barrier()
        P.emit(es)
        return nc, P

    gcyc_stack = ExitStack()
    hT_stack = ExitStack()
    hT = sb("hT", [128, 8, TOK], BF16, hT_stack)
    st1 = ExitStack()
    stage_hT(hT, st1)
    P.barrier()
    st1.close()

    def proj_fm(wt, wkey, cc, T, bank, rhs_fn=None, rkeys=None):
        for kc in range(8):
            rhs = hT[:, kc, T * 512:(T + 1) * 512] if rhs_fn is None else rhs_fn(kc)
            rk_ = [("hT", 4 * T + j) for j in range(4)] if rkeys is None else rkeys(kc)
            o_ = pbf[bank] if len(rhs.shape) == 2 else pbf[bank].rearrange("p (b t) -> p b t", b=rhs.shape[1])
            P.add("pe", lambda E, kc=kc, rhs=rhs, o_=o_: E.matmul(o_, lhsT=wt[:, kc, cc * 128:(cc + 1) * 128], rhs=rhs,
                                                                  start=(kc == 0), stop=(kc == 7)),
                  reads=rk_ + [wkey], writes=[("pb", bank)])

    st3 = ExitStack()
    sg = [sb("sg%d" % i, [128, 512], F32, st3) for i in range(2)]
    hl = sb("hl", [128, 8, 256], F32, st3)
    P.add("sp", lambda E: E.dma_start(out=hl[:], in_=halo_d), writes=[("hl",)], dma=True)
    P.add("dve", lambda E: E.tensor_copy(
        out=_ap(bufA, 0, [[GW, 64], [1, 32]]), in_=hl[:].rearrange("p c (i t) -> p (c i) t", t=32)),
        reads=[("hl",)], writes=[("gh",)])
    it = 0
    for g in range(2):
        wa, wakey = w_next()
        wb_, wbkey = w_next(hold=True)
        for cc in range(4):
            c = 4 * g + cc
            for T in range(4):
                ba, bb = 1 + (it % 2) * 2, 2 + (it % 2) * 2
                proj_fm(wa, wakey, cc, T, ba)
                proj_fm(wb_, wbkey, cc, T, bb)
                s_ = sg[it % 2]
                P.add("act", lambda E, s_=s_, bb=bb: E.activation(out=s_[:], in_=pbf[bb], func=AF.Sigmoid),
                      reads=[("pb", bb)], writes=[("sg", it % 2)])
                P.add("dve", lambda E, s_=s_, ba=ba, c=c, T=T: E.tensor_tensor(
                    out=gT(c, 2 * T, 2, 32, 256), in0=pbf[ba].rearrange("p (b t) -> p b t", b=2),
                    in1=s_[:].rearrange("p (b t) -> p b t", b=2), op=ALU.mult),
                    reads=[("pb", ba), ("sg", it % 2)], writes=[("g", c, T)])
                it += 1
    P.barrier()
    st3.close()
    if upto == 3:
        return finish()

    st4 = ExitStack()
    gcyc = sb("gcyc", [128, 8, TOK], BF16, gcyc_stack)
    cpre = sb("cpre", [128, 8, 512], F32, st4)
    cb = [sb("cb%d" % i, [128, 512], BF16, st4) for i in range(2)]
    csq = [sb("csq%d" % i, [128, 512], BF16, st4) for i in range(2)]
    mean = sb("mean", [128, 512], F32, st4)
    msq = sb("msq", [128, 512], F32, st4)
    rstd = sb("rstd", [128, 512], F32, st4)
    dw = sb("dw", [128, CK, 128], BF16, st4)
    sz = [sb("sz%d" % i, [128, 512], F32, st4) for i in range(2)]
    it = 0
    for T in range(4):
        for c in range(8):
            P.add("dve", lambda E, c=c: E.tensor_tensor(
                out=dw[:], in0=_ap(identf, 0, [[0, CK], [1, 128]]), in1=_ap(cwT, c * CK, [[1, CK], [0, 128]]),
                op=ALU.mult), reads=[("identf",), ("cwT",)], writes=[("dw",)])
            bank = 1 + it % 2
            for j in range(CK):
                P.add("pe", lambda E, j=j, c=c, T=T, bank=bank: E.matmul(
                    pbf[bank].rearrange("p (b t) -> p b t", b=2), lhsT=dw[:, j, :], rhs=gT(c, 2 * T, 2, 2 + j, 256),
                    start=(j == 0), stop=(j == CK - 1)),
                    reads=[("dw",), ("g", c, T), ("gh",)], writes=[("pb", bank)])
            P.add("act", lambda E, c=c, bank=bank: E.activation(out=cpre[:, c, :], in_=pbf[bank], func=AF.Identity,
                                                                bias=cvec[:, 0, c:c + 1]),
                  reads=[("pb", bank), ("cvec",)], writes=[("cpre", c)])
            P.add("act", lambda E, c=c, bank=bank, k=it % 2: E.activation(out=csq[k][:], in_=pbf[bank], func=AF.Square,
                                                                bias=cvec[:, 0, c:c + 1]),
                  reads=[("pb", bank), ("cvec",)], writes=[("csq", it % 2)])
            P.add("dve", lambda E, c=c, k=it % 2: E.tensor_copy(out=cb[k][:], in_=cpre[:, c, :]),
                  reads=[("cpre", c)], writes=[("cb", it % 2)])
            P.add("pe", lambda E, c=c, k=it % 2: E.matmul(pbf[3], lhsT=ones_s[:], rhs=cb[k][:], start=(c == 0), stop=(c == 7)),
                  reads=[("ones_s",), ("cb", it % 2)], writes=[("pb", 3)])
            P.add("pe", lambda E, c=c, k=it % 2: E.matmul(pbf[4], lhsT=ones_s[:], rhs=csq[k][:], start=(c == 0), stop=(c == 7)),
                  reads=[("ones_s",), ("csq", it % 2)], writes=[("pb", 4)])
            it += 1
        P.add("dve", lambda E: E.tensor_copy(out=mean[:], in_=pbf[3]), reads=[("pb", 3)], writes=[("mean",)])
        P.add("dve", lambda E: E.tensor_tensor(out=msq[:], in0=mean[:], in1=mean[:], op=ALU.mult),
              reads=[("mean",)], writes=[("msq",)])
        P.add("dve", lambda E: E.tensor_tensor(out=rstd[:], in0=pbf[4], in1=msq[:], op=ALU.subtract),
              reads=[("pb", 4), ("msq",)], writes=[("rstd",)])
        P.add("act", lambda E: E.activation(out=rstd[:], in_=rstd[:], func=AF.Sqrt, bias=epsc[:, 0:1]),
              reads=[("rstd",), ("epsc",)], writes=[("rstd",)])
        P.add("dve", lambda E: E.reciprocal(out=rstd[:], in_=rstd[:]), reads=[("rstd",)], writes=[("rstd",)])
        for c in range(8):
            P.add("dve", lambda E, c=c: E.tensor_tensor(out=cpre[:, c, :], in0=cpre[:, c, :], in1=mean[:], op=ALU.subtract),
                  reads=[("cpre", c), ("mean",)], writes=[("cpre", c)])
            P.add("dve", lambda E, c=c: E.tensor_tensor(out=cpre[:, c, :], in0=cpre[:, c, :], in1=rstd[:], op=ALU.mult),
                  reads=[("cpre", c), ("rstd",)], writes=[("cpre", c)])
            P.add("act", lambda E, c=c, T=T: E.activation(
                out=gT(c, 2 * T, 2, 32, 256), in_=cpre[:, c, :].rearrange("p (b t) -> p b t", b=2), func=AF.Silu,
                bias=cvec[:, 2, c:c + 1], scale=cvec[:, 1, c:c + 1]),
                reads=[("cpre", c), ("cvec",)], writes=[("g", c, T)])
    it = 0
    for g in range(2):
        wt, wkey = w_next()
        for cc in range(4):
            c = 4 * g + cc
            for T in range(4):
                bank = 5 + it % 2
                proj_fm(wt, wkey, cc, T, bank)
                s_ = sz[it % 2]
                P.add("act", lambda E, s_=s_, bank=bank: E.activation(out=s_[:], in_=pbf[bank], func=AF.Silu),
                      reads=[("pb", bank)], writes=[("sz", it % 2)])
                P.add("dve", lambda E, s_=s_, c=c, T=T: E.tensor_tensor(
                    out=gT(c, 2 * T, 2, 32, 256), in0=gT(c, 2 * T, 2, 32, 256),
                    in1=s_[:].rearrange("p (b t) -> p b t", b=2), op=ALU.mult),
                    reads=[("g", c, T), ("sz", it % 2)], writes=[("g", c, T)])
                it += 1
    for g in range(2):
        wt, wkey = w_next()
        for cc in range(4):
            c = 4 * g + cc
            for T in range(4):
                bank = 5 + it % 2
                proj_fm(wt, wkey, cc, T, bank)
                P.add("act", lambda E, bank=bank, c=c, T=T: E.activation(
                    out=gcyc[:, c, T * 512:(T + 1) * 512], in_=pbf[bank], func=AF.Sigmoid, bias=bgate[:, 8 + c:9 + c]),
                    reads=[("pb", bank), ("bgate",)], writes=[("gc", c, T)])
                it += 1
    for g in range(2):
        wt, wkey = w_next()
        for cc in range(4):
            c = 4 * g + cc
            for T in range(4):
                bank = 5 + it % 2
                proj_fm(wt, wkey, cc, T, bank, rhs_fn=lambda kc, T=T: gT(kc, 2 * T, 2, 32, 256),
                        rkeys=lambda kc, T=T: [("g", kc, T)])
                P.add("dve", lambda E, bank=bank, c=c, T=T: E.tensor_tensor(
                    out=gcyc[:, c, T * 512:(T + 1) * 512], in0=pbf[bank], in1=gcyc[:, c, T * 512:(T + 1) * 512],
                    op=ALU.mult), reads=[("pb", bank), ("gc", c, T)], writes=[("gc", c, T)])
                it += 1
    P.add("pool", lambda E: E.dma_start(out=gsc, in_=gcyc[:]),
          reads=[("gc", c, T) for c in range(8) for T in range(4)], writes=[("gsc",)], dma=True)
    P.barrier()
    if upto == 4:
        return finish(gcyc)
    st4.close()
    gcyc_stack.close()

    st5 = ExitStack()
    stage_qk(hT, st5, 0, lambda h0, tt: _ap(bufA, h0 * TOK + tt * 128, [[TOK, 4], [1, 128]]), "q")
    P.barrier()
    st5.close()
    hT_stack.close()
    if upto == 5:
        return finish()

    st6 = ExitStack()
    kth = [sb("kth%d" % i, [128, SEQ], BF16, st6) for i in range(2)]
    vh = [sb("vh%d" % i, [128, 64, 128], BF16, st6) for i in range(2)]
    kto = [sb("kto%d" % i, [128, TOK], BF16, st6) for i in range(2)]
    vo = [sb("vo%d" % i, [128, NT, 128], BF16, st6) for i in range(2)]
    gm = sb("gm", [128, NB], F32, st6)
    m8 = sb("m8", [128, 8], F32, st6)
    thr = sb("thr", [128, 1], F32, st6)
    biasb = [sb("biasb%d" % i, [128, NB], F32, st6) for i in range(2)]
    rsb = [sb("rsb%d" % i, [128, 40], F32, st6) for i in range(2)]
    Pb = [sb("Pb%d" % i, [128, 512], BF16, st6) for i in range(3)]
    PTb = [sb("PTb%d" % i, [128, 512], BF16, st6) for i in range(3)]
    scm = [sb("scm%d" % i, [128, 256], F32, st6) for i in range(2)]
    rsum = sb("rsum", [128, 1], F32, st6)
    rinv = sb("rinv", [128, 1], F32, st6)
    obf = [sb("obf%d" % i, [128, 128], BF16, st6) for i in range(2)]

    def load_head(h):
        hb = h % 2
        P.add("sp", lambda E: E.dma_start(out=kth[hb][:], in_=kt_all[h]), writes=[("kth", hb)], dma=True)
        P.add("sp", lambda E: E.dma_start(out=vh[hb][:], in_=v_all[h]), writes=[("vh", hb)], dma=True)
        P.add("sp", lambda E: E.dma_start(out=kto[hb][:], in_=kt_own[h]), writes=[("kto", hb)], dma=True)
        P.add("sp", lambda E: E.dma_start(out=vo[hb][:], in_=v_own[h]), writes=[("vo", hb)], dma=True)

    load_head(0)
    sidx = [0]
    for h in range(NH):
        hb = h % 2
        if h + 1 < NH:
            load_head(h + 1)
        items = []
        for qt in range(NT):
            i, half = qt // 2, qt % 2
            nb = 4 * i + 3
            col = 0
            ng_ = (nb + 1) // 2
            for g in range(ng_):
                nblk = min(2, nb - 2 * g)
                items.append(dict(qt=qt, kind="g", g=g, nblk=nblk, ncols=256 * nblk, first=(g == 0), last=False, col=col))
                col += nblk
            items.append(dict(qt=qt, kind="d", ncols=128 * (half + 1), first=False, last=True, col=col))
        for it_ in items:
            it_["s"] = sidx[0]
            sidx[0] += 1

        def do_qk(itm):
            qt, s = itm["qt"], itm["s"]
            i, half = qt // 2, qt % 2
            qslice = qT(h, qt * 128, 128)
            kmh = kmb[:, h, :]
            if itm["first"]:
                P.add("pe", lambda E: E.matmul(pbf[0][:, 0:NB], lhsT=qslice, rhs=kmh, start=True, stop=True),
                      reads=[("q", h, qt), ("kmb",)], writes=[("pb", 0)])
                P.add("dve", lambda E: E.tensor_tensor(out=gm[:], in0=pbf[0][:, 0:NB], in1=gmask[:, i, :], op=ALU.add),
                      reads=[("pb", 0), ("gmask",)], writes=[("gm",)])
                P.add("dve", lambda E: E.max(out=m8[:], in_=gm[:]), reads=[("gm",)], writes=[("m8",)])
                P.add("dve", lambda E: E.tensor_scalar(out=thr[:], in0=m8[:, 2:3], scalar1=-1e30, scalar2=None, op0=ALU.max),
                      reads=[("m8",)], writes=[("thr",)])
                P.add("dve", lambda E: E.tensor_scalar(out=biasb[qt % 2][:], in0=gm[:], scalar1=thr[:, 0:1], scalar2=NEG,
                                                       op0=ALU.is_lt, op1=ALU.mult),
                      reads=[("gm",), ("thr",)], writes=[("biasb", qt % 2)])
            bank = 1 + s % 3
            nc_ = itm["ncols"]
            if itm["kind"] == "g":
                rhs = kth[hb][:, 512 * itm["g"]:512 * itm["g"] + nc_]
                rk_ = ("kth", hb)
            else:
                rhs = kto[hb][:, i * BLK:i * BLK + nc_]
                rk_ = ("kto", hb)
            P.add("pe", lambda E: E.matmul(pbf[bank][:, 0:nc_], lhsT=qslice, rhs=rhs, start=True, stop=True),
                  reads=[("q", h, qt), rk_], writes=[("pb", bank)])

        def do_exp(itm):
            qt, s = itm["qt"], itm["s"]
            half = qt % 2
            bank = 1 + s % 3
            pt_, pk = Pb[s % 3], ("Pb", s % 3)
            nc_ = itm["ncols"]
            if itm["kind"] == "g":
                for b_ in range(itm["nblk"]):
                    n = 2 * itm["g"] + b_
                    col = itm["col"] + b_
                    P.add("act", lambda E, b_=b_, n=n, col=col: E.activation(
                        out=pt_[:, b_ * 256:(b_ + 1) * 256], in_=pbf[bank][:, b_ * 256:(b_ + 1) * 256], func=AF.Exp,
                        bias=biasb[qt % 2][:, n:n + 1], scale=SCALE, accum_out=rsb[qt % 2][:, col:col + 1]),
                        reads=[("pb", bank), ("biasb", qt % 2)], writes=[pk, ("rsb", qt % 2, col)])
            else:
                sm, sk = scm[s % 2], ("scm", s % 2)
                col = itm["col"]
                P.add("dve", lambda E: E.tensor_tensor(out=sm[:, 0:nc_], in0=pbf[bank][:, 0:nc_], in1=tri[:, half, 0:nc_],
                                                       op=ALU.add), reads=[("pb", bank), ("tri",)], writes=[sk])
                P.add("act", lambda E: E.activation(out=pt_[:, 0:nc_], in_=sm[:, 0:nc_], func=AF.Exp, scale=SCALE,
                                                    accum_out=rsb[qt % 2][:, col:col + 1]),
                      reads=[sk], writes=[pk, ("rsb", qt % 2, col)])

        def do_tr(itm):
            s = itm["s"]
            tb = 4 + s % 2
            for c in range(itm["ncols"] // 128):
                P.add("pe", lambda E, c=c: E.transpose(out=pbh[tb][:, c * 128:(c + 1) * 128],
                                                       in_=Pb[s % 3][:, c * 128:(c + 1) * 128], identity=ident[:]),
                      reads=[("Pb", s % 3), ("ident",)], writes=[("ptp", tb)])

        def do_pv(itm):
            qt, s = itm["qt"], itm["s"]
            i = qt // 2
            tb = 4 + s % 2
            nc_ = itm["ncols"]
            P.add("dve", lambda E: E.tensor_copy(out=PTb[s % 3][:, 0:nc_], in_=pbh[tb][:, 0:nc_]),
                  reads=[("ptp", tb)], writes=[("PTb", s % 3)])
            ob = 6 + qt % 2
            nchunk = nc_ // 128
            for c in range(nchunk):
                if itm["kind"] == "g":
                    rhs = vh[hb][:, 4 * itm["g"] + c, :]
                    rk_ = ("vh", hb)
                else:
                    rhs = vo[hb][:, 2 * i + c, :]
                    rk_ = ("vo", hb)
                st_ = itm["first"] and c == 0
                sp_ = itm["last"] and c == nchunk - 1
                P.add("pe", lambda E, c=c, rhs=rhs, st_=st_, sp_=sp_: E.matmul(
                    pbf[ob][:, 0:128], lhsT=PTb[s % 3][:, c * 128:(c + 1) * 128], rhs=rhs, start=st_, stop=sp_),
                    reads=[("PTb", s % 3), rk_], writes=[("ops", ob)])
            if itm["last"]:
                ncol = itm["col"] + 1
                P.add("dve", lambda E: E.tensor_reduce(out=rsum[:], in_=rsb[qt % 2][:, 0:ncol], axis=AX.X, op=ALU.add),
                      reads=[("rsb", qt % 2, c_) for c_ in range(ncol)], writes=[("rsum",)])
                P.add("dve", lambda E: E.reciprocal(out=rinv[:], in_=rsum[:]), reads=[("rsum",)], writes=[("rinv",)])
                P.add("dve", lambda E: E.tensor_scalar(out=obf[qt % 2][:], in0=pbf[ob][:, 0:128], scalar1=rinv[:, 0:1],
                                                       scalar2=None, op0=ALU.mult),
                      reads=[("ops", ob), ("rinv",)], writes=[("obf", qt % 2)])
                ot_ = pbh[0][:, 512:640]
                P.add("pe", lambda E: E.transpose(out=ot_, in_=obf[qt % 2][:], identity=ident[:]),
                      reads=[("obf", qt % 2), ("ident",)], writes=[("pb", 0)])
                odst = qT(h, qt * 128, 128)
                P.add("dve", lambda E: E.tensor_copy(out=odst, in_=ot_),
                      reads=[("pb", 0)], writes=[("q", h, qt)])

        n_it = len(items)
        for s_ in range(n_it + 3):
            if s_ < n_it:
                do_qk(items[s_])
            if 0 <= s_ - 1 < n_it:
                do_exp(items[s_ - 1])
            if 0 <= s_ - 2 < n_it:
                do_tr(items[s_ - 2])
            if 0 <= s_ - 3 < n_it:
                do_pv(items[s_ - 3])
    P.barrier()
    st6.close()
    if upto == 6:
        return finish()

    st7 = ExitStack()
    ga = sb("ga", [128, 8, TOK], BF16, st7)
    hT_stack = ExitStack()
    hT = sb("hT", [128, 8, TOK], BF16, hT_stack)
    st1 = ExitStack()
    stage_hT(hT, st1)
    P.barrier()
    st1.close()
    st7a = ExitStack()
    sz = [sb("sz%d" % i, [128, 512], F32, st7a) for i in range(2)]
    it = 0
    for g in range(2):
        wt, wkey = w_next()
        for cc in range(4):
            c = 4 * g + cc
            for T in range(4):
                bank = 1 + it % 3
                proj_fm(wt, wkey, cc, T, bank)
                s_ = sz[it % 2]
                P.add("act", lambda E, s_=s_, bank=bank: E.activation(out=s_[:], in_=pbf[bank], func=AF.Silu),
                      reads=[("pb", bank)], writes=[("sz", it % 2)])
                P.add("dve", lambda E, s_=s_, c=c, T=T: E.tensor_tensor(
                    out=qT(c, T * 512, 512), in0=qT(c, T * 512, 512), in1=s_[:], op=ALU.mult),
                    reads=[("q", c, 4 * T + j) for j in range(4)] + [("sz", it % 2)],
                    writes=[("q", c, 4 * T + j) for j in range(4)])
                it += 1
    for g in range(2):
        wt, wkey = w_next()
        for cc in range(4):
            c = 4 * g + cc
            for T in range(4):
                bank = 1 + it % 3
                proj_fm(wt, wkey, cc, T, bank)
                P.add("act", lambda E, bank=bank, c=c, T=T: E.activation(
                    out=ga[:, c, T * 512:(T + 1) * 512], in_=pbf[bank], func=AF.Sigmoid, bias=bgate[:, c:c + 1]),
                    reads=[("pb", bank), ("bgate",)], writes=[("ga", c, T)])
                it += 1
    P.barrier()
    st7a.close()
    hT_stack.close()
    gcyc = sb("gcyc", [128, 8, TOK], BF16, st7)
    tmpf = [sb("tmpf%d" % i, [128, 512], F32, st7) for i in range(2)]
    xt = [sb("xo%d" % i, [128, D], F32, st7) for i in range(2)]
    ot = [sb("ot%d" % i, [128, D], F32, st7) for i in range(2)]
    P.add("sp", lambda E: E.dma_start(out=gcyc[:], in_=gsc), reads=[("gsc",)],
          writes=[("gc", c, T) for c in range(8) for T in range(4)], dma=True)
    for g in range(2):
        wt, wkey = w_next()
        for cc in range(4):
            c = 4 * g + cc
            for T in range(4):
                bank = 1 + it % 3
                proj_fm(wt, wkey, cc, T, bank, rhs_fn=lambda kc, T=T: qT(kc, T * 512, 512),
                        rkeys=lambda kc, T=T: [("q", kc, 4 * T + j) for j in range(4)])
                tf = tmpf[it % 2]
                P.add("dve", lambda E, tf=tf, bank=bank, c=c, T=T: E.tensor_tensor(
                    out=tf[:], in0=pbf[bank], in1=ga[:, c, T * 512:(T + 1) * 512], op=ALU.mult),
                    reads=[("pb", bank), ("ga", c, T)], writes=[("tmpf", it % 2)])
                P.add("dve", lambda E, tf=tf, c=c, T=T: E.tensor_tensor(
                    out=gcyc[:, c, T * 512:(T + 1) * 512], in0=tf[:], in1=gcyc[:, c, T * 512:(T + 1) * 512], op=ALU.add),
                    reads=[("tmpf", it % 2), ("gc", c, T)], writes=[("gc", c, T)])
                it += 1
    wo0, wo0k = w_next()
    wo1, wo1k = w_next(hold=True)
    for tt in range(NT):
        b = tt % 2
        P.add("sp", lambda E, tt=tt, b=b: E.dma_start(out=xt[b][:], in_=x_d[tt * 128:(tt + 1) * 128, :]),
              writes=[("xo", b)], dma=True)
        for cg, (wt, wkey) in enumerate(((wo0, wo0k), (wo1, wo1k))):
            bank = 1 + it % 3
            for kc in range(8):
                P.add("pe", lambda E, kc=kc, tt=tt, bank=bank, wt=wt: E.matmul(
                    pbf[bank], lhsT=gcyc[:, kc, tt * 128:(tt + 1) * 128], rhs=wt[:, kc, :],
                    start=(kc == 0), stop=(kc == 7)),
                    reads=[("gc", kc, tt // 4), wkey], writes=[("pb", bank)])
            P.add("dve", lambda E, bank=bank, b=b, cg=cg: E.tensor_tensor(
                out=ot[b][:, cg * 512:(cg + 1) * 512], in0=pbf[bank], in1=xt[b][:, cg * 512:(cg + 1) * 512], op=ALU.add),
                reads=[("pb", bank), ("xo", b)], writes=[("ot", b, cg)])
            it += 1
        P.add("pool", lambda E, tt=tt, b=b: E.dma_start(out=y_d[tt * 128:(tt + 1) * 128, :], in_=ot[b][:]),
              reads=[("ot", b, 0), ("ot", b, 1)], writes=[("y", tt)], dma=True)
    P.add("sp", None, reads=[("y", tt) for tt in range(NT)])
    P.barrier()
    P.emit(es)
    st7.close()
    es.close()
    return nc, P


_CACHE = {}


def _prog(mode):
    if mode not in _CACHE:
        _CACHE[mode] = build(mode)[0]
    return _CACHE[mode]


def _own_blocks(r):
    return [4 * i + r for i in range(8)]


def _rope_table(r):
    inv_freq = (np.float32(500000.0) ** (-np.arange(0, 32, 2, dtype=np.float32) / np.float32(32))).astype(np.float32)
    pos = np.concatenate([np.arange(b * BLK, (b + 1) * BLK) for b in _own_blocks(r)]).astype(np.float32)
    ang = (pos[:, None] * inv_freq[None, :]).astype(np.float32)
    cs = np.concatenate([np.cos(ang), np.sin(ang)], axis=1).astype(np.float32)
    return np.ascontiguousarray(cs.reshape(NT, 128, 32).transpose(1, 0, 2))


def _consts(r):
    gmask = np.full((8, NB), -1e36, np.float32)
    for i in range(8):
        gmask[i, :4 * i + r] = 0.0
    gmask = np.ascontiguousarray(np.broadcast_to(gmask[None], (128, 8, NB)))
    q = np.arange(128)[:, None]
    k = np.arange(128)[None, :]
    t = np.where(k <= q, 0.0, NEG).astype(np.float32)
    tri = np.zeros((128, 2, 256), np.float32)
    tri[:, 0, :128] = t
    tri[:, 0, 128:] = NEG
    tri[:, 1, 128:] = t
    return gmask, tri


def _pk(v):
    return np.ascontiguousarray(np.asarray(v, np.float32).reshape(8, 128).T)


def _layer(xs, l, p):
    n = 8
    w_in = np.ascontiguousarray(p["w_in"][l])
    ng = _pk(p["norm_g"][l])
    qkg = np.ascontiguousarray(np.broadcast_to(
        np.stack([p["q_norm_g"][l], p["k_norm_g"][l]])[None], (128, 2, 128))).astype(np.float32)
    ins_a = []
    for c in range(n):
        r = c % 4
        ins_a.append({"x": xs[c], "w_in": w_in, "ng": ng, "cs": _rope_table(r), "qkg": qkg})
    ra = run_bass_kernel_spmd(_prog("A"), ins_a, core_ids=list(range(n))).results
    ins_b = []
    for c in range(n):
        b, r = c // 4, c % 4
        grp = [ra[4 * b + rr] for rr in range(4)]
        kt = np.stack([g["kt_o"].reshape(128, NH, 8, BLK) for g in grp], axis=3)
        kt_all = np.ascontiguousarray(kt.reshape(128, NH, SEQ).transpose(1, 0, 2))
        v = np.stack([g["v_o"].reshape(128, 8, 2, NH, HD) for g in grp], axis=2)
        v_all = np.ascontiguousarray(v.reshape(128, 64, NH, HD).transpose(2, 0, 1, 3))
        km = np.stack([g["km_o"].reshape(128, NH, 8) for g in grp], axis=3)
        km_all = np.ascontiguousarray(km.reshape(128, NH, NB))
        halo = np.zeros((128, 8, 8, 32), np.float32)
        for i in range(8):
            gb = 4 * i + r - 1
            if gb >= 0:
                halo[:, :, i, :] = grp[gb % 4]["gt_o"].reshape(128, 8, 8, 32)[:, :, gb // 4, :]
        gmask, tri = _consts(r)
        cvec = np.ascontiguousarray(np.stack([_pk(p["conv_b"][l]), _pk(p["cn_g"][l]), _pk(p["cn_b"][l])], axis=1))
        ins_b.append({
            "x": xs[c], "w_in": w_in, "ng": ng, "cs": _rope_table(r), "qkg": qkg,
            "kt_all": kt_all, "v_all": v_all, "kt_own": np.ascontiguousarray(ra[c]["kt_o"].transpose(1, 0, 2)),
            "v_own": np.ascontiguousarray(ra[c]["v_o"].reshape(128, NT, NH, HD).transpose(2, 0, 1, 3)),
            "km_all": km_all, "halo": np.ascontiguousarray(halo.reshape(128, 8, 256)),
            "gmask": gmask, "tri": tri,
            "bgate": np.ascontiguousarray(np.asarray(p["b_gate"][l], np.float32).reshape(16, 128).T),
            "convw": np.ascontiguousarray(np.asarray(p["conv_w"][l], np.float32).reshape(CK, 8, 128).transpose(2, 1, 0)),
            "cvec": cvec,
            "w_ap": np.ascontiguousarray(p["w_attn_proj"][l]), "w_cp": np.ascontiguousarray(p["w_conv_proj"][l]),
            "w_o": np.ascontiguousarray(p["w_out"][l]),
        })
    rb = run_bass_kernel_spmd(_prog("B"), ins_b, core_ids=list(range(n))).results
    return [np.asarray(rb[c]["y"], np.float32) for c in range(n)]


def kernel(**inputs):
    p = {k: np.asarray(v) for k, v in inputs.items()}
    x = np.asarray(p["x"], np.float32)
    xs = []
    for c in range(8):
        b, r = c // 4, c % 4
        xs.append(np.ascontiguousarray(x[b].reshape(NB, BLK, D)[_own_blocks(r)].reshape(TOK, D)))
    for l in range(2):
        xs = _layer(xs, l, p)
    out = np.empty((2, NB, BLK, D), np.float32)
    for c in range(8):
        b, r = c // 4, c % 4
        out[b, _own_blocks(r)] = xs[c].reshape(8, BLK, D)
    return out.reshape(2, SEQ, D)
```

```python
import numpy as np
import ml_dtypes
from contextlib import ExitStack
import concourse.bass as bass
import concourse.mybir as mybir
from concourse.bass_utils import run_bass_kernel_spmd

F32 = mybir.dt.float32
BF16 = mybir.dt.bfloat16
AF = mybir.ActivationFunctionType
ALU = mybir.AluOpType
AX = mybir.AxisListType
NPBF = ml_dtypes.bfloat16

D = 1024
SEQ = 8192
NB = 32
BLK = 256
NH = 8
HD = 128
NIN = 9216
TOK = 2048
NT = 16
EPS = 1e-6
NEG = -30000.0
SCALE = 1.0 / float(np.sqrt(HD))
CK = 31
GW = 288

C_Q, C_K, C_V, C_ZA, C_UA, C_UB, C_ZC, C_GA, C_GC = 0, 1024, 2048, 3072, 4096, 5120, 6144, 7168, 8192


class _Op:
    __slots__ = ("eng", "fn", "deps", "dma", "need", "ord", "dsem", "dval", "nobar", "cc")


class Prog:
    CE = ("pe", "act", "dve", "pool")
    ALL = ("sp", "pool", "act", "dve", "pe")
    NDS = 8

    def __init__(self, nc):
        self.nc = nc
        self.ops = []
        self.lastw = {}
        self.readers = {}
        self.persist = set()
        self.since_bar = []

    def add(self, eng, fn, reads=(), writes=(), dma=False, nobar=False, cc=False):
        op = _Op()
        op.cc = cc
        op.eng, op.fn, op.dma, op.need, op.ord, op.dsem, op.dval, op.nobar = eng, fn, dma, False, 0, None, 0, nobar
        deps = {}
        for b in reads:
            w = self.lastw.get(b)
            if w is not None:
                deps[id(w)] = w
        for b in writes:
            w = self.lastw.get(b)
            if w is not None:
                deps[id(w)] = w
            for r in self.readers.get(b, ()):
                deps[id(r)] = r
        for b in writes:
            self.lastw[b] = op
            self.readers[b] = []
        for b in reads:
            self.readers.setdefault(b, []).append(op)
        op.deps = [d for d in deps.values()
                   if d is not op and not (d.eng == "pe" and eng == "pe" and not d.dma and not dma)]
        for d in op.deps:
            d.need = True
        self.ops.append(op)
        if not nobar:
            self.since_bar.append(op)
        return op

    def barrier(self):
        last = {}
        dmas = []
        for op in self.since_bar:
            if op.dma:
                dmas.append(op)
            elif op.fn is not None:
                last[op.eng] = op
        deps = list(last.values()) + dmas
        for d in deps:
            d.need = True
        for e in self.ALL:
            op = _Op()
            op.cc = False
            op.eng, op.fn, op.dma, op.need, op.ord, op.dsem, op.dval, op.nobar = e, None, False, False, 0, None, 0, False
            op.deps = list(deps)
            self.ops.append(op)
        self.since_bar = []
        self.lastw = {k: v for k, v in self.lastw.items() if k in self.persist}
        self.readers = {k: v for k, v in self.readers.items() if k in self.persist}

    def emit(self, es):
        nc = self.nc
        sems = {e: es.enter_context(nc.semaphore("c_" + e)) for e in self.CE}
        dsems = {q: [es.enter_context(nc.semaphore("d_%s%d" % (q, j))) for j in range(self.NDS)]
                 for q in ("sp", "pool")}
        cnt = {e: 0 for e in self.CE}
        dcnt = {q: 0 for q in dsems}
        duse = {q: [0] * self.NDS for q in dsems}
        prev_on_sem = {}
        ncc = sum(1 for op in self.ops if op.cc)
        ccsems = [es.enter_context(nc.semaphore("cc%d" % j)) for j in range(ncc)]
        icc = 0
        for op in self.ops:
            if op.cc:
                op.dsem = ccsems[icc]
                op.dval = 1
                icc += 1
            elif op.dma:
                q = op.eng
                j = dcnt[q] % self.NDS
                dcnt[q] += 1
                duse[q][j] += 1
                op.dsem = dsems[q][j]
                op.dval = 16 * duse[q][j]
            elif op.fn is not None and op.need:
                cnt[op.eng] += 1
                op.ord = cnt[op.eng]
        block = es.enter_context(nc.Block())
        bname = {"sp": "sync", "pool": "gpsimd", "act": "scalar", "dve": "vector", "pe": "tensor"}
        ninst = [0]
        for e in self.ALL:
            ops_e = [op for op in self.ops if op.eng == e]

            def body(E, ops_e=ops_e, e=e):
                waited = {}
                for op in ops_e:
                    w = {}
                    for d in op.deps:
                        if d.dma:
                            s, v = d.dsem, d.dval
                        else:
                            s, v = sems[d.eng], d.ord
                        k = id(s)
                        if k not in w or w[k][1] < v:
                            w[k] = (s, v)
                    if op.dma and not op.cc and op.dval > 16:
                        k = id(op.dsem)
                        v = op.dval - 16
                        if k not in w or w[k][1] < v:
                            w[k] = (op.dsem, v)
                    for k, (s, v) in w.items():
                        if waited.get(k, 0) < v:
                            E.wait_ge(s, v)
                            waited[k] = v
                            ninst[0] += 1
                    if op.fn is not None:
                        ins = op.fn(E)
                        ninst[0] += 1
                        if op.cc:
                            ins.then_inc(op.dsem, 1)
                        elif op.dma:
                            ins.then_inc(op.dsem, 16)
                        elif op.need:
                            ins.then_inc(sems[e], 1)

            getattr(block, bname[e])(body)
        self.ninst = ninst[0]


def _ap(t, off, dims, parts=128):
    ps = 1
    for s in list(t.shape)[1:]:
        ps *= int(s)
    return bass.AP(t, off, [[ps, parts]] + [list(d) for d in dims])


KOFF, VOFF, KMOFF, TOFF, GWID = 0, NH * TOK, 2 * NH * TOK, 2 * NH * TOK + 64, 2 * NH * TOK + 64 + 2048
RG = [[0, 1, 2, 3], [4, 5, 6, 7]]


def build():
    nc = bass.Bass("TRN2", target_bir_lowering=False)
    es = ExitStack()
    P = Prog(nc)

    def din(name, shape, dt=F32):
        return nc.dram_tensor(name, list(shape), dt, kind="ExternalInput").ap()

    def dout(name, shape, dt=F32):
        return nc.dram_tensor(name, list(shape), dt, kind="ExternalOutput").ap()

    def dscr(name, shape, dt=F32):
        return nc.dram_tensor(name, list(shape), dt, kind="Internal").ap()

    uniq = [0]

    def sb(name, shape, dt=F32, stack=None):
        uniq[0] += 1
        return (stack or es).enter_context(nc.sbuf_tensor("%s_%d" % (name, uniq[0]), list(shape), dt))

    x_d = din("x", [TOK, D])
    win_d = din("w_in", [2, D, NIN])
    ng_d = din("ng", [128, 2, 8])
    cs_d = din("cs", [128, NT, 32])
    qkg_d = din("qkg", [128, 2, 2, 128])
    gmask_d = din("gmask", [128, 8, NB])
    tri_d = din("tri", [128, 2, 256])
    smask_d = din("smask", [128, 4])
    hm_d = din("hm", [128, 4])
    bg_d = din("bgate", [128, 2, 16])
    cw_d = din("convw", [128, 2, 8, CK])
    cv_d = din("cvec", [128, 2, 3, 8])
    wap_d = din("w_ap", [2, D, D])
    wcp_d = din("w_cp", [2, D, D])
    wo_d = din("w_o", [2, D, D])
    y_d = dout("y", [TOK, D])
    gsc = dscr("gsc", [128, 8, TOK], BF16)
    xs1 = dscr("xs1", [TOK, D])
    NCH = 18
    gin = [[dscr("gin%d_%d" % (l, ch), [512, 2048]) for ch in range(NCH)] for l in range(2)]
    gout = [[dscr("gout%d_%d" % (l, ch), [512, 2048]) for ch in range(NCH)] for l in range(2)]
    ktown_d = dscr("ktown", [128, NH * TOK], BF16)
    vown_d = dscr("vownd", [128, NT * D], BF16)

    pb = [es.enter_context(nc.psum_tensor("pb%d" % i, [128, 512], F32)) for i in range(8)]
    pbf = [pb[i][:] for i in range(8)]
    pbh = [pb[i][:].bitcast(BF16) for i in range(8)]

    identf = sb("identf", [128, 128], F32)
    ident = sb("ident", [128, 128], BF16)
    ng2 = sb("ng2", [128, 2, 8])
    cs = sb("cs", [128, NT, 32])
    qkg2 = sb("qkg2", [128, 2, 2, 128])
    gmask = sb("gmask", [128, 8, NB])
    tri = sb("tri", [128, 2, 256])
    smask = sb("smask", [128, 4])
    hm = sb("hm", [128, 4])
    bgate2 = sb("bgate2", [128, 2, 16])
    cwT2 = sb("cwT2", [128, 2, 8, CK])
    cvec2 = sb("cvec2", [128, 2, 3, 8])
    kmb = sb("kmb", [128, NH, NB], BF16)
    ones_s = sb("ones_s", [128, 128], BF16)
    epsc = sb("epsc", [128, 1])
    NST, NBF = 2, 2
    wst = [sb("wst%d" % i, [128, 8, 512], F32) for i in range(NST)]
    wbf = [sb("wbf%d" % i, [128, 8, 512], BF16) for i in range(NBF)]
    bufA = sb("bufA", [128, 8 * 8 * GW], BF16)
    for i in range(NST):
        P.persist.add(("wst", i))
    for i in range(NBF):
        P.persist.add(("wbf", i))
    P.persist.add(("ng",))
    for l_ in range(2):
        for ch_ in range(NCH):
            P.persist.add(("gout", l_, ch_))

    def qT(h, t0, n):
        return _ap(bufA, h * TOK + t0, [[1, n]])

    def gT(c, blk0, nblk, off, n):
        return _ap(bufA, c * 8 * GW + blk0 * GW + off, [[GW, nblk], [1, n]])

    P.add("pool", lambda E: E.memset(identf[:], 0.0), writes=[("identf",)])
    P.add("pool", lambda E: E.affine_select(out=identf[:], in_=identf[:], compare_op=ALU.not_equal, fill=1.0,
                                            base=0, pattern=[[-1, 128]], channel_multiplier=1),
          reads=[("identf",)], writes=[("identf",)])
    P.add("pool", lambda E: E.tensor_copy(out=ident[:], in_=identf[:]), reads=[("identf",)], writes=[("ident",)])
    P.add("pool", lambda E: E.memset(epsc[:], float(EPS)), writes=[("epsc",)])
    P.add("pool", lambda E: E.memset(ones_s[:], 1.0 / D), writes=[("ones_s",)])
    for (t_, d_, k_) in ((ng2, ng_d, "ng"), (cs, cs_d, "cs"), (qkg2, qkg_d, "qkg"), (gmask, gmask_d, "gmask"),
                         (tri, tri_d, "tri"), (smask, smask_d, "smask"), (hm, hm_d, "hm"), (bgate2, bg_d, "bgate"),
                         (cwT2, cw_d, "cwT"), (cvec2, cv_d, "cvec")):
        P.add("sp", lambda E, t_=t_, d_=d_: E.dma_start(out=t_[:], in_=d_), writes=[(k_,)], dma=True)
    P.barrier()

    wlist = []
    for l_ in range(2):
        wlist += [(l_, "in", C_UA), (l_, "in", C_UB), (l_, "in", C_UA + 512), (l_, "in", C_UB + 512),
                  (l_, "in", C_K), (l_, "in", C_V), (l_, "in", C_K + 512), (l_, "in", C_V + 512),
                  (l_, "in", C_UA), (l_, "in", C_UB), (l_, "in", C_UA + 512), (l_, "in", C_UB + 512),
                  (l_, "in", C_ZC), (l_, "in", C_ZC + 512), (l_, "in", C_GC), (l_, "in", C_GC + 512),
                  (l_, "cp", 0), (l_, "cp", 512), (l_, "in", C_Q), (l_, "in", C_Q + 512),
                  (l_, "in", C_ZA), (l_, "in", C_ZA + 512), (l_, "in", C_GA), (l_, "in", C_GA + 512),
                  (l_, "ap", 0), (l_, "ap", 512), (l_, "o", 0), (l_, "o", 512)]
    wstate = {"emitted": 0, "next": 0}

    def w_load(j):
        lw, kind, c0 = wlist[j]
        base = {"in": win_d, "cp": wcp_d, "ap": wap_d, "o": wo_d}[kind]
        s_ap = base[lw].rearrange("(k p) c -> p k c", p=128)[:, :, c0:c0 + 512]
        st = j % NST
        P.add("sp", lambda E: E.dma_start(out=wst[st][:], in_=s_ap), writes=[("wst", st)], dma=True, nobar=True)

    def w_cast(j):
        lw, kind, c0 = wlist[j]
        st, bf = j % NST, j % NBF
        if kind == "in":
            ngb = _ap(ng2, lw * 8, [[1, 8], [0, 512]])
            P.add("dve", lambda E: E.tensor_tensor(out=wbf[bf][:], in0=wst[st][:], in1=ngb, op=ALU.mult),
                  reads=[("wst", st), ("ng",)], writes=[("wbf", bf)], nobar=True)
        else:
            P.add("act", lambda E: E.activation(out=wbf[bf][:], in_=wst[st][:], func=AF.Copy),
                  reads=[("wst", st)], writes=[("wbf", bf)], nobar=True)

    def w_next(hold=False):
        j = wstate["next"]
        wstate["next"] += 1
        while wstate["emitted"] <= min(len(wlist) - 1, j):
            w_load(wstate["emitted"])
            wstate["emitted"] += 1
        w_cast(j)
        while wstate["emitted"] <= min(len(wlist) - 1, j + 1):
            w_load(wstate["emitted"])
            wstate["emitted"] += 1
        return wbf[j % NBF], ("wbf", j % NBF)


    def layer(l):
        x_src = x_d if l == 0 else xs1
        y_dst = xs1 if l == 0 else y_d
        gin_l, gout_l = gin[l], gout[l]
        def stage_hT(hT, st):
            xt = [sb("xt%d" % i, [128, D], F32, st) for i in range(2)]
            sqj = sb("sqj", [128, D], BF16, st)
            xn = [sb("xn%d" % i, [128, D], BF16, st) for i in range(2)]
            ss = sb("ss", [128, NT], F32, st)
            rs_ = sb("rs_", [128, NT], F32, st)
            for tt in range(NT):
                b = tt % 2
                P.add("sp", lambda E, tt=tt, b=b: E.dma_start(out=xt[b][:], in_=x_src[tt * 128:(tt + 1) * 128, :]),
                      writes=[("xt", b)], dma=True)
                P.add("act", lambda E, tt=tt, b=b: E.activation(out=sqj[:], in_=xt[b][:], func=AF.Square,
                                                                accum_out=ss[:, tt:tt + 1]),
                      reads=[("xt", b)], writes=[("sqj",), ("ss", tt)])
                P.add("act", lambda E, tt=tt: E.activation(out=rs_[:, tt:tt + 1], in_=ss[:, tt:tt + 1], func=AF.Sqrt,
                                                           bias=epsc[:, 0:1], scale=1.0 / D),
                      reads=[("ss", tt), ("epsc",)], writes=[("rs_", tt)])
                P.add("dve", lambda E, tt=tt: E.reciprocal(out=rs_[:, tt:tt + 1], in_=rs_[:, tt:tt + 1]),
                      reads=[("rs_", tt)], writes=[("rs_", tt)])
                P.add("dve", lambda E, tt=tt, b=b: E.tensor_scalar(out=xn[b][:], in0=xt[b][:], scalar1=rs_[:, tt:tt + 1],
                                                                   scalar2=None, op0=ALU.mult),
                      reads=[("xt", b), ("rs_", tt)], writes=[("xn", b)])
                bank = 4 + b
                for c in range(8):
                    P.add("pe", lambda E, c=c, b=b, bank=bank: E.transpose(out=pbh[bank][:, c * 128:(c + 1) * 128],
                                                                           in_=xn[b][:, c * 128:(c + 1) * 128],
                                                                           identity=ident[:]),
                          reads=[("xn", b), ("ident",)], writes=[("pb", bank)])
                P.add("act", lambda E, tt=tt, bank=bank: E.activation(
                    out=_ap(hT, tt * 128, [[TOK, 8], [1, 128]]),
                    in_=pbh[bank].rearrange("p (c t) -> p c t", c=8), func=AF.Copy),
                    reads=[("pb", bank)], writes=[("hT", tt)])

        def stage_qk(hT, st, which, dst_fn, dst_key, after_group=None, groups=(0, 1)):
            sqk = sb("sqk", [128, 512], F32, st)
            ssk = sb("ssk", [128, 4], F32, st)
            rk = sb("rk", [128, 4], F32, st)
            kn = sb("kn", [128, 512], F32, st)
            kb = [sb("kb%d" % i, [128, 512], BF16, st) for i in range(2)]
            rt = [sb("rt%d" % i, [128, 4, 16], F32, st) for i in range(4)]
            gvec = _ap(qkg2, (l * 2 + which) * 128, [[0, 4], [1, 128]])
            it = 0
            for kg in groups:
                wt, wkey = w_next()
                for tt in range(NT):
                    bank = 1 + it % 3
                    for kc in range(8):
                        P.add("pe", lambda E, kc=kc, tt=tt, bank=bank, wt=wt: E.matmul(
                            pbf[bank], lhsT=hT[:, kc, tt * 128:(tt + 1) * 128], rhs=wt[:, kc, :],
                            start=(kc == 0), stop=(kc == 7)),
                            reads=[("hT", tt), wkey], writes=[("pb", bank)])
                    ps3 = pbf[bank].rearrange("p (h d) -> p h d", h=4)
                    P.add("act", lambda E, bank=bank: E.activation(out=sqk[:], in_=pbf[bank], func=AF.Square),
                          reads=[("pb", bank)], writes=[("sqk",)])
                    P.add("dve", lambda E: E.tensor_reduce(out=ssk[:], in_=sqk[:].rearrange("p (h d) -> p h d", h=4),
                                                           axis=AX.X, op=ALU.add),
                          reads=[("sqk",)], writes=[("ssk",)])
                    P.add("act", lambda E: E.activation(out=rk[:], in_=ssk[:], func=AF.Sqrt, bias=epsc[:, 0:1], scale=1.0 / HD),
                          reads=[("ssk",), ("epsc",)], writes=[("rk",)])
                    P.add("dve", lambda E: E.reciprocal(out=rk[:], in_=rk[:]), reads=[("rk",)], writes=[("rk",)])
                    kn3 = kn[:].rearrange("p (h d) -> p h d", h=4)
                    P.add("dve", lambda E, ps3=ps3, kn3=kn3: E.tensor_tensor(
                        out=kn3, in0=ps3, in1=_ap(rk, 0, [[1, 4], [0, 128]]), op=ALU.mult),
                        reads=[("pb", bank), ("rk",)], writes=[("kn",)])
                    P.add("dve", lambda E, kn3=kn3: E.tensor_tensor(out=kn3, in0=kn3, in1=gvec, op=ALU.mult),
                          reads=[("kn",), ("qkg",)], writes=[("kn",)])
                    kbb = kb[it % 2]
                    kbk = ("kb", it % 2)
                    P.add("act", lambda E, kbb=kbb: E.activation(out=kbb[:], in_=kn[:], func=AF.Copy),
                          reads=[("kn",)], writes=[kbk])
                    t1 = _ap(kn, 0, [[128, 4], [1, 16]])
                    t2 = _ap(kn, 16, [[128, 4], [1, 16]])
                    cosb = _ap(cs, tt * 32, [[0, 4], [1, 16]])
                    sinb = _ap(cs, tt * 32 + 16, [[0, 4], [1, 16]])
                    o1 = _ap(kbb, 0, [[128, 4], [1, 16]])
                    o2 = _ap(kbb, 16, [[128, 4], [1, 16]])
                    P.add("dve", lambda E, t1=t1, cosb=cosb: E.tensor_tensor(out=rt[0][:], in0=t1, in1=cosb, op=ALU.mult),
                          reads=[("kn",), ("cs",)], writes=[("rt", 0)])
                    P.add("dve", lambda E, t2=t2, sinb=sinb: E.tensor_tensor(out=rt[1][:], in0=t2, in1=sinb, op=ALU.mult),
                          reads=[("kn",), ("cs",)], writes=[("rt", 1)])
                    P.add("dve", lambda E, t2=t2, cosb=cosb: E.tensor_tensor(out=rt[2][:], in0=t2, in1=cosb, op=ALU.mult),
                          reads=[("kn",), ("cs",)], writes=[("rt", 2)])
                    P.add("dve", lambda E, t1=t1, sinb=sinb: E.tensor_tensor(out=rt[3][:], in0=t1, in1=sinb, op=ALU.mult),
                          reads=[("kn",), ("cs",)], writes=[("rt", 3)])
                    P.add("dve", lambda E, o1=o1: E.tensor_tensor(out=o1, in0=rt[0][:], in1=rt[1][:], op=ALU.subtract),
                          reads=[("rt", 0), ("rt", 1), kbk], writes=[kbk])
                    P.add("dve", lambda E, o2=o2: E.tensor_tensor(out=o2, in0=rt[2][:], in1=rt[3][:], op=ALU.add),
                          reads=[("rt", 2), ("rt", 3), kbk], writes=[kbk])
                    tb = 5 + it % 2
                    for h in range(4):
                        P.add("pe", lambda E, h=h, tb=tb, kbb=kbb: E.transpose(
                            out=pbh[tb][:, h * 128:(h + 1) * 128], in_=kbb[:, h * 128:(h + 1) * 128], identity=ident[:]),
                            reads=[kbk, ("ident",)], writes=[("pb", tb)])
                    P.add("act", lambda E, tb=tb, kg=kg, tt=tt: E.activation(
                        out=dst_fn(kg * 4, tt), in_=pbh[tb][:, 0:512].rearrange("p (h t) -> p h t", h=4), func=AF.Copy),
                        reads=[("pb", tb)], writes=[(dst_key, kg * 4 + h, tt) for h in range(4)])
                    it += 1
                if after_group is not None:
                    after_group(kg)


        hT_stack = ExitStack()
        hT = sb("hT", [128, 8, TOK], BF16, hT_stack)
        st1 = ExitStack()
        stage_hT(hT, st1)
        P.barrier()
        st1.close()
        st2 = ExitStack()
        vown = sb("vown", [128, NT, D], BF16, st2)
        kms = sb("kms", [128, 64], F32, st2)
        sgA = sb("sgA", [128, 256], F32, st2)
        gtl = sb("gtl", [128, 8, 256], F32, st2)
        stg = [sb("stg%d" % i, [128, 2048], BF16, st2) for i in range(3)]
        stf = [sb("stf%d" % i, [128, 2048], F32, st2) for i in range(2)]
        skm = sb("skm", [128, 4, 64], F32, st2)
        allq = [("q", h, tt) for h in range(NH) for tt in range(NT)]
        allv = [("vown", tt, vg) for tt in range(NT) for vg in range(2)]
        cidx = [0]

        def gather_chunk(ch):
            gks = []
            for j in range(4):
                rows = slice(j * 128, (j + 1) * 128)
                mj = smask[:, j:j + 1]
                gk = ("gin", l, j, ch)
                if ch < 16:
                    ci = cidx[0]
                    cidx[0] += 1
                    s_ = stg[ci % 3]
                    if ch < 8:
                        src_ap, rk_ = qT(ch, 0, TOK), [("q", ch, tt) for tt in range(NT)]
                        o_ap = s_[:]
                    else:
                        hv = ch - 8
                        src_ap, rk_ = vown[:, :, hv * 128:(hv + 1) * 128], [("vown", tt, hv // 4) for tt in range(NT)]
                        o_ap = s_[:].rearrange("p (t d) -> p t d", d=128)
                    sk = ("stg", ci % 3)
                    if ci % 2 == 0:
                        P.add("act", lambda E, o_ap=o_ap, src_ap=src_ap, mj=mj: E.activation(out=o_ap, in_=src_ap, func=AF.Copy,
                                                                                             scale=mj),
                              reads=rk_ + [("smask",)], writes=[sk])
                    else:
                        P.add("dve", lambda E, o_ap=o_ap, src_ap=src_ap, mj=mj: E.tensor_scalar(out=o_ap, in0=src_ap, scalar1=mj,
                                                                                                scalar2=None, op0=ALU.mult),
                              reads=rk_ + [("smask",)], writes=[sk])
                    P.add("pool", lambda E, s_=s_, rows=rows: E.dma_start(out=gin_l[ch][rows, :], in_=s_[:]),
                          reads=[sk], writes=[gk], dma=True)
                elif ch == 16:
                    P.add("dve", lambda E, j=j, mj=mj: E.tensor_scalar(out=skm[:, j, :], in0=kms[:], scalar1=mj, scalar2=None,
                                                                       op0=ALU.mult),
                          reads=[("kms",), ("smask",)], writes=[("skm", j)])
                    P.add("pool", lambda E, j=j, rows=rows: E.dma_start(out=gin_l[ch][rows, 0:64], in_=skm[:, j, :]),
                          reads=[("skm", j)], writes=[gk], dma=True)
                else:
                    f_ = stf[j % 2]
                    P.add("dve", lambda E, f_=f_, mj=mj: E.tensor_scalar(out=f_[:], in0=gtl[:].rearrange("p c t -> p (c t)"),
                                                                         scalar1=mj, scalar2=None, op0=ALU.mult),
                          reads=[("gtl", c) for c in range(8)] + [("smask",)], writes=[("stf", j % 2)])
                    P.add("pool", lambda E, f_=f_, rows=rows: E.dma_start(out=gin_l[ch][rows, :], in_=f_[:]),
                          reads=[("stf", j % 2)], writes=[gk], dma=True)
                gks.append(gk)
            P.add("pool", lambda E: E.collective_compute("AllReduce", ALU.add, replica_groups=RG, ins=[gin_l[ch]],
                                                         outs=[gout_l[ch]]),
                  reads=gks, writes=[("gout", l, ch)], dma=True, nobar=True, cc=True)

        for g in range(2):
            wa, wakey = w_next()
            wb, wbkey = w_next(hold=True)
            for cc in range(4):
                c = 4 * g + cc
                for (bank, wt, wkey) in ((1, wa, wakey), (2, wb, wbkey)):
                    for kc in range(8):
                        rhsT = _ap(hT, kc * TOK + 224, [[BLK, 8], [1, 32]])
                        P.add("pe", lambda E, kc=kc, cc=cc, bank=bank, wt=wt, rhsT=rhsT: E.matmul(
                            pbf[bank][:, 0:256], lhsT=wt[:, kc, cc * 128:(cc + 1) * 128],
                            rhs=rhsT,
                            start=(kc == 0), stop=(kc == 7)),
                            reads=[("hT", tt) for tt in range(NT)] + [wkey], writes=[("pb", bank)])
                P.add("act", lambda E: E.activation(out=sgA[:], in_=pbf[2][:, 0:256], func=AF.Sigmoid),
                      reads=[("pb", 2)], writes=[("sg",)])
                P.add("dve", lambda E, c=c: E.tensor_tensor(out=gtl[:, c, :], in0=pbf[1][:, 0:256], in1=sgA[:], op=ALU.mult),
                      reads=[("pb", 1), ("sg",)], writes=[("gtl", c)])
        gather_chunk(17)

        vit = [0]

        def v_group(vg):
            wt, wkey = w_next()
            for tt in range(NT):
                bank = 1 + vit[0] % 3
                for kc in range(8):
                    P.add("pe", lambda E, kc=kc, tt=tt, bank=bank, wt=wt, hT=hT: E.matmul(
                        pbf[bank], lhsT=hT[:, kc, tt * 128:(tt + 1) * 128], rhs=wt[:, kc, :],
                        start=(kc == 0), stop=(kc == 7)),
                        reads=[("hT", tt), wkey], writes=[("pb", bank)])
                P.add("act", lambda E, bank=bank, tt=tt, vg=vg: E.activation(
                    out=vown[:, tt, vg * 512:(vg + 1) * 512], in_=pbf[bank], func=AF.Copy),
                    reads=[("pb", bank)], writes=[("vown", tt, vg)])
                vit[0] += 1

        for grp in range(2):
            stq = ExitStack()
            stage_qk(hT, stq, 1, lambda h0, tt: _ap(bufA, h0 * TOK + tt * 128, [[TOK, 4], [1, 128]]), "q", groups=(grp,))
            stq.close()
            v_group(grp)
            for h_ in range(4 * grp, 4 * grp + 4):
                gather_chunk(h_)
                gather_chunk(8 + h_)
        P.add("dve", lambda E: E.tensor_reduce(out=kms[:], in_=_ap(bufA, 0, [[BLK, 64], [1, BLK]]), axis=AX.X, op=ALU.add),
              reads=allq, writes=[("kms",)])
        P.add("dve", lambda E: E.tensor_scalar(out=kms[:], in0=kms[:], scalar1=1.0 / BLK, scalar2=None, op0=ALU.mult),
              reads=[("kms",)], writes=[("kms",)])
        gather_chunk(16)
        P.add("sp", lambda E: E.dma_start(out=ktown_d, in_=_ap(bufA, 0, [[1, NH * TOK]])), reads=allq, writes=[("ktown",)], dma=True)
        P.add("sp", lambda E: E.dma_start(out=vown_d, in_=vown[:].rearrange("p t d -> p (t d)")), reads=allv, writes=[("vownd",)],
              dma=True)
        P.barrier()
        st2.close()
        def proj_fm(wt, wkey, cc, T, bank, rhs_fn=None, rkeys=None):
            for kc in range(8):
                rhs = hT[:, kc, T * 512:(T + 1) * 512] if rhs_fn is None else rhs_fn(kc)
                rk_ = [("hT", 4 * T + j) for j in range(4)] if rkeys is None else rkeys(kc)
                o_ = pbf[bank] if len(rhs.shape) == 2 else pbf[bank].rearrange("p (b t) -> p b t", b=rhs.shape[1])
                P.add("pe", lambda E, kc=kc, rhs=rhs, o_=o_: E.matmul(o_, lhsT=wt[:, kc, cc * 128:(cc + 1) * 128], rhs=rhs,
                                                                      start=(kc == 0), stop=(kc == 7)),
                      reads=rk_ + [wkey], writes=[("pb", bank)])

        st3 = ExitStack()
        sg = [sb("sg%d" % i, [128, 512], F32, st3) for i in range(2)]
        it = 0
        for g in range(2):
            wa, wakey = w_next()
            wb_, wbkey = w_next(hold=True)
            for cc in range(4):
                c = 4 * g + cc
                for T in range(4):
                    ba, bb = 1 + (it % 2) * 2, 2 + (it % 2) * 2
                    proj_fm(wa, wakey, cc, T, ba)
                    proj_fm(wb_, wbkey, cc, T, bb)
                    s_ = sg[it % 2]
                    P.add("act", lambda E, s_=s_, bb=bb: E.activation(out=s_[:], in_=pbf[bb], func=AF.Sigmoid),
                          reads=[("pb", bb)], writes=[("sg", it % 2)])
                    P.add("dve", lambda E, s_=s_, ba=ba, c=c, T=T: E.tensor_tensor(
                        out=gT(c, 2 * T, 2, 32, 256), in0=pbf[ba].rearrange("p (b t) -> p b t", b=2),
                        in1=s_[:].rearrange("p (b t) -> p b t", b=2), op=ALU.mult),
                        reads=[("pb", ba), ("sg", it % 2)], writes=[("g", c, T)])
                    it += 1
        P.barrier()
        st3.close()

        sth = ExitStack()
        hl4 = sb("hl4", [128, 4, 2048], F32, sth)
        hacc = sb("hacc", [128, 2048], F32, sth)
        kmraw = sb("kmraw", [128, 4, 64], F32, sth)
        P.add("dve", lambda E: E.memset(hl4[:, 3, :], 0.0), writes=[("hl4", 3)])
        for j in range(3):
            P.add("sp", lambda E, j=j: E.dma_start(out=hl4[:, j, :], in_=gout_l[17][j * 128:(j + 1) * 128, :]),
                  reads=[("gout", l, 17)], writes=[("hl4", j)], dma=True)
        P.add("sp", lambda E: E.dma_start(
            out=_ap(hl4, 3 * 2048 + 32, [[256, 8], [1, 224]]),
            in_=gout_l[17][384:512, :].rearrange("p (c t) -> p c t", c=8)[:, :, 0:224]),
            reads=[("gout", l, 17), ("hl4", 3)], writes=[("hl4", 3)], dma=True)
        P.add("dve", lambda E: E.tensor_scalar(out=hacc[:], in0=hl4[:, 0, :], scalar1=hm[:, 0:1], scalar2=None, op0=ALU.mult),
              reads=[("hl4", 0), ("hm",)], writes=[("hacc",)])
        for j in range(1, 4):
            P.add("dve", lambda E, j=j: E.scalar_tensor_tensor(out=hacc[:], in0=hl4[:, j, :], scalar=hm[:, j:j + 1], in1=hacc[:],
                                                               op0=ALU.mult, op1=ALU.add),
                  reads=[("hl4", j), ("hm",), ("hacc",)], writes=[("hacc",)])
        P.add("dve", lambda E: E.tensor_copy(
            out=_ap(bufA, 0, [[GW, 64], [1, 32]]), in_=hacc[:].rearrange("p (ci t) -> p ci t", t=32)),
            reads=[("hacc",)], writes=[("gh",)])
        for rr in range(4):
            P.add("sp", lambda E, rr=rr: E.dma_start(out=kmraw[:, rr, :], in_=gout_l[16][rr * 128:(rr + 1) * 128, 0:64]),
                  reads=[("gout", l, 16)], writes=[("kmraw", rr)], dma=True)
        P.add("dve", lambda E: E.tensor_copy(out=_ap(kmb, 0, [[1, 4], [NB, 8], [4, 8]]),
                                             in_=kmraw[:].rearrange("p r (h i) -> p r h i", h=8)),
              reads=[("kmraw", rr) for rr in range(4)], writes=[("kmb",)])
        P.barrier()
        sth.close()

        st4 = ExitStack()
        gcy4_stack = ExitStack()
        gcy4 = sb("gcyc", [128, 8, TOK], BF16, gcy4_stack)
        cpre = sb("cpre", [128, 8, 512], F32, st4)
        cb = [sb("cb%d" % i, [128, 512], BF16, st4) for i in range(2)]
        csq = [sb("csq%d" % i, [128, 512], BF16, st4) for i in range(2)]
        mean = sb("mean", [128, 512], F32, st4)
        msq = sb("msq", [128, 512], F32, st4)
        rstd = sb("rstd", [128, 512], F32, st4)
        dw = sb("dw", [128, CK, 128], BF16, st4)
        sz = [sb("sz%d" % i, [128, 512], F32, st4) for i in range(2)]
        it = 0
        for T in range(4):
            for c in range(8):
                P.add("dve", lambda E, c=c: E.tensor_tensor(
                    out=dw[:], in0=_ap(identf, 0, [[0, CK], [1, 128]]), in1=_ap(cwT2, (l * 8 + c) * CK, [[1, CK], [0, 128]]),
                    op=ALU.mult), reads=[("identf",), ("cwT",)], writes=[("dw",)])
                bank = 1 + it % 2
                for j in range(CK):
                    P.add("pe", lambda E, j=j, c=c, T=T, bank=bank: E.matmul(
                        pbf[bank].rearrange("p (b t) -> p b t", b=2), lhsT=dw[:, j, :], rhs=gT(c, 2 * T, 2, 2 + j, 256),
                        start=(j == 0), stop=(j == CK - 1)),
                        reads=[("dw",), ("g", c, T), ("gh",)], writes=[("pb", bank)])
                P.add("act", lambda E, c=c, bank=bank: E.activation(out=cpre[:, c, :], in_=pbf[bank], func=AF.Identity,
                                                                    bias=cvec2[:, l, 0, c:c + 1]),
                      reads=[("pb", bank), ("cvec",)], writes=[("cpre", c)])
                P.add("act", lambda E, c=c, bank=bank, k=it % 2: E.activation(out=csq[k][:], in_=pbf[bank], func=AF.Square,
                                                                    bias=cvec2[:, l, 0, c:c + 1]),
                      reads=[("pb", bank), ("cvec",)], writes=[("csq", it % 2)])
                P.add("dve", lambda E, c=c, k=it % 2: E.tensor_copy(out=cb[k][:], in_=cpre[:, c, :]),
                      reads=[("cpre", c)], writes=[("cb", it % 2)])
                P.add("pe", lambda E, c=c, k=it % 2: E.matmul(pbf[3], lhsT=ones_s[:], rhs=cb[k][:], start=(c == 0), stop=(c == 7)),
                      reads=[("ones_s",), ("cb", it % 2)], writes=[("pb", 3)])
                P.add("pe", lambda E, c=c, k=it % 2: E.matmul(pbf[4], lhsT=ones_s[:], rhs=csq[k][:], start=(c == 0), stop=(c == 7)),
                      reads=[("ones_s",), ("csq", it % 2)], writes=[("pb", 4)])
                it += 1
            P.add("dve", lambda E: E.tensor_copy(out=mean[:], in_=pbf[3]), reads=[("pb", 3)], writes=[("mean",)])
            P.add("dve", lambda E: E.tensor_tensor(out=msq[:], in0=mean[:], in1=mean[:], op=ALU.mult),
                  reads=[("mean",)], writes=[("msq",)])
            P.add("dve", lambda E: E.tensor_tensor(out=rstd[:], in0=pbf[4], in1=msq[:], op=ALU.subtract),
                  reads=[("pb", 4), ("msq",)], writes=[("rstd",)])
            P.add("act", lambda E: E.activation(out=rstd[:], in_=rstd[:], func=AF.Sqrt, bias=epsc[:, 0:1]),
                  reads=[("rstd",), ("epsc",)], writes=[("rstd",)])
            P.add("dve", lambda E: E.reciprocal(out=rstd[:], in_=rstd[:]), reads=[("rstd",)], writes=[("rstd",)])
            for c in range(8):
                P.add("dve", lambda E, c=c: E.tensor_tensor(out=cpre[:, c, :], in0=cpre[:, c, :], in1=mean[:], op=ALU.subtract),
                      reads=[("cpre", c), ("mean",)], writes=[("cpre", c)])
                P.add("dve", lambda E, c=c: E.tensor_tensor(out=cpre[:, c, :], in0=cpre[:, c, :], in1=rstd[:], op=ALU.mult),
                      reads=[("cpre", c), ("rstd",)], writes=[("cpre", c)])
                P.add("act", lambda E, c=c, T=T: E.activation(
                    out=gT(c, 2 * T, 2, 32, 256), in_=cpre[:, c, :].rearrange("p (b t) -> p b t", b=2), func=AF.Silu,
                    bias=cvec2[:, l, 2, c:c + 1], scale=cvec2[:, l, 1, c:c + 1]),
                    reads=[("cpre", c), ("cvec",)], writes=[("g", c, T)])
        it = 0
        for g in range(2):
            wt, wkey = w_next()
            for cc in range(4):
                c = 4 * g + cc
                for T in range(4):
                    bank = 5 + it % 2
                    proj_fm(wt, wkey, cc, T, bank)
                    s_ = sz[it % 2]
                    P.add("act", lambda E, s_=s_, bank=bank: E.activation(out=s_[:], in_=pbf[bank], func=AF.Silu),
                          reads=[("pb", bank)], writes=[("sz", it % 2)])
                    P.add("dve", lambda E, s_=s_, c=c, T=T: E.tensor_tensor(
                        out=gT(c, 2 * T, 2, 32, 256), in0=gT(c, 2 * T, 2, 32, 256),
                        in1=s_[:].rearrange("p (b t) -> p b t", b=2), op=ALU.mult),
                        reads=[("g", c, T), ("sz", it % 2)], writes=[("g", c, T)])
                    it += 1
        for g in range(2):
            wt, wkey = w_next()
            for cc in range(4):
                c = 4 * g + cc
                for T in range(4):
                    bank = 5 + it % 2
                    proj_fm(wt, wkey, cc, T, bank)
                    P.add("act", lambda E, bank=bank, c=c, T=T: E.activation(
                        out=gcy4[:, c, T * 512:(T + 1) * 512], in_=pbf[bank], func=AF.Sigmoid, bias=bgate2[:, l, 8 + c:9 + c]),
                        reads=[("pb", bank), ("bgate",)], writes=[("gc", c, T)])
                    it += 1
        for g in range(2):
            wt, wkey = w_next()
            for cc in range(4):
                c = 4 * g + cc
                for T in range(4):
                    bank = 5 + it % 2
                    proj_fm(wt, wkey, cc, T, bank, rhs_fn=lambda kc, T=T: gT(kc, 2 * T, 2, 32, 256),
                            rkeys=lambda kc, T=T: [("g", kc, T)])
                    P.add("dve", lambda E, bank=bank, c=c, T=T: E.tensor_tensor(
                        out=gcy4[:, c, T * 512:(T + 1) * 512], in0=pbf[bank], in1=gcy4[:, c, T * 512:(T + 1) * 512],
                        op=ALU.mult), reads=[("pb", bank), ("gc", c, T)], writes=[("gc", c, T)])
                    it += 1
        P.add("pool", lambda E: E.dma_start(out=gsc, in_=gcy4[:]),
              reads=[("gc", c, T) for c in range(8) for T in range(4)], writes=[("gsc",)], dma=True)
        P.barrier()
        st4.close()
        gcy4_stack.close()

        st5 = ExitStack()
        stage_qk(hT, st5, 0, lambda h0, tt: _ap(bufA, h0 * TOK + tt * 128, [[TOK, 4], [1, 128]]), "q")
        P.barrier()
        st5.close()
        hT_stack.close()

        st6 = ExitStack()
        kth = [sb("kth%d" % i, [128, SEQ], BF16, st6) for i in range(2)]
        vh = [sb("vh%d" % i, [128, 64, 128], BF16, st6) for i in range(2)]
        kto = [sb("kto%d" % i, [128, TOK], BF16, st6) for i in range(2)]
        vo = [sb("vo%d" % i, [128, NT, 128], BF16, st6) for i in range(2)]
        gm = sb("gm", [128, NB], F32, st6)
        m8 = sb("m8", [128, 8], F32, st6)
        thr = sb("thr", [128, 1], F32, st6)
        biasb = [sb("biasb%d" % i, [128, NB], F32, st6) for i in range(2)]
        rsb = [sb("rsb%d" % i, [128, 40], F32, st6) for i in range(2)]
        Pb = [sb("Pb%d" % i, [128, 512], BF16, st6) for i in range(3)]
        PTb = [sb("PTb%d" % i, [128, 512], BF16, st6) for i in range(3)]
        scm = [sb("scm%d" % i, [128, 256], F32, st6) for i in range(2)]
        rsum = sb("rsum", [128, 1], F32, st6)
        rinv = sb("rinv", [128, 1], F32, st6)
        obf = [sb("obf%d" % i, [128, 128], BF16, st6) for i in range(2)]

        def load_head(h):
            hb = h % 2
            for rr in range(4):
                P.add("pool", lambda E, rr=rr: E.dma_start(
                    out=_ap(kth[hb], rr * BLK, [[4 * BLK, 8], [1, BLK]]),
                    in_=gout_l[h][rr * 128:(rr + 1) * 128, :].rearrange("p (i t) -> p i t", t=BLK)),
                    reads=[("gout", l, h)], writes=[("kth", hb)], dma=True)
            for rr in range(4):
                P.add("pool", lambda E, rr=rr: E.dma_start(
                    out=_ap(vh[hb], rr * 256, [[1024, 8], [1, 256]]),
                    in_=gout_l[8 + h][rr * 128:(rr + 1) * 128, :].rearrange("p (i t) -> p i t", t=256)),
                    reads=[("gout", l, 8 + h)], writes=[("vh", hb)], dma=True)
            P.add("sp", lambda E: E.dma_start(out=kto[hb][:], in_=ktown_d[:, h * TOK:(h + 1) * TOK]), writes=[("kto", hb)], dma=True)
            vsrc = vown_d.rearrange("p (t d) -> p t d", d=D)
            for a in range(2):
                P.add("sp", lambda E, a=a: E.dma_start(out=vo[hb][:, 8 * a:8 * a + 8, :],
                                                       in_=vsrc[:, 8 * a:8 * a + 8, h * 128:(h + 1) * 128]),
                      writes=[("vo", hb)], dma=True)

        load_head(0)
        sidx = [0]
        for h in range(NH):
            hb = h % 2
            if h + 1 < NH:
                load_head(h + 1)
            items = []
            for qt in range(NT):
                i, half = qt // 2, qt % 2
                nb = 4 * i + 3
                col = 0
                ng_ = (nb + 1) // 2
                for g in range(ng_):
                    nblk = min(2, nb - 2 * g)
                    items.append(dict(qt=qt, kind="g", g=g, nblk=nblk, ncols=256 * nblk, first=(g == 0), last=False, col=col))
                    col += nblk
                items.append(dict(qt=qt, kind="d", ncols=128 * (half + 1), first=False, last=True, col=col))
            for it_ in items:
                it_["s"] = sidx[0]
                sidx[0] += 1

            def do_qk(itm):
                qt, s = itm["qt"], itm["s"]
                i, half = qt // 2, qt % 2
                qslice = qT(h, qt * 128, 128)
                kmh = kmb[:, h, :]
                if itm["first"]:
                    P.add("pe", lambda E: E.matmul(pbf[0][:, 0:NB], lhsT=qslice, rhs=kmh, start=True, stop=True),
                          reads=[("q", h, qt), ("kmb",)], writes=[("pb", 0)])
                    P.add("dve", lambda E: E.tensor_tensor(out=gm[:], in0=pbf[0][:, 0:NB], in1=gmask[:, i, :], op=ALU.add),
                          reads=[("pb", 0), ("gmask",)], writes=[("gm",)])
                    P.add("dve", lambda E: E.max(out=m8[:], in_=gm[:]), reads=[("gm",)], writes=[("m8",)])
                    P.add("dve", lambda E: E.tensor_scalar(out=thr[:], in0=m8[:, 2:3], scalar1=-1e30, scalar2=None, op0=ALU.max),
                          reads=[("m8",)], writes=[("thr",)])
                    P.add("dve", lambda E: E.tensor_scalar(out=biasb[qt % 2][:], in0=gm[:], scalar1=thr[:, 0:1], scalar2=NEG,
                                                           op0=ALU.is_lt, op1=ALU.mult),
                          reads=[("gm",), ("thr",)], writes=[("biasb", qt % 2)])
                bank = 1 + s % 3
                nc_ = itm["ncols"]
                if itm["kind"] == "g":
                    rhs = kth[hb][:, 512 * itm["g"]:512 * itm["g"] + nc_]
                    rk_ = ("kth", hb)
                else:
                    rhs = kto[hb][:, i * BLK:i * BLK + nc_]
                    rk_ = ("kto", hb)
                P.add("pe", lambda E: E.matmul(pbf[bank][:, 0:nc_], lhsT=qslice, rhs=rhs, start=True, stop=True),
                      reads=[("q", h, qt), rk_], writes=[("pb", bank)])

            def do_exp(itm):
                qt, s = itm["qt"], itm["s"]
                half = qt % 2
                bank = 1 + s % 3
                pt_, pk = Pb[s % 3], ("Pb", s % 3)
                nc_ = itm["ncols"]
                if itm["kind"] == "g":
                    for b_ in range(itm["nblk"]):
                        n = 2 * itm["g"] + b_
                        col = itm["col"] + b_
                        P.add("act", lambda E, b_=b_, n=n, col=col: E.activation(
                            out=pt_[:, b_ * 256:(b_ + 1) * 256], in_=pbf[bank][:, b_ * 256:(b_ + 1) * 256], func=AF.Exp,
                            bias=biasb[qt % 2][:, n:n + 1], scale=SCALE, accum_out=rsb[qt % 2][:, col:col + 1]),
                            reads=[("pb", bank), ("biasb", qt % 2)], writes=[pk, ("rsb", qt % 2, col)])
                else:
                    sm, sk = scm[s % 2], ("scm", s % 2)
                    col = itm["col"]
                    P.add("dve", lambda E: E.tensor_tensor(out=sm[:, 0:nc_], in0=pbf[bank][:, 0:nc_], in1=tri[:, half, 0:nc_],
                                                           op=ALU.add), reads=[("pb", bank), ("tri",)], writes=[sk])
                    P.add("act", lambda E: E.activation(out=pt_[:, 0:nc_], in_=sm[:, 0:nc_], func=AF.Exp, scale=SCALE,
                                                        accum_out=rsb[qt % 2][:, col:col + 1]),
                          reads=[sk], writes=[pk, ("rsb", qt % 2, col)])

            def do_tr(itm):
                s = itm["s"]
                tb = 4 + s % 2
                for c in range(itm["ncols"] // 128):
                    P.add("pe", lambda E, c=c: E.transpose(out=pbh[tb][:, c * 128:(c + 1) * 128],
                                                           in_=Pb[s % 3][:, c * 128:(c + 1) * 128], identity=ident[:]),
                          reads=[("Pb", s % 3), ("ident",)], writes=[("ptp", tb)])

            def do_pv(itm):
                qt, s = itm["qt"], itm["s"]
                i = qt // 2
                tb = 4 + s % 2
                nc_ = itm["ncols"]
                P.add("dve", lambda E: E.tensor_copy(out=PTb[s % 3][:, 0:nc_], in_=pbh[tb][:, 0:nc_]),
                      reads=[("ptp", tb)], writes=[("PTb", s % 3)])
                ob = 6 + qt % 2
                nchunk = nc_ // 128
                for c in range(nchunk):
                    if itm["kind"] == "g":
                        rhs = vh[hb][:, 4 * itm["g"] + c, :]
                        rk_ = ("vh", hb)
                    else:
                        rhs = vo[hb][:, 2 * i + c, :]
                        rk_ = ("vo", hb)
                    st_ = itm["first"] and c == 0
                    sp_ = itm["last"] and c == nchunk - 1
                    P.add("pe", lambda E, c=c, rhs=rhs, st_=st_, sp_=sp_: E.matmul(
                        pbf[ob][:, 0:128], lhsT=PTb[s % 3][:, c * 128:(c + 1) * 128], rhs=rhs, start=st_, stop=sp_),
                        reads=[("PTb", s % 3), rk_], writes=[("ops", ob)])
                if itm["last"]:
                    ncol = itm["col"] + 1
                    P.add("dve", lambda E: E.tensor_reduce(out=rsum[:], in_=rsb[qt % 2][:, 0:ncol], axis=AX.X, op=ALU.add),
                          reads=[("rsb", qt % 2, c_) for c_ in range(ncol)], writes=[("rsum",)])
                    P.add("dve", lambda E: E.reciprocal(out=rinv[:], in_=rsum[:]), reads=[("rsum",)], writes=[("rinv",)])
                    P.add("dve", lambda E: E.tensor_scalar(out=obf[qt % 2][:], in0=pbf[ob][:, 0:128], scalar1=rinv[:, 0:1],
                                                           scalar2=None, op0=ALU.mult),
                          reads=[("ops", ob), ("rinv",)], writes=[("obf", qt % 2)])
                    ot_ = pbh[0][:, 512:640]
                    P.add("pe", lambda E: E.transpose(out=ot_, in_=obf[qt % 2][:], identity=ident[:]),
                          reads=[("obf", qt % 2), ("ident",)], writes=[("pb", 0)])
                    odst = qT(h, qt * 128, 128)
                    P.add("dve", lambda E: E.tensor_copy(out=odst, in_=ot_),
                          reads=[("pb", 0)], writes=[("q", h, qt)])

            n_it = len(items)
            for s_ in range(n_it + 3):
                if s_ < n_it:
                    do_qk(items[s_])
                if 0 <= s_ - 1 < n_it:
                    do_exp(items[s_ - 1])
                if 0 <= s_ - 2 < n_it:
                    do_tr(items[s_ - 2])
                if 0 <= s_ - 3 < n_it:
                    do_pv(items[s_ - 3])
        P.barrier()
        st6.close()

        st7 = ExitStack()
        ga = sb("ga", [128, 8, TOK], BF16, st7)
        hT_stack = ExitStack()
        hT = sb("hT", [128, 8, TOK], BF16, hT_stack)
        st1 = ExitStack()
        stage_hT(hT, st1)
        P.barrier()
        st1.close()
        st7a = ExitStack()
        sz = [sb("sz%d" % i, [128, 512], F32, st7a) for i in range(2)]
        it = 0
        for g in range(2):
            wt, wkey = w_next()
            for cc in range(4):
                c = 4 * g + cc
                for T in range(4):
                    bank = 1 + it % 3
                    proj_fm(wt, wkey, cc, T, bank)
                    s_ = sz[it % 2]
                    P.add("act", lambda E, s_=s_, bank=bank: E.activation(out=s_[:], in_=pbf[bank], func=AF.Silu),
                          reads=[("pb", bank)], writes=[("sz", it % 2)])
                    P.add("dve", lambda E, s_=s_, c=c, T=T: E.tensor_tensor(
                        out=qT(c, T * 512, 512), in0=qT(c, T * 512, 512), in1=s_[:], op=ALU.mult),
                        reads=[("q", c, 4 * T + j) for j in range(4)] + [("sz", it % 2)],
                        writes=[("q", c, 4 * T + j) for j in range(4)])
                    it += 1
        for g in range(2):
            wt, wkey = w_next()
            for cc in range(4):
                c = 4 * g + cc
                for T in range(4):
                    bank = 1 + it % 3
                    proj_fm(wt, wkey, cc, T, bank)
                    P.add("act", lambda E, bank=bank, c=c, T=T: E.activation(
                        out=ga[:, c, T * 512:(T + 1) * 512], in_=pbf[bank], func=AF.Sigmoid, bias=bgate2[:, l, c:c + 1]),
                        reads=[("pb", bank), ("bgate",)], writes=[("ga", c, T)])
                    it += 1
        P.barrier()
        st7a.close()
        hT_stack.close()
        gcyc = sb("gcyc", [128, 8, TOK], BF16, st7)
        tmpf = [sb("tmpf%d" % i, [128, 512], F32, st7) for i in range(2)]
        xt = [sb("xo%d" % i, [128, D], F32, st7) for i in range(2)]
        ot = [sb("ot%d" % i, [128, D], F32, st7) for i in range(2)]
        P.add("sp", lambda E: E.dma_start(out=gcyc[:], in_=gsc), reads=[("gsc",)],
              writes=[("gc", c, T) for c in range(8) for T in range(4)], dma=True)
        for g in range(2):
            wt, wkey = w_next()
            for cc in range(4):
                c = 4 * g + cc
                for T in range(4):
                    bank = 1 + it % 3
                    proj_fm(wt, wkey, cc, T, bank, rhs_fn=lambda kc, T=T: qT(kc, T * 512, 512),
                            rkeys=lambda kc, T=T: [("q", kc, 4 * T + j) for j in range(4)])
                    tf = tmpf[it % 2]
                    P.add("dve", lambda E, tf=tf, bank=bank, c=c, T=T: E.tensor_tensor(
                        out=tf[:], in0=pbf[bank], in1=ga[:, c, T * 512:(T + 1) * 512], op=ALU.mult),
                        reads=[("pb", bank), ("ga", c, T)], writes=[("tmpf", it % 2)])
                    P.add("dve", lambda E, tf=tf, c=c, T=T: E.tensor_tensor(
                        out=gcyc[:, c, T * 512:(T + 1) * 512], in0=tf[:], in1=gcyc[:, c, T * 512:(T + 1) * 512], op=ALU.add),
                        reads=[("tmpf", it % 2), ("gc", c, T)], writes=[("gc", c, T)])
                    it += 1
        wo0, wo0k = w_next()
        wo1, wo1k = w_next(hold=True)
        for tt in range(NT):
            b = tt % 2
            P.add("sp", lambda E, tt=tt, b=b: E.dma_start(out=xt[b][:], in_=x_src[tt * 128:(tt + 1) * 128, :]),
                  writes=[("xo", b)], dma=True)
            for cg, (wt, wkey) in enumerate(((wo0, wo0k), (wo1, wo1k))):
                bank = 1 + it % 3
                for kc in range(8):
                    P.add("pe", lambda E, kc=kc, tt=tt, bank=bank, wt=wt: E.matmul(
                        pbf[bank], lhsT=gcyc[:, kc, tt * 128:(tt + 1) * 128], rhs=wt[:, kc, :],
                        start=(kc == 0), stop=(kc == 7)),
                        reads=[("gc", kc, tt // 4), wkey], writes=[("pb", bank)])
                P.add("dve", lambda E, bank=bank, b=b, cg=cg: E.tensor_tensor(
                    out=ot[b][:, cg * 512:(cg + 1) * 512], in0=pbf[bank], in1=xt[b][:, cg * 512:(cg + 1) * 512], op=ALU.add),
                    reads=[("pb", bank), ("xo", b)], writes=[("ot", b, cg)])
                it += 1
            P.add("pool", lambda E, tt=tt, b=b: E.dma_start(out=y_dst[tt * 128:(tt + 1) * 128, :], in_=ot[b][:]),
                  reads=[("ot", b, 0), ("ot", b, 1)], writes=[("y", tt)], dma=True)
        P.add("sp", None, reads=[("y", tt) for tt in range(NT)])
        P.barrier()
        st7.close()

    layer(0)
    layer(1)
    P.emit(es)
    es.close()
    return nc, P


_CACHE = {}


def _prog():
    if "F" not in _CACHE:
        _CACHE["F"] = build()[0]
    return _CACHE["F"]


def _own_blocks(r):
    return [4 * i + r for i in range(8)]


def _rope_table(r):
    inv_freq = (np.float32(500000.0) ** (-np.arange(0, 32, 2, dtype=np.float32) / np.float32(32))).astype(np.float32)
    pos = np.concatenate([np.arange(b * BLK, (b + 1) * BLK) for b in _own_blocks(r)]).astype(np.float32)
    ang = (pos[:, None] * inv_freq[None, :]).astype(np.float32)
    cs = np.concatenate([np.cos(ang), np.sin(ang)], axis=1).astype(np.float32)
    return np.ascontiguousarray(cs.reshape(NT, 128, 32).transpose(1, 0, 2))


def _consts(r):
    gmask = np.full((8, NB), -1e36, np.float32)
    for i in range(8):
        gmask[i, :4 * i + r] = 0.0
    gmask = np.ascontiguousarray(np.broadcast_to(gmask[None], (128, 8, NB)))
    q = np.arange(128)[:, None]
    k = np.arange(128)[None, :]
    t = np.where(k <= q, 0.0, NEG).astype(np.float32)
    tri = np.zeros((128, 2, 256), np.float32)
    tri[:, 0, :128] = t
    tri[:, 0, 128:] = NEG
    tri[:, 1, 128:] = t
    smask = np.zeros((128, 4), np.float32)
    smask[:, r] = 1.0
    hm = np.zeros((128, 4), np.float32)
    hm[:, (r - 1) % 4] = 1.0
    return gmask, tri, smask, hm


def _pk2(v):
    return np.ascontiguousarray(np.asarray(v, np.float32).reshape(2, 8, 128).transpose(2, 0, 1))


def kernel(**inputs):
    p = {k: np.asarray(v) for k, v in inputs.items()}
    x = np.asarray(p["x"], np.float32)
    f32 = lambda a: np.ascontiguousarray(np.asarray(a, np.float32))
    shared = {
        "w_in": f32(p["w_in"]), "ng": _pk2(p["norm_g"]),
        "qkg": np.ascontiguousarray(np.broadcast_to(
            np.stack([p["q_norm_g"], p["k_norm_g"]], axis=1)[None], (128, 2, 2, 128))).astype(np.float32),
        "bgate": np.ascontiguousarray(np.asarray(p["b_gate"], np.float32).reshape(2, 16, 128).transpose(2, 0, 1)),
        "convw": np.ascontiguousarray(np.asarray(p["conv_w"], np.float32).reshape(2, CK, 8, 128).transpose(3, 0, 2, 1)),
        "cvec": np.ascontiguousarray(np.stack([_pk2(p["conv_b"]), _pk2(p["cn_g"]), _pk2(p["cn_b"])], axis=2)),
        "w_ap": f32(p["w_attn_proj"]), "w_cp": f32(p["w_conv_proj"]), "w_o": f32(p["w_out"]),
    }
    ins = []
    for c in range(8):
        b, r = c // 4, c % 4
        gmask, tri, smask, hm = _consts(r)
        d = dict(shared)
        d.update({"x": np.ascontiguousarray(x[b].reshape(NB, BLK, D)[_own_blocks(r)].reshape(TOK, D)),
                  "cs": _rope_table(r), "gmask": gmask, "tri": tri, "smask": smask, "hm": hm})
        ins.append(d)
    res = run_bass_kernel_spmd(_prog(), ins, core_ids=list(range(8))).results
    out = np.empty((2, NB, BLK, D), np.float32)
    for c in range(8):
        b, r = c // 4, c % 4
        out[b, _own_blocks(r)] = np.asarray(res[c]["y"], np.float32).reshape(8, BLK, D)
    return out.reshape(2, SEQ, D)
```

```python
import numpy as np
import ml_dtypes
from contextlib import ExitStack
import concourse.bass as bass
import concourse.mybir as mybir
from concourse.bass_utils import run_bass_kernel_spmd

F32 = mybir.dt.float32
BF16 = mybir.dt.bfloat16
AF = mybir.ActivationFunctionType
ALU = mybir.AluOpType
AX = mybir.AxisListType
NPBF = ml_dtypes.bfloat16

D = 1024
SEQ = 8192
NB = 32
BLK = 256
NH = 8
HD = 128
NIN = 9216
TOK = 2048
NT = 16
EPS = 1e-6
NEG = -30000.0
SCALE = 1.0 / float(np.sqrt(HD))
CK = 31
GW = 288

C_Q, C_K, C_V, C_ZA, C_UA, C_UB, C_ZC, C_GA, C_GC = 0, 1024, 2048, 3072, 4096, 5120, 6144, 7168, 8192


class _Op:
    __slots__ = ("eng", "fn", "deps", "dma", "need", "ord", "dsem", "dval", "nobar", "cc")


class Prog:
    CE = ("pe", "act", "dve", "pool")
    ALL = ("sp", "pool", "act", "dve", "pe")
    NDS = 8

    def __init__(self, nc):
        self.nc = nc
        self.ops = []
        self.lastw = {}
        self.readers = {}
        self.persist = set()
        self.since_bar = []

    def add(self, eng, fn, reads=(), writes=(), dma=False, nobar=False, cc=False):
        op = _Op()
        op.cc = cc
        op.eng, op.fn, op.dma, op.need, op.ord, op.dsem, op.dval, op.nobar = eng, fn, dma, False, 0, None, 0, nobar
        deps = {}
        for b in reads:
            w = self.lastw.get(b)
            if w is not None:
                deps[id(w)] = w
        for b in writes:
            w = self.lastw.get(b)
            if w is not None:
                deps[id(w)] = w
            for r in self.readers.get(b, ()):
                deps[id(r)] = r
        for b in writes:
            self.lastw[b] = op
            self.readers[b] = []
        for b in reads:
            self.readers.setdefault(b, []).append(op)
        op.deps = [d for d in deps.values()
                   if d is not op and not (d.eng == "pe" and eng == "pe" and not d.dma and not dma)]
        for d in op.deps:
            d.need = True
        self.ops.append(op)
        if not nobar:
            self.since_bar.append(op)
        return op

    def barrier(self):
        last = {}
        dmas = []
        for op in self.since_bar:
            if op.dma:
                dmas.append(op)
            elif op.fn is not None:
                last[op.eng] = op
        deps = list(last.values()) + dmas
        for d in deps:
            d.need = True
        for e in self.ALL:
            op = _Op()
            op.cc = False
            op.eng, op.fn, op.dma, op.need, op.ord, op.dsem, op.dval, op.nobar = e, None, False, False, 0, None, 0, False
            op.deps = list(deps)
            self.ops.append(op)
        self.since_bar = []
        self.lastw = {k: v for k, v in self.lastw.items() if k in self.persist}
        self.readers = {k: v for k, v in self.readers.items() if k in self.persist}

    def emit(self, es):
        nc = self.nc
        sems = {e: es.enter_context(nc.semaphore("c_" + e)) for e in self.CE}
        dsems = {q: [es.enter_context(nc.semaphore("d_%s%d" % (q, j))) for j in range(self.NDS)]
                 for q in ("sp", "pool")}
        cnt = {e: 0 for e in self.CE}
        dcnt = {q: 0 for q in dsems}
        duse = {q: [0] * self.NDS for q in dsems}
        prev_on_sem = {}
        ncc = sum(1 for op in self.ops if op.cc)
        ccsems = [es.enter_context(nc.semaphore("cc%d" % j)) for j in range(ncc)]
        icc = 0
        for op in self.ops:
            if op.cc:
                op.dsem = ccsems[icc]
                op.dval = 1
                icc += 1
            elif op.dma:
                q = op.eng
                j = dcnt[q] % self.NDS
                dcnt[q] += 1
                duse[q][j] += 1
                op.dsem = dsems[q][j]
                op.dval = 16 * duse[q][j]
            elif op.fn is not None and op.need:
                cnt[op.eng] += 1
                op.ord = cnt[op.eng]
        block = es.enter_context(nc.Block())
        bname = {"sp": "sync", "pool": "gpsimd", "act": "scalar", "dve": "vector", "pe": "tensor"}
        ninst = [0]
        for e in self.ALL:
            ops_e = [op for op in self.ops if op.eng == e]

            def body(E, ops_e=ops_e, e=e):
                waited = {}
                for op in ops_e:
                    w = {}
                    for d in op.deps:
                        if d.dma:
                            s, v = d.dsem, d.dval
                        else:
                            s, v = sems[d.eng], d.ord
                        k = id(s)
                        if k not in w or w[k][1] < v:
                            w[k] = (s, v)
                    if op.dma and not op.cc and op.dval > 16:
                        k = id(op.dsem)
                        v = op.dval - 16
                        if k not in w or w[k][1] < v:
                            w[k] = (op.dsem, v)
                    for k, (s, v) in w.items():
                        if waited.get(k, 0) < v:
                            E.wait_ge(s, v)
                            waited[k] = v
                            ninst[0] += 1
                    if op.fn is not None:
                        ins = op.fn(E)
                        ninst[0] += 1
                        if op.cc:
                            ins.then_inc(op.dsem, 1)
                        elif op.dma:
                            ins.then_inc(op.dsem, 16)
                        elif op.need:
                            ins.then_inc(sems[e], 1)

            getattr(block, bname[e])(body)
        self.ninst = ninst[0]


def _ap(t, off, dims, parts=128):
    ps = 1
    for s in list(t.shape)[1:]:
        ps *= int(s)
    return bass.AP(t, off, [[ps, parts]] + [list(d) for d in dims])


KOFF, VOFF, KMOFF, TOFF, GWID = 0, NH * TOK, 2 * NH * TOK, 2 * NH * TOK + 64, 2 * NH * TOK + 64 + 2048
RG = [[0, 1, 2, 3], [4, 5, 6, 7]]


def build():
    nc = bass.Bass("TRN2", target_bir_lowering=False)
    es = ExitStack()
    P = Prog(nc)

    def din(name, shape, dt=F32):
        return nc.dram_tensor(name, list(shape), dt, kind="ExternalInput").ap()

    def dout(name, shape, dt=F32):
        return nc.dram_tensor(name, list(shape), dt, kind="ExternalOutput").ap()

    def dscr(name, shape, dt=F32):
        return nc.dram_tensor(name, list(shape), dt, kind="Internal").ap()

    uniq = [0]

    def sb(name, shape, dt=F32, stack=None):
        uniq[0] += 1
        return (stack or es).enter_context(nc.sbuf_tensor("%s_%d" % (name, uniq[0]), list(shape), dt))

    x_d = din("x", [TOK, D])
    win_d = din("w_in", [2, D, NIN])
    ng_d = din("ng", [128, 2, 8])
    cs_d = din("cs", [128, NT, 32])
    qkg_d = din("qkg", [128, 2, 2, 128])
    gmask_d = din("gmask", [128, 8, NB])
    tri_d = din("tri", [128, 2, 256])
    smask_d = din("smask", [128, 4])
    hm_d = din("hm", [128, 4])
    bg_d = din("bgate", [128, 2, 16])
    cw_d = din("convw", [128, 2, 8, CK])
    cv_d = din("cvec", [128, 2, 3, 8])
    wap_d = din("w_ap", [2, D, D])
    wcp_d = din("w_cp", [2, D, D])
    wo_d = din("w_o", [2, D, D])
    y_d = dout("y", [TOK, D])
    gsc = dscr("gsc", [128, 8, TOK], BF16)
    xs1 = dscr("xs1", [TOK, D])
    NCH = 18
    gin = [[dscr("gin%d_%d" % (l, ch), [512, 2048]) for ch in range(NCH)] for l in range(2)]
    gout = [[dscr("gout%d_%d" % (l, ch), [512, 2048]) for ch in range(NCH)] for l in range(2)]
    ktown_d = dscr("ktown", [128, NH * TOK], BF16)
    vown_d = dscr("vownd", [128, NT * D], BF16)

    pb = [es.enter_context(nc.psum_tensor("pb%d" % i, [128, 512], F32)) for i in range(8)]
    pbf = [pb[i][:] for i in range(8)]
    pbh = [pb[i][:].bitcast(BF16) for i in range(8)]

    identf = sb("identf", [128, 128], F32)
    ident = sb("ident", [128, 128], BF16)
    ng2 = sb("ng2", [128, 2, 8])
    cs = sb("cs", [128, NT, 32])
    qkg2 = sb("qkg2", [128, 2, 2, 128])
    gmask = sb("gmask", [128, 8, NB])
    tri = sb("tri", [128, 2, 256])
    smask = sb("smask", [128, 4])
    hm = sb("hm", [128, 4])
    bgate2 = sb("bgate2", [128, 2, 16])
    cwT2 = sb("cwT2", [128, 2, 8, CK])
    cvec2 = sb("cvec2", [128, 2, 3, 8])
    kmb = sb("kmb", [128, NH, NB], BF16)
    ones_s = sb("ones_s", [128, 128], BF16)
    epsc = sb("epsc", [128, 1])
    NST, NBF = 2, 2
    wst = [sb("wst%d" % i, [128, 8, 512], F32) for i in range(NST)]
    wbf = [sb("wbf%d" % i, [128, 8, 512], BF16) for i in range(NBF)]
    bufA = sb("bufA", [128, 8 * 8 * GW], BF16)
    for i in range(NST):
        P.persist.add(("wst", i))
    for i in range(NBF):
        P.persist.add(("wbf", i))
    P.persist.add(("ng",))
    for l_ in range(2):
        for ch_ in range(NCH):
            P.persist.add(("gout", l_, ch_))

    def qT(h, t0, n):
        return _ap(bufA, h * TOK + t0, [[1, n]])

    def gT(c, blk0, nblk, off, n):
        return _ap(bufA, c * 8 * GW + blk0 * GW + off, [[GW, nblk], [1, n]])

    P.add("pool", lambda E: E.memset(identf[:], 0.0), writes=[("identf",)])
    P.add("pool", lambda E: E.affine_select(out=identf[:], in_=identf[:], compare_op=ALU.not_equal, fill=1.0,
                                            base=0, pattern=[[-1, 128]], channel_multiplier=1),
          reads=[("identf",)], writes=[("identf",)])
    P.add("pool", lambda E: E.tensor_copy(out=ident[:], in_=identf[:]), reads=[("identf",)], writes=[("ident",)])
    P.add("pool", lambda E: E.memset(epsc[:], float(EPS)), writes=[("epsc",)])
    P.add("pool", lambda E: E.memset(ones_s[:], 1.0 / D), writes=[("ones_s",)])
    for (t_, d_, k_) in ((ng2, ng_d, "ng"), (cs, cs_d, "cs"), (qkg2, qkg_d, "qkg"), (gmask, gmask_d, "gmask"),
                         (tri, tri_d, "tri"), (smask, smask_d, "smask"), (hm, hm_d, "hm"), (bgate2, bg_d, "bgate"),
                         (cwT2, cw_d, "cwT"), (cvec2, cv_d, "cvec")):
        P.add("sp", lambda E, t_=t_, d_=d_: E.dma_start(out=t_[:], in_=d_), writes=[(k_,)], dma=True)
    P.barrier()

    wlist = []
    for l_ in range(2):
        wlist += [(l_, "in", C_UA), (l_, "in", C_UB), (l_, "in", C_UA + 512), (l_, "in", C_UB + 512),
                  (l_, "in", C_K), (l_, "in", C_K + 512), (l_, "in", C_V), (l_, "in", C_V + 512),
                  (l_, "in", C_UA), (l_, "in", C_UB), (l_, "in", C_UA + 512), (l_, "in", C_UB + 512),
                  (l_, "in", C_ZC), (l_, "in", C_ZC + 512), (l_, "in", C_GC), (l_, "in", C_GC + 512),
                  (l_, "cp", 0), (l_, "cp", 512), (l_, "in", C_Q), (l_, "in", C_Q + 512),
                  (l_, "in", C_ZA), (l_, "in", C_ZA + 512), (l_, "in", C_GA), (l_, "in", C_GA + 512),
                  (l_, "ap", 0), (l_, "ap", 512), (l_, "o", 0), (l_, "o", 512)]
    wstate = {"emitted": 0, "next": 0}

    def w_load(j):
        lw, kind, c0 = wlist[j]
        base = {"in": win_d, "cp": wcp_d, "ap": wap_d, "o": wo_d}[kind]
        s_ap = base[lw].rearrange("(k p) c -> p k c", p=128)[:, :, c0:c0 + 512]
        st = j % NST
        P.add("sp", lambda E: E.dma_start(out=wst[st][:], in_=s_ap), writes=[("wst", st)], dma=True, nobar=True)

    def w_cast(j):
        lw, kind, c0 = wlist[j]
        st, bf = j % NST, j % NBF
        if kind == "in":
            ngb = _ap(ng2, lw * 8, [[1, 8], [0, 512]])
            P.add("dve", lambda E: E.tensor_tensor(out=wbf[bf][:], in0=wst[st][:], in1=ngb, op=ALU.mult),
                  reads=[("wst", st), ("ng",)], writes=[("wbf", bf)], nobar=True)
        else:
            P.add("act", lambda E: E.activation(out=wbf[bf][:], in_=wst[st][:], func=AF.Copy),
                  reads=[("wst", st)], writes=[("wbf", bf)], nobar=True)

    def w_next(hold=False):
        j = wstate["next"]
        wstate["next"] += 1
        while wstate["emitted"] <= min(len(wlist) - 1, j):
            w_load(wstate["emitted"])
            wstate["emitted"] += 1
        w_cast(j)
        while wstate["emitted"] <= min(len(wlist) - 1, j + 1):
            w_load(wstate["emitted"])
            wstate["emitted"] += 1
        return wbf[j % NBF], ("wbf", j % NBF)


    def layer(l):
        x_src = x_d if l == 0 else xs1
        y_dst = xs1 if l == 0 else y_d
        gin_l, gout_l = gin[l], gout[l]
        def stage_hT(hT, st):
            xt = [sb("xt%d" % i, [128, D], F32, st) for i in range(2)]
            sqj = sb("sqj", [128, D], BF16, st)
            xn = [sb("xn%d" % i, [128, D], BF16, st) for i in range(2)]
            ss = sb("ss", [128, NT], F32, st)
            rs_ = sb("rs_", [128, NT], F32, st)
            for tt in range(NT):
                b = tt % 2
                P.add("sp", lambda E, tt=tt, b=b: E.dma_start(out=xt[b][:], in_=x_src[tt * 128:(tt + 1) * 128, :]),
                      writes=[("xt", b)], dma=True)
                P.add("act", lambda E, tt=tt, b=b: E.activation(out=sqj[:], in_=xt[b][:], func=AF.Square,
                                                                accum_out=ss[:, tt:tt + 1]),
                      reads=[("xt", b)], writes=[("sqj",), ("ss", tt)])
                P.add("act", lambda E, tt=tt: E.activation(out=rs_[:, tt:tt + 1], in_=ss[:, tt:tt + 1], func=AF.Sqrt,
                                                           bias=epsc[:, 0:1], scale=1.0 / D),
                      reads=[("ss", tt), ("epsc",)], writes=[("rs_", tt)])
                P.add("dve", lambda E, tt=tt: E.reciprocal(out=rs_[:, tt:tt + 1], in_=rs_[:, tt:tt + 1]),
                      reads=[("rs_", tt)], writes=[("rs_", tt)])
                P.add("dve", lambda E, tt=tt, b=b: E.tensor_scalar(out=xn[b][:], in0=xt[b][:], scalar1=rs_[:, tt:tt + 1],
                                                                   scalar2=None, op0=ALU.mult),
                      reads=[("xt", b), ("rs_", tt)], writes=[("xn", b)])
                bank = 4 + b
                for c in range(8):
                    P.add("pe", lambda E, c=c, b=b, bank=bank: E.transpose(out=pbh[bank][:, c * 128:(c + 1) * 128],
                                                                           in_=xn[b][:, c * 128:(c + 1) * 128],
                                                                           identity=ident[:]),
                          reads=[("xn", b), ("ident",)], writes=[("pb", bank)])
                P.add("act", lambda E, tt=tt, bank=bank: E.activation(
                    out=_ap(hT, tt * 128, [[TOK, 8], [1, 128]]),
                    in_=pbh[bank].rearrange("p (c t) -> p c t", c=8), func=AF.Copy),
                    reads=[("pb", bank)], writes=[("hT", tt)])

        def stage_qk(hT, st, which, dst_fn, dst_key, after_group=None):
            sqk = sb("sqk", [128, 512], F32, st)
            ssk = sb("ssk", [128, 4], F32, st)
            rk = sb("rk", [128, 4], F32, st)
            kn = sb("kn", [128, 512], F32, st)
            kb = [sb("kb%d" % i, [128, 512], BF16, st) for i in range(2)]
            rt = [sb("rt%d" % i, [128, 4, 16], F32, st) for i in range(4)]
            gvec = _ap(qkg2, (l * 2 + which) * 128, [[0, 4], [1, 128]])
            it = 0
            for kg in range(2):
                wt, wkey = w_next()
                for tt in range(NT):
                    bank = 1 + it % 3
                    for kc in range(8):
                        P.add("pe", lambda E, kc=kc, tt=tt, bank=bank, wt=wt: E.matmul(
                            pbf[bank], lhsT=hT[:, kc, tt * 128:(tt + 1) * 128], rhs=wt[:, kc, :],
                            start=(kc == 0), stop=(kc == 7)),
                            reads=[("hT", tt), wkey], writes=[("pb", bank)])
                    ps3 = pbf[bank].rearrange("p (h d) -> p h d", h=4)
                    P.add("act", lambda E, bank=bank: E.activation(out=sqk[:], in_=pbf[bank], func=AF.Square),
                          reads=[("pb", bank)], writes=[("sqk",)])
                    P.add("dve", lambda E: E.tensor_reduce(out=ssk[:], in_=sqk[:].rearrange("p (h d) -> p h d", h=4),
                                                           axis=AX.X, op=ALU.add),
                          reads=[("sqk",)], writes=[("ssk",)])
                    P.add("act", lambda E: E.activation(out=rk[:], in_=ssk[:], func=AF.Sqrt, bias=epsc[:, 0:1], scale=1.0 / HD),
                          reads=[("ssk",), ("epsc",)], writes=[("rk",)])
                    P.add("dve", lambda E: E.reciprocal(out=rk[:], in_=rk[:]), reads=[("rk",)], writes=[("rk",)])
                    kn3 = kn[:].rearrange("p (h d) -> p h d", h=4)
                    P.add("dve", lambda E, ps3=ps3, kn3=kn3: E.tensor_tensor(
                        out=kn3, in0=ps3, in1=_ap(rk, 0, [[1, 4], [0, 128]]), op=ALU.mult),
                        reads=[("pb", bank), ("rk",)], writes=[("kn",)])
                    P.add("dve", lambda E, kn3=kn3: E.tensor_tensor(out=kn3, in0=kn3, in1=gvec, op=ALU.mult),
                          reads=[("kn",), ("qkg",)], writes=[("kn",)])
                    kbb = kb[it % 2]
                    kbk = ("kb", it % 2)
                    P.add("act", lambda E, kbb=kbb: E.activation(out=kbb[:], in_=kn[:], func=AF.Copy),
                          reads=[("kn",)], writes=[kbk])
                    t1 = _ap(kn, 0, [[128, 4], [1, 16]])
                    t2 = _ap(kn, 16, [[128, 4], [1, 16]])
                    cosb = _ap(cs, tt * 32, [[0, 4], [1, 16]])
                    sinb = _ap(cs, tt * 32 + 16, [[0, 4], [1, 16]])
                    o1 = _ap(kbb, 0, [[128, 4], [1, 16]])
                    o2 = _ap(kbb, 16, [[128, 4], [1, 16]])
                    P.add("dve", lambda E, t1=t1, cosb=cosb: E.tensor_tensor(out=rt[0][:], in0=t1, in1=cosb, op=ALU.mult),
                          reads=[("kn",), ("cs",)], writes=[("rt", 0)])
                    P.add("dve", lambda E, t2=t2, sinb=sinb: E.tensor_tensor(out=rt[1][:], in0=t2, in1=sinb, op=ALU.mult),
                          reads=[("kn",), ("cs",)], writes=[("rt", 1)])
                    P.add("dve", lambda E, t2=t2, cosb=cosb: E.tensor_tensor(out=rt[2][:], in0=t2, in1=cosb, op=ALU.mult),
                          reads=[("kn",), ("cs",)], writes=[("rt", 2)])
                    P.add("dve", lambda E, t1=t1, sinb=sinb: E.tensor_tensor(out=rt[3][:], in0=t1, in1=sinb, op=ALU.mult),
                          reads=[("kn",), ("cs",)], writes=[("rt", 3)])
                    P.add("dve", lambda E, o1=o1: E.tensor_tensor(out=o1, in0=rt[0][:], in1=rt[1][:], op=ALU.subtract),
                          reads=[("rt", 0), ("rt", 1), kbk], writes=[kbk])
                    P.add("dve", lambda E, o2=o2: E.tensor_tensor(out=o2, in0=rt[2][:], in1=rt[3][:], op=ALU.add),
                          reads=[("rt", 2), ("rt", 3), kbk], writes=[kbk])
                    tb = 5 + it % 2
                    for h in range(4):
                        P.add("pe", lambda E, h=h, tb=tb, kbb=kbb: E.transpose(
                            out=pbh[tb][:, h * 128:(h + 1) * 128], in_=kbb[:, h * 128:(h + 1) * 128], identity=ident[:]),
                            reads=[kbk, ("ident",)], writes=[("pb", tb)])
                    P.add("act", lambda E, tb=tb, kg=kg, tt=tt: E.activation(
                        out=dst_fn(kg * 4, tt), in_=pbh[tb][:, 0:512].rearrange("p (h t) -> p h t", h=4), func=AF.Copy),
                        reads=[("pb", tb)], writes=[(dst_key, kg * 4 + h, tt) for h in range(4)])
                    it += 1
                if after_group is not None:
                    after_group(kg)


        hT_stack = ExitStack()
        hT = sb("hT", [128, 8, TOK], BF16, hT_stack)
        st1 = ExitStack()
        stage_hT(hT, st1)
        P.barrier()
        st1.close()
        st2 = ExitStack()
        vown = sb("vown", [128, NT, D], BF16, st2)
        kms = sb("kms", [128, 64], F32, st2)
        sgA = sb("sgA", [128, 256], F32, st2)
        gtl = sb("gtl", [128, 8, 256], F32, st2)
        stg = [sb("stg%d" % i, [128, 2048], BF16, st2) for i in range(3)]
        stf = [sb("stf%d" % i, [128, 2048], F32, st2) for i in range(2)]
        skm = sb("skm", [128, 4, 64], F32, st2)
        allq = [("q", h, tt) for h in range(NH) for tt in range(NT)]
        allv = [("vown", tt, vg) for tt in range(NT) for vg in range(2)]
        cidx = [0]

        def gather_chunk(ch):
            gks = []
            for j in range(4):
                rows = slice(j * 128, (j + 1) * 128)
                mj = smask[:, j:j + 1]
                gk = ("gin", l, j, ch)
                if ch < 16:
                    ci = cidx[0]
                    cidx[0] += 1
                    s_ = stg[ci % 3]
                    if ch < 8:
                        src_ap, rk_ = qT(ch, 0, TOK), [("q", ch, tt) for tt in range(NT)]
                        o_ap = s_[:]
                    else:
                        hv = ch - 8
                        src_ap, rk_ = vown[:, :, hv * 128:(hv + 1) * 128], [("vown", tt, hv // 4) for tt in range(NT)]
                        o_ap = s_[:].rearrange("p (t d) -> p t d", d=128)
                    sk = ("stg", ci % 3)
                    if ci % 2 == 0:
                        P.add("act", lambda E, o_ap=o_ap, src_ap=src_ap, mj=mj: E.activation(out=o_ap, in_=src_ap, func=AF.Copy,
                                                                                             scale=mj),
                              reads=rk_ + [("smask",)], writes=[sk])
                    else:
                        P.add("dve", lambda E, o_ap=o_ap, src_ap=src_ap, mj=mj: E.tensor_scalar(out=o_ap, in0=src_ap, scalar1=mj,
                                                                                                scalar2=None, op0=ALU.mult),
                              reads=rk_ + [("smask",)], writes=[sk])
                    P.add("pool", lambda E, s_=s_, rows=rows: E.dma_start(out=gin_l[ch][rows, :], in_=s_[:]),
                          reads=[sk], writes=[gk], dma=True)
                elif ch == 16:
                    P.add("dve", lambda E, j=j, mj=mj: E.tensor_scalar(out=skm[:, j, :], in0=kms[:], scalar1=mj, scalar2=None,
                                                                       op0=ALU.mult),
                          reads=[("kms",), ("smask",)], writes=[("skm", j)])
                    P.add("pool", lambda E, j=j, rows=rows: E.dma_start(out=gin_l[ch][rows, 0:64], in_=skm[:, j, :]),
                          reads=[("skm", j)], writes=[gk], dma=True)
                else:
                    f_ = stf[j % 2]
                    P.add("dve", lambda E, f_=f_, mj=mj: E.tensor_scalar(out=f_[:], in0=gtl[:].rearrange("p c t -> p (c t)"),
                                                                         scalar1=mj, scalar2=None, op0=ALU.mult),
                          reads=[("gtl", c) for c in range(8)] + [("smask",)], writes=[("stf", j % 2)])
                    P.add("pool", lambda E, f_=f_, rows=rows: E.dma_start(out=gin_l[ch][rows, :], in_=f_[:]),
                          reads=[("stf", j % 2)], writes=[gk], dma=True)
                gks.append(gk)
            P.add("pool", lambda E: E.collective_compute("AllReduce", ALU.add, replica_groups=RG, ins=[gin_l[ch]],
                                                         outs=[gout_l[ch]]),
                  reads=gks, writes=[("gout", l, ch)], dma=True, nobar=True, cc=True)

        for g in range(2):
            wa, wakey = w_next()
            wb, wbkey = w_next(hold=True)
            for cc in range(4):
                c = 4 * g + cc
                for (bank, wt, wkey) in ((1, wa, wakey), (2, wb, wbkey)):
                    for kc in range(8):
                        rhsT = _ap(hT, kc * TOK + 224, [[BLK, 8], [1, 32]])
                        P.add("pe", lambda E, kc=kc, cc=cc, bank=bank, wt=wt, rhsT=rhsT: E.matmul(
                            pbf[bank][:, 0:256], lhsT=wt[:, kc, cc * 128:(cc + 1) * 128],
                            rhs=rhsT,
                            start=(kc == 0), stop=(kc == 7)),
                            reads=[("hT", tt) for tt in range(NT)] + [wkey], writes=[("pb", bank)])
                P.add("act", lambda E: E.activation(out=sgA[:], in_=pbf[2][:, 0:256], func=AF.Sigmoid),
                      reads=[("pb", 2)], writes=[("sg",)])
                P.add("dve", lambda E, c=c: E.tensor_tensor(out=gtl[:, c, :], in0=pbf[1][:, 0:256], in1=sgA[:], op=ALU.mult),
                      reads=[("pb", 1), ("sg",)], writes=[("gtl", c)])
        gather_chunk(17)

        def after_k(kg):
            for h_ in range(4 * kg, 4 * kg + 4):
                gather_chunk(h_)

        stage_qk(hT, st2, 1, lambda h0, tt: _ap(bufA, h0 * TOK + tt * 128, [[TOK, 4], [1, 128]]), "q", after_group=after_k)
        P.add("dve", lambda E: E.tensor_reduce(out=kms[:], in_=_ap(bufA, 0, [[BLK, 64], [1, BLK]]), axis=AX.X, op=ALU.add),
              reads=allq, writes=[("kms",)])
        P.add("dve", lambda E: E.tensor_scalar(out=kms[:], in0=kms[:], scalar1=1.0 / BLK, scalar2=None, op0=ALU.mult),
              reads=[("kms",)], writes=[("kms",)])
        gather_chunk(16)
        it = 0
        for vg in range(2):
            wt, wkey = w_next()
            for tt in range(NT):
                bank = 1 + it % 3
                for kc in range(8):
                    P.add("pe", lambda E, kc=kc, tt=tt, bank=bank, wt=wt, hT=hT: E.matmul(
                        pbf[bank], lhsT=hT[:, kc, tt * 128:(tt + 1) * 128], rhs=wt[:, kc, :],
                        start=(kc == 0), stop=(kc == 7)),
                        reads=[("hT", tt), wkey], writes=[("pb", bank)])
                P.add("act", lambda E, bank=bank, tt=tt, vg=vg: E.activation(
                    out=vown[:, tt, vg * 512:(vg + 1) * 512], in_=pbf[bank], func=AF.Copy),
                    reads=[("pb", bank)], writes=[("vown", tt, vg)])
                it += 1
            for h_ in range(4 * vg, 4 * vg + 4):
                gather_chunk(8 + h_)
        P.add("sp", lambda E: E.dma_start(out=ktown_d, in_=_ap(bufA, 0, [[1, NH * TOK]])), reads=allq, writes=[("ktown",)], dma=True)
        P.add("sp", lambda E: E.dma_start(out=vown_d, in_=vown[:].rearrange("p t d -> p (t d)")), reads=allv, writes=[("vownd",)],
              dma=True)
        P.barrier()
        st2.close()
        def proj_fm(wt, wkey, cc, T, bank, rhs_fn=None, rkeys=None):
            for kc in range(8):
                rhs = hT[:, kc, T * 512:(T + 1) * 512] if rhs_fn is None else rhs_fn(kc)
                rk_ = [("hT", 4 * T + j) for j in range(4)] if rkeys is None else rkeys(kc)
                o_ = pbf[bank] if len(rhs.shape) == 2 else pbf[bank].rearrange("p (b t) -> p b t", b=rhs.shape[1])
                P.add("pe", lambda E, kc=kc, rhs=rhs, o_=o_: E.matmul(o_, lhsT=wt[:, kc, cc * 128:(cc + 1) * 128], rhs=rhs,
                                                                      start=(kc == 0), stop=(kc == 7)),
                      reads=rk_ + [wkey], writes=[("pb", bank)])

        st3 = ExitStack()
        sg = [sb("sg%d" % i, [128, 512], F32, st3) for i in range(2)]
        it = 0
        for g in range(2):
            wa, wakey = w_next()
            wb_, wbkey = w_next(hold=True)
            for cc in range(4):
                c = 4 * g + cc
                for T in range(4):
                    ba, bb = 1 + (it % 2) * 2, 2 + (it % 2) * 2
                    proj_fm(wa, wakey, cc, T, ba)
                    proj_fm(wb_, wbkey, cc, T, bb)
                    s_ = sg[it % 2]
                    P.add("act", lambda E, s_=s_, bb=bb: E.activation(out=s_[:], in_=pbf[bb], func=AF.Sigmoid),
                          reads=[("pb", bb)], writes=[("sg", it % 2)])
                    P.add("dve", lambda E, s_=s_, ba=ba, c=c, T=T: E.tensor_tensor(
                        out=gT(c, 2 * T, 2, 32, 256), in0=pbf[ba].rearrange("p (b t) -> p b t", b=2),
                        in1=s_[:].rearrange("p (b t) -> p b t", b=2), op=ALU.mult),
                        reads=[("pb", ba), ("sg", it % 2)], writes=[("g", c, T)])
                    it += 1
        P.barrier()
        st3.close()

        sth = ExitStack()
        hl4 = sb("hl4", [128, 4, 2048], F32, sth)
        hacc = sb("hacc", [128, 2048], F32, sth)
        kmraw = sb("kmraw", [128, 4, 64], F32, sth)
        P.add("dve", lambda E: E.memset(hl4[:, 3, :], 0.0), writes=[("hl4", 3)])
        for j in range(3):
            P.add("sp", lambda E, j=j: E.dma_start(out=hl4[:, j, :], in_=gout_l[17][j * 128:(j + 1) * 128, :]),
                  reads=[("gout", l, 17)], writes=[("hl4", j)], dma=True)
        P.add("sp", lambda E: E.dma_start(
            out=_ap(hl4, 3 * 2048 + 32, [[256, 8], [1, 224]]),
            in_=gout_l[17][384:512, :].rearrange("p (c t) -> p c t", c=8)[:, :, 0:224]),
            reads=[("gout", l, 17), ("hl4", 3)], writes=[("hl4", 3)], dma=True)
        P.add("dve", lambda E: E.tensor_scalar(out=hacc[:], in0=hl4[:, 0, :], scalar1=hm[:, 0:1], scalar2=None, op0=ALU.mult),
              reads=[("hl4", 0), ("hm",)], writes=[("hacc",)])
        for j in range(1, 4):
            P.add("dve", lambda E, j=j: E.scalar_tensor_tensor(out=hacc[:], in0=hl4[:, j, :], scalar=hm[:, j:j + 1], in1=hacc[:],
                                                               op0=ALU.mult, op1=ALU.add),
                  reads=[("hl4", j), ("hm",), ("hacc",)], writes=[("hacc",)])
        P.add("dve", lambda E: E.tensor_copy(
            out=_ap(bufA, 0, [[GW, 64], [1, 32]]), in_=hacc[:].rearrange("p (ci t) -> p ci t", t=32)),
            reads=[("hacc",)], writes=[("gh",)])
        for rr in range(4):
            P.add("sp", lambda E, rr=rr: E.dma_start(out=kmraw[:, rr, :], in_=gout_l[16][rr * 128:(rr + 1) * 128, 0:64]),
                  reads=[("gout", l, 16)], writes=[("kmraw", rr)], dma=True)
        P.add("dve", lambda E: E.tensor_copy(out=_ap(kmb, 0, [[1, 4], [NB, 8], [4, 8]]),
                                             in_=kmraw[:].rearrange("p r (h i) -> p r h i", h=8)),
              reads=[("kmraw", rr) for rr in range(4)], writes=[("kmb",)])
        P.barrier()
        sth.close()

        st4 = ExitStack()
        gcy4_stack = ExitStack()
        gcy4 = sb("gcyc", [128, 8, TOK], BF16, gcy4_stack)
        cpre = sb("cpre", [128, 8, 512], F32, st4)
        cb = [sb("cb%d" % i, [128, 512], BF16, st4) for i in range(2)]
        csq = [sb("csq%d" % i, [128, 512], BF16, st4) for i in range(2)]
        mean = sb("mean", [128, 512], F32, st4)
        msq = sb("msq", [128, 512], F32, st4)
        rstd = sb("rstd", [128, 512], F32, st4)
        dw = sb("dw", [128, CK, 128], BF16, st4)
        sz = [sb("sz%d" % i, [128, 512], F32, st4) for i in range(2)]
        it = 0
        for T in range(4):
            for c in range(8):
                P.add("dve", lambda E, c=c: E.tensor_tensor(
                    out=dw[:], in0=_ap(identf, 0, [[0, CK], [1, 128]]), in1=_ap(cwT2, (l * 8 + c) * CK, [[1, CK], [0, 128]]),
                    op=ALU.mult), reads=[("identf",), ("cwT",)], writes=[("dw",)])
                bank = 1 + it % 2
                for j in range(CK):
                    P.add("pe", lambda E, j=j, c=c, T=T, bank=bank: E.matmul(
                        pbf[bank].rearrange("p (b t) -> p b t", b=2), lhsT=dw[:, j, :], rhs=gT(c, 2 * T, 2, 2 + j, 256),
                        start=(j == 0), stop=(j == CK - 1)),
                        reads=[("dw",), ("g", c, T), ("gh",)], writes=[("pb", bank)])
                P.add("act", lambda E, c=c, bank=bank: E.activation(out=cpre[:, c, :], in_=pbf[bank], func=AF.Identity,
                                                                    bias=cvec2[:, l, 0, c:c + 1]),
                      reads=[("pb", bank), ("cvec",)], writes=[("cpre", c)])
                P.add("act", lambda E, c=c, bank=bank, k=it % 2: E.activation(out=csq[k][:], in_=pbf[bank], func=AF.Square,
                                                                    bias=cvec2[:, l, 0, c:c + 1]),
                      reads=[("pb", bank), ("cvec",)], writes=[("csq", it % 2)])
                P.add("dve", lambda E, c=c, k=it % 2: E.tensor_copy(out=cb[k][:], in_=cpre[:, c, :]),
                      reads=[("cpre", c)], writes=[("cb", it % 2)])
                P.add("pe", lambda E, c=c, k=it % 2: E.matmul(pbf[3], lhsT=ones_s[:], rhs=cb[k][:], start=(c == 0), stop=(c == 7)),
                      reads=[("ones_s",), ("cb", it % 2)], writes=[("pb", 3)])
                P.add("pe", lambda E, c=c, k=it % 2: E.matmul(pbf[4], lhsT=ones_s[:], rhs=csq[k][:], start=(c == 0), stop=(c == 7)),
                      reads=[("ones_s",), ("csq", it % 2)], writes=[("pb", 4)])
                it += 1
            P.add("dve", lambda E: E.tensor_copy(out=mean[:], in_=pbf[3]), reads=[("pb", 3)], writes=[("mean",)])
            P.add("dve", lambda E: E.tensor_tensor(out=msq[:], in0=mean[:], in1=mean[:], op=ALU.mult),
                  reads=[("mean",)], writes=[("msq",)])
            P.add("dve", lambda E: E.tensor_tensor(out=rstd[:], in0=pbf[4], in1=msq[:], op=ALU.subtract),
                  reads=[("pb", 4), ("msq",)], writes=[("rstd",)])
            P.add("act", lambda E: E.activation(out=rstd[:], in_=rstd[:], func=AF.Sqrt, bias=epsc[:, 0:1]),
                  reads=[("rstd",), ("epsc",)], writes=[("rstd",)])
            P.add("dve", lambda E: E.reciprocal(out=rstd[:], in_=rstd[:]), reads=[("rstd",)], writes=[("rstd",)])
            for c in range(8):
                P.add("dve", lambda E, c=c: E.tensor_tensor(out=cpre[:, c, :], in0=cpre[:, c, :], in1=mean[:], op=ALU.subtract),
                      reads=[("cpre", c), ("mean",)], writes=[("cpre", c)])
                P.add("dve", lambda E, c=c: E.tensor_tensor(out=cpre[:, c, :], in0=cpre[:, c, :], in1=rstd[:], op=ALU.mult),
                      reads=[("cpre", c), ("rstd",)], writes=[("cpre", c)])
                P.add("act", lambda E, c=c, T=T: E.activation(
                    out=gT(c, 2 * T, 2, 32, 256), in_=cpre[:, c, :].rearrange("p (b t) -> p b t", b=2), func=AF.Silu,
                    bias=cvec2[:, l, 2, c:c + 1], scale=cvec2[:, l, 1, c:c + 1]),
                    reads=[("cpre", c), ("cvec",)], writes=[("g", c, T)])
        it = 0
        for g in range(2):
            wt, wkey = w_next()
            for cc in range(4):
                c = 4 * g + cc
                for T in range(4):
                    bank = 5 + it % 2
                    proj_fm(wt, wkey, cc, T, bank)
                    s_ = sz[it % 2]
                    P.add("act", lambda E, s_=s_, bank=bank: E.activation(out=s_[:], in_=pbf[bank], func=AF.Silu),
                          reads=[("pb", bank)], writes=[("sz", it % 2)])
                    P.add("dve", lambda E, s_=s_, c=c, T=T: E.tensor_tensor(
                        out=gT(c, 2 * T, 2, 32, 256), in0=gT(c, 2 * T, 2, 32, 256),
                        in1=s_[:].rearrange("p (b t) -> p b t", b=2), op=ALU.mult),
                        reads=[("g", c, T), ("sz", it % 2)], writes=[("g", c, T)])
                    it += 1
        for g in range(2):
            wt, wkey = w_next()
            for cc in range(4):
                c = 4 * g + cc
                for T in range(4):
                    bank = 5 + it % 2
                    proj_fm(wt, wkey, cc, T, bank)
                    P.add("act", lambda E, bank=bank, c=c, T=T: E.activation(
                        out=gcy4[:, c, T * 512:(T + 1) * 512], in_=pbf[bank], func=AF.Sigmoid, bias=bgate2[:, l, 8 + c:9 + c]),
                        reads=[("pb", bank), ("bgate",)], writes=[("gc", c, T)])
                    it += 1
        for g in range(2):
            wt, wkey = w_next()
            for cc in range(4):
                c = 4 * g + cc
                for T in range(4):
                    bank = 5 + it % 2
                    proj_fm(wt, wkey, cc, T, bank, rhs_fn=lambda kc, T=T: gT(kc, 2 * T, 2, 32, 256),
                            rkeys=lambda kc, T=T: [("g", kc, T)])
                    P.add("dve", lambda E, bank=bank, c=c, T=T: E.tensor_tensor(
                        out=gcy4[:, c, T * 512:(T + 1) * 512], in0=pbf[bank], in1=gcy4[:, c, T * 512:(T + 1) * 512],
                        op=ALU.mult), reads=[("pb", bank), ("gc", c, T)], writes=[("gc", c, T)])
                    it += 1
        P.add("pool", lambda E: E.dma_start(out=gsc, in_=gcy4[:]),
              reads=[("gc", c, T) for c in range(8) for T in range(4)], writes=[("gsc",)], dma=True)
        P.barrier()
        st4.close()
        gcy4_stack.close()

        st5 = ExitStack()
        stage_qk(hT, st5, 0, lambda h0, tt: _ap(bufA, h0 * TOK + tt * 128, [[TOK, 4], [1, 128]]), "q")
        P.barrier()
        st5.close()
        hT_stack.close()

        st6 = ExitStack()
        kth = [sb("kth%d" % i, [128, SEQ], BF16, st6) for i in range(2)]
        vh = [sb("vh%d" % i, [128, 64, 128], BF16, st6) for i in range(2)]
        kto = [sb("kto%d" % i, [128, TOK], BF16, st6) for i in range(2)]
        vo = [sb("vo%d" % i, [128, NT, 128], BF16, st6) for i in range(2)]
        gm = sb("gm", [128, NB], F32, st6)
        m8 = sb("m8", [128, 8], F32, st6)
        thr = sb("thr", [128, 1], F32, st6)
        biasb = [sb("biasb%d" % i, [128, NB], F32, st6) for i in range(2)]
        rsb = [sb("rsb%d" % i, [128, 40], F32, st6) for i in range(2)]
        Pb = [sb("Pb%d" % i, [128, 512], BF16, st6) for i in range(3)]
        PTb = [sb("PTb%d" % i, [128, 512], BF16, st6) for i in range(3)]
        scm = [sb("scm%d" % i, [128, 256], F32, st6) for i in range(2)]
        rsum = sb("rsum", [128, 1], F32, st6)
        rinv = sb("rinv", [128, 1], F32, st6)
        obf = [sb("obf%d" % i, [128, 128], BF16, st6) for i in range(2)]

        def load_head(h):
            hb = h % 2
            for rr in range(4):
                P.add("pool", lambda E, rr=rr: E.dma_start(
                    out=_ap(kth[hb], rr * BLK, [[4 * BLK, 8], [1, BLK]]),
                    in_=gout_l[h][rr * 128:(rr + 1) * 128, :].rearrange("p (i t) -> p i t", t=BLK)),
                    reads=[("gout", l, h)], writes=[("kth", hb)], dma=True)
            for rr in range(4):
                P.add("pool", lambda E, rr=rr: E.dma_start(
                    out=_ap(vh[hb], rr * 256, [[1024, 8], [1, 256]]),
                    in_=gout_l[8 + h][rr * 128:(rr + 1) * 128, :].rearrange("p (i t) -> p i t", t=256)),
                    reads=[("gout", l, 8 + h)], writes=[("vh", hb)], dma=True)
            P.add("sp", lambda E: E.dma_start(out=kto[hb][:], in_=ktown_d[:, h * TOK:(h + 1) * TOK]), writes=[("kto", hb)], dma=True)
            vsrc = vown_d.rearrange("p (t d) -> p t d", d=D)
            for a in range(2):
                P.add("sp", lambda E, a=a: E.dma_start(out=vo[hb][:, 8 * a:8 * a + 8, :],
                                                       in_=vsrc[:, 8 * a:8 * a + 8, h * 128:(h + 1) * 128]),
                      writes=[("vo", hb)], dma=True)

        load_head(0)
        sidx = [0]
        for h in range(NH):
            hb = h % 2
            if h + 1 < NH:
                load_head(h + 1)
            items = []
            for qt in range(NT):
                i, half = qt // 2, qt % 2
                nb = 4 * i + 3
                col = 0
                ng_ = (nb + 1) // 2
                for g in range(ng_):
                    nblk = min(2, nb - 2 * g)
                    items.append(dict(qt=qt, kind="g", g=g, nblk=nblk, ncols=256 * nblk, first=(g == 0), last=False, col=col))
                    col += nblk
                items.append(dict(qt=qt, kind="d", ncols=128 * (half + 1), first=False, last=True, col=col))
            for it_ in items:
                it_["s"] = sidx[0]
                sidx[0] += 1

            def do_qk(itm):
                qt, s = itm["qt"], itm["s"]
                i, half = qt // 2, qt % 2
                qslice = qT(h, qt * 128, 128)
                kmh = kmb[:, h, :]
                if itm["first"]:
                    P.add("pe", lambda E: E.matmul(pbf[0][:, 0:NB], lhsT=qslice, rhs=kmh, start=True, stop=True),
                          reads=[("q", h, qt), ("kmb",)], writes=[("pb", 0)])
                    P.add("dve", lambda E: E.tensor_tensor(out=gm[:], in0=pbf[0][:, 0:NB], in1=gmask[:, i, :], op=ALU.add),
                          reads=[("pb", 0), ("gmask",)], writes=[("gm",)])
                    P.add("dve", lambda E: E.max(out=m8[:], in_=gm[:]), reads=[("gm",)], writes=[("m8",)])
                    P.add("dve", lambda E: E.tensor_scalar(out=thr[:], in0=m8[:, 2:3], scalar1=-1e30, scalar2=None, op0=ALU.max),
                          reads=[("m8",)], writes=[("thr",)])
                    P.add("dve", lambda E: E.tensor_scalar(out=biasb[qt % 2][:], in0=gm[:], scalar1=thr[:, 0:1], scalar2=NEG,
                                                           op0=ALU.is_lt, op1=ALU.mult),
                          reads=[("gm",), ("thr",)], writes=[("biasb", qt % 2)])
                bank = 1 + s % 3
                nc_ = itm["ncols"]
                if itm["kind"] == "g":
                    rhs = kth[hb][:, 512 * itm["g"]:512 * itm["g"] + nc_]
                    rk_ = ("kth", hb)
                else:
                    rhs = kto[hb][:, i * BLK:i * BLK + nc_]
                    rk_ = ("kto", hb)
                P.add("pe", lambda E: E.matmul(pbf[bank][:, 0:nc_], lhsT=qslice, rhs=rhs, start=True, stop=True),
                      reads=[("q", h, qt), rk_], writes=[("pb", bank)])

            def do_exp(itm):
                qt, s = itm["qt"], itm["s"]
                half = qt % 2
                bank = 1 + s % 3
                pt_, pk = Pb[s % 3], ("Pb", s % 3)
                nc_ = itm["ncols"]
                if itm["kind"] == "g":
                    for b_ in range(itm["nblk"]):
                        n = 2 * itm["g"] + b_
                        col = itm["col"] + b_
                        P.add("act", lambda E, b_=b_, n=n, col=col: E.activation(
                            out=pt_[:, b_ * 256:(b_ + 1) * 256], in_=pbf[bank][:, b_ * 256:(b_ + 1) * 256], func=AF.Exp,
                            bias=biasb[qt % 2][:, n:n + 1], scale=SCALE, accum_out=rsb[qt % 2][:, col:col + 1]),
                            reads=[("pb", bank), ("biasb", qt % 2)], writes=[pk, ("rsb", qt % 2, col)])
                else:
                    sm, sk = scm[s % 2], ("scm", s % 2)
                    col = itm["col"]
                    P.add("dve", lambda E: E.tensor_tensor(out=sm[:, 0:nc_], in0=pbf[bank][:, 0:nc_], in1=tri[:, half, 0:nc_],
                                                           op=ALU.add), reads=[("pb", bank), ("tri",)], writes=[sk])
                    P.add("act", lambda E: E.activation(out=pt_[:, 0:nc_], in_=sm[:, 0:nc_], func=AF.Exp, scale=SCALE,
                                                        accum_out=rsb[qt % 2][:, col:col + 1]),
                          reads=[sk], writes=[pk, ("rsb", qt % 2, col)])

            def do_tr(itm):
                s = itm["s"]
                tb = 4 + s % 2
                for c in range(itm["ncols"] // 128):
                    P.add("pe", lambda E, c=c: E.transpose(out=pbh[tb][:, c * 128:(c + 1) * 128],
                                                           in_=Pb[s % 3][:, c * 128:(c + 1) * 128], identity=ident[:]),
                          reads=[("Pb", s % 3), ("ident",)], writes=[("ptp", tb)])

            def do_pv(itm):
                qt, s = itm["qt"], itm["s"]
                i = qt // 2
                tb = 4 + s % 2
                nc_ = itm["ncols"]
                P.add("dve", lambda E: E.tensor_copy(out=PTb[s % 3][:, 0:nc_], in_=pbh[tb][:, 0:nc_]),
                      reads=[("ptp", tb)], writes=[("PTb", s % 3)])
                ob = 6 + qt % 2
                nchunk = nc_ // 128
                for c in range(nchunk):
                    if itm["kind"] == "g":
                        rhs = vh[hb][:, 4 * itm["g"] + c, :]
                        rk_ = ("vh", hb)
                    else:
                        rhs = vo[hb][:, 2 * i + c, :]
                        rk_ = ("vo", hb)
                    st_ = itm["first"] and c == 0
                    sp_ = itm["last"] and c == nchunk - 1
                    P.add("pe", lambda E, c=c, rhs=rhs, st_=st_, sp_=sp_: E.matmul(
                        pbf[ob][:, 0:128], lhsT=PTb[s % 3][:, c * 128:(c + 1) * 128], rhs=rhs, start=st_, stop=sp_),
                        reads=[("PTb", s % 3), rk_], writes=[("ops", ob)])
                if itm["last"]:
                    ncol = itm["col"] + 1
                    P.add("dve", lambda E: E.tensor_reduce(out=rsum[:], in_=rsb[qt % 2][:, 0:ncol], axis=AX.X, op=ALU.add),
                          reads=[("rsb", qt % 2, c_) for c_ in range(ncol)], writes=[("rsum",)])
                    P.add("dve", lambda E: E.reciprocal(out=rinv[:], in_=rsum[:]), reads=[("rsum",)], writes=[("rinv",)])
                    P.add("dve", lambda E: E.tensor_scalar(out=obf[qt % 2][:], in0=pbf[ob][:, 0:128], scalar1=rinv[:, 0:1],
                                                           scalar2=None, op0=ALU.mult),
                          reads=[("ops", ob), ("rinv",)], writes=[("obf", qt % 2)])
                    ot_ = pbh[0][:, 512:640]
                    P.add("pe", lambda E: E.transpose(out=ot_, in_=obf[qt % 2][:], identity=ident[:]),
                          reads=[("obf", qt % 2), ("ident",)], writes=[("pb", 0)])
                    odst = qT(h, qt * 128, 128)
                    P.add("dve", lambda E: E.tensor_copy(out=odst, in_=ot_),
                          reads=[("pb", 0)], writes=[("q", h, qt)])

            n_it = len(items)
            for s_ in range(n_it + 3):
                if s_ < n_it:
                    do_qk(items[s_])
                if 0 <= s_ - 1 < n_it:
                    do_exp(items[s_ - 1])
                if 0 <= s_ - 2 < n_it:
                    do_tr(items[s_ - 2])
                if 0 <= s_ - 3 < n_it:
                    do_pv(items[s_ - 3])
        P.barrier()
        st6.close()

        st7 = ExitStack()
        ga = sb("ga", [128, 8, TOK], BF16, st7)
        hT_stack = ExitStack()
        hT = sb("hT", [128, 8, TOK], BF16, hT_stack)
        st1 = ExitStack()
        stage_hT(hT, st1)
        P.barrier()
        st1.close()
        st7a = ExitStack()
        sz = [sb("sz%d" % i, [128, 512], F32, st7a) for i in range(2)]
        it = 0
        for g in range(2):
            wt, wkey = w_next()
            for cc in range(4):
                c = 4 * g + cc
                for T in range(4):
                    bank = 1 + it % 3
                    proj_fm(wt, wkey, cc, T, bank)
                    s_ = sz[it % 2]
                    P.add("act", lambda E, s_=s_, bank=bank: E.activation(out=s_[:], in_=pbf[bank], func=AF.Silu),
                          reads=[("pb", bank)], writes=[("sz", it % 2)])
                    P.add("dve", lambda E, s_=s_, c=c, T=T: E.tensor_tensor(
                        out=qT(c, T * 512, 512), in0=qT(c, T * 512, 512), in1=s_[:], op=ALU.mult),
                        reads=[("q", c, 4 * T + j) for j in range(4)] + [("sz", it % 2)],
                        writes=[("q", c, 4 * T + j) for j in range(4)])
                    it += 1
        for g in range(2):
            wt, wkey = w_next()
            for cc in range(4):
                c = 4 * g + cc
                for T in range(4):
                    bank = 1 + it % 3
                    proj_fm(wt, wkey, cc, T, bank)
                    P.add("act", lambda E, bank=bank, c=c, T=T: E.activation(
                        out=ga[:, c, T * 512:(T + 1) * 512], in_=pbf[bank], func=AF.Sigmoid, bias=bgate2[:, l, c:c + 1]),
                        reads=[("pb", bank), ("bgate",)], writes=[("ga", c, T)])
                    it += 1
        P.barrier()
        st7a.close()
        hT_stack.close()
        gcyc = sb("gcyc", [128, 8, TOK], BF16, st7)
        tmpf = [sb("tmpf%d" % i, [128, 512], F32, st7) for i in range(2)]
        xt = [sb("xo%d" % i, [128, D], F32, st7) for i in range(2)]
        ot = [sb("ot%d" % i, [128, D], F32, st7) for i in range(2)]
        P.add("sp", lambda E: E.dma_start(out=gcyc[:], in_=gsc), reads=[("gsc",)],
              writes=[("gc", c, T) for c in range(8) for T in range(4)], dma=True)
        for g in range(2):
            wt, wkey = w_next()
            for cc in range(4):
                c = 4 * g + cc
                for T in range(4):
                    bank = 1 + it % 3
                    proj_fm(wt, wkey, cc, T, bank, rhs_fn=lambda kc, T=T: qT(kc, T * 512, 512),
                            rkeys=lambda kc, T=T: [("q", kc, 4 * T + j) for j in range(4)])
                    tf = tmpf[it % 2]
                    P.add("dve", lambda E, tf=tf, bank=bank, c=c, T=T: E.tensor_tensor(
                        out=tf[:], in0=pbf[bank], in1=ga[:, c, T * 512:(T + 1) * 512], op=ALU.mult),
                        reads=[("pb", bank), ("ga", c, T)], writes=[("tmpf", it % 2)])
                    P.add("dve", lambda E, tf=tf, c=c, T=T: E.tensor_tensor(
                        out=gcyc[:, c, T * 512:(T + 1) * 512], in0=tf[:], in1=gcyc[:, c, T * 512:(T + 1) * 512], op=ALU.add),
                        reads=[("tmpf", it % 2), ("gc", c, T)], writes=[("gc", c, T)])
                    it += 1
        wo0, wo0k = w_next()
        wo1, wo1k = w_next(hold=True)
        for tt in range(NT):
            b = tt % 2
            P.add("sp", lambda E, tt=tt, b=b: E.dma_start(out=xt[b][:], in_=x_src[tt * 128:(tt + 1) * 128, :]),
                  writes=[("xo", b)], dma=True)
            for cg, (wt, wkey) in enumerate(((wo0, wo0k), (wo1, wo1k))):
                bank = 1 + it % 3
                for kc in range(8):
                    P.add("pe", lambda E, kc=kc, tt=tt, bank=bank, wt=wt: E.matmul(
                        pbf[bank], lhsT=gcyc[:, kc, tt * 128:(tt + 1) * 128], rhs=wt[:, kc, :],
                        start=(kc == 0), stop=(kc == 7)),
                        reads=[("gc", kc, tt // 4), wkey], writes=[("pb", bank)])
                P.add("dve", lambda E, bank=bank, b=b, cg=cg: E.tensor_tensor(
                    out=ot[b][:, cg * 512:(cg + 1) * 512], in0=pbf[bank], in1=xt[b][:, cg * 512:(cg + 1) * 512], op=ALU.add),
                    reads=[("pb", bank), ("xo", b)], writes=[("ot", b, cg)])
                it += 1
            P.add("pool", lambda E, tt=tt, b=b: E.dma_start(out=y_dst[tt * 128:(tt + 1) * 128, :], in_=ot[b][:]),
                  reads=[("ot", b, 0), ("ot", b, 1)], writes=[("y", tt)], dma=True)
        P.add("sp", None, reads=[("y", tt) for tt in range(NT)])
        P.barrier()
        st7.close()

    layer(0)
    layer(1)
    P.emit(es)
    es.close()
    return nc, P


_CACHE = {}


def _prog():
    if "F" not in _CACHE:
        _CACHE["F"] = build()[0]
    return _CACHE["F"]


def _own_blocks(r):
    return [4 * i + r for i in range(8)]


def _rope_table(r):
    inv_freq = (np.float32(500000.0) ** (-np.arange(0, 32, 2, dtype=np.float32) / np.float32(32))).astype(np.float32)
    pos = np.concatenate([np.arange(b * BLK, (b + 1) * BLK) for b in _own_blocks(r)]).astype(np.float32)
    ang = (pos[:, None] * inv_freq[None, :]).astype(np.float32)
    cs = np.concatenate([np.cos(ang), np.sin(ang)], axis=1).astype(np.float32)
    return np.ascontiguousarray(cs.reshape(NT, 128, 32).transpose(1, 0, 2))


def _consts(r):
    gmask = np.full((8, NB), -1e36, np.float32)
    for i in range(8):
        gmask[i, :4 * i + r] = 0.0
    gmask = np.ascontiguousarray(np.broadcast_to(gmask[None], (128, 8, NB)))
    q = np.arange(128)[:, None]
    k = np.arange(128)[None, :]
    t = np.where(k <= q, 0.0, NEG).astype(np.float32)
    tri = np.zeros((128, 2, 256), np.float32)
    tri[:, 0, :128] = t
    tri[:, 0, 128:] = NEG
    tri[:, 1, 128:] = t
    smask = np.zeros((128, 4), np.float32)
    smask[:, r] = 1.0
    hm = np.zeros((128, 4), np.float32)
    hm[:, (r - 1) % 4] = 1.0
    return gmask, tri, smask, hm


def _pk2(v):
    return np.ascontiguousarray(np.asarray(v, np.float32).reshape(2, 8, 128).transpose(2, 0, 1))


def kernel(**inputs):
    p = {k: np.asarray(v) for k, v in inputs.items()}
    x = np.asarray(p["x"], np.float32)
    f32 = lambda a: np.ascontiguousarray(np.asarray(a, np.float32))
    shared = {
        "w_in": f32(p["w_in"]), "ng": _pk2(p["norm_g"]),
        "qkg": np.ascontiguousarray(np.broadcast_to(
            np.stack([p["q_norm_g"], p["k_norm_g"]], axis=1)[None], (128, 2, 2, 128))).astype(np.float32),
        "bgate": np.ascontiguousarray(np.asarray(p["b_gate"], np.float32).reshape(2, 16, 128).transpose(2, 0, 1)),
        "convw": np.ascontiguousarray(np.asarray(p["conv_w"], np.float32).reshape(2, CK, 8, 128).transpose(3, 0, 2, 1)),
        "cvec": np.ascontiguousarray(np.stack([_pk2(p["conv_b"]), _pk2(p["cn_g"]), _pk2(p["cn_b"])], axis=2)),
        "w_ap": f32(p["w_attn_proj"]), "w_cp": f32(p["w_conv_proj"]), "w_o": f32(p["w_out"]),
    }
    ins = []
    for c in range(8):
        b, r = c // 4, c % 4
        gmask, tri, smask, hm = _consts(r)
        d = dict(shared)
        d.update({"x": np.ascontiguousarray(x[b].reshape(NB, BLK, D)[_own_blocks(r)].reshape(TOK, D)),
                  "cs": _rope_table(r), "gmask": gmask, "tri": tri, "smask": smask, "hm": hm})
        ins.append(d)
    res = run_bass_kernel_spmd(_prog(), ins, core_ids=list(range(8))).results
    out = np.empty((2, NB, BLK, D), np.float32)
    for c in range(8):
        b, r = c // 4, c % 4
        out[b, _own_blocks(r)] = np.asarray(res[c]["y"], np.float32).reshape(8, BLK, D)
    return out.reshape(2, SEQ, D)
```

```python
import numpy as np
import ml_dtypes
from contextlib import ExitStack
import concourse.bass as bass
import concourse.mybir as mybir
from concourse.bass_utils import run_bass_kernel_spmd

F32 = mybir.dt.float32
BF16 = mybir.dt.bfloat16
AF = mybir.ActivationFunctionType
ALU = mybir.AluOpType
AX = mybir.AxisListType
NPBF = ml_dtypes.bfloat16

D = 1024
SEQ = 8192
NB = 32
BLK = 256
NH = 8
HD = 128
NIN = 9216
TOK = 2048
NT = 16
EPS = 1e-6
NEG = -30000.0
SCALE = 1.0 / float(np.sqrt(HD))
CK = 31
GW = 288

C_Q, C_K, C_V, C_ZA, C_UA, C_UB, C_ZC, C_GA, C_GC = 0, 1024, 2048, 3072, 4096, 5120, 6144, 7168, 8192


class _Op:
    __slots__ = ("eng", "fn", "deps", "dma", "need", "ord", "dsem", "dval", "nobar", "cc")


class Prog:
    CE = ("pe", "act", "dve", "pool")
    ALL = ("sp", "pool", "act", "dve", "pe")
    NDS = 8

    def __init__(self, nc):
        self.nc = nc
        self.ops = []
        self.lastw = {}
        self.readers = {}
        self.persist = set()
        self.since_bar = []

    def add(self, eng, fn, reads=(), writes=(), dma=False, nobar=False, cc=False):
        op = _Op()
        op.cc = cc
        op.eng, op.fn, op.dma, op.need, op.ord, op.dsem, op.dval, op.nobar = eng, fn, dma, False, 0, None, 0, nobar
        deps = {}
        for b in reads:
            w = self.lastw.get(b)
            if w is not None:
                deps[id(w)] = w
        for b in writes:
            w = self.lastw.get(b)
            if w is not None:
                deps[id(w)] = w
            for r in self.readers.get(b, ()):
                deps[id(r)] = r
        for b in writes:
            self.lastw[b] = op
            self.readers[b] = []
        for b in reads:
            self.readers.setdefault(b, []).append(op)
        op.deps = [d for d in deps.values()
                   if d is not op and not (d.eng == "pe" and eng == "pe" and not d.dma and not dma)]
        for d in op.deps:
            d.need = True
        self.ops.append(op)
        if not nobar:
            self.since_bar.append(op)
        return op

    def barrier(self):
        last = {}
        dmas = []
        for op in self.since_bar:
            if op.dma:
                dmas.append(op)
            elif op.fn is not None:
                last[op.eng] = op
        deps = list(last.values()) + dmas
        for d in deps:
            d.need = True
        for e in self.ALL:
            op = _Op()
            op.cc = False
            op.eng, op.fn, op.dma, op.need, op.ord, op.dsem, op.dval, op.nobar = e, None, False, False, 0, None, 0, False
            op.deps = list(deps)
            self.ops.append(op)
        self.since_bar = []
        self.lastw = {k: v for k, v in self.lastw.items() if k in self.persist}
        self.readers = {k: v for k, v in self.readers.items() if k in self.persist}

    def emit(self, es):
        nc = self.nc
        sems = {e: es.enter_context(nc.semaphore("c_" + e)) for e in self.CE}
        dsems = {q: [es.enter_context(nc.semaphore("d_%s%d" % (q, j))) for j in range(self.NDS)]
                 for q in ("sp", "pool")}
        cnt = {e: 0 for e in self.CE}
        dcnt = {q: 0 for q in dsems}
        duse = {q: [0] * self.NDS for q in dsems}
        prev_on_sem = {}
        ncc = sum(1 for op in self.ops if op.cc)
        ccsems = [es.enter_context(nc.semaphore("cc%d" % j)) for j in range(ncc)]
        icc = 0
        for op in self.ops:
            if op.cc:
                op.dsem = ccsems[icc]
                op.dval = 1
                icc += 1
            elif op.dma:
                q = op.eng
                j = dcnt[q] % self.NDS
                dcnt[q] += 1
                duse[q][j] += 1
                op.dsem = dsems[q][j]
                op.dval = 16 * duse[q][j]
            elif op.fn is not None and op.need:
                cnt[op.eng] += 1
                op.ord = cnt[op.eng]
        block = es.enter_context(nc.Block())
        bname = {"sp": "sync", "pool": "gpsimd", "act": "scalar", "dve": "vector", "pe": "tensor"}
        ninst = [0]
        for e in self.ALL:
            ops_e = [op for op in self.ops if op.eng == e]

            def body(E, ops_e=ops_e, e=e):
                waited = {}
                for op in ops_e:
                    w = {}
                    for d in op.deps:
                        if d.dma:
                            s, v = d.dsem, d.dval
                        else:
                            s, v = sems[d.eng], d.ord
                        k = id(s)
                        if k not in w or w[k][1] < v:
                            w[k] = (s, v)
                    if op.dma and not op.cc and op.dval > 16:
                        k = id(op.dsem)
                        v = op.dval - 16
                        if k not in w or w[k][1] < v:
                            w[k] = (op.dsem, v)
                    for k, (s, v) in w.items():
                        if waited.get(k, 0) < v:
                            E.wait_ge(s, v)
                            waited[k] = v
                            ninst[0] += 1
                    if op.fn is not None:
                        ins = op.fn(E)
                        ninst[0] += 1
                        if op.cc:
                            ins.then_inc(op.dsem, 1)
                        elif op.dma:
                            ins.then_inc(op.dsem, 16)
                        elif op.need:
                            ins.then_inc(sems[e], 1)

            getattr(block, bname[e])(body)
        self.ninst = ninst[0]


def _ap(t, off, dims, parts=128):
    ps = 1
    for s in list(t.shape)[1:]:
        ps *= int(s)
    return bass.AP(t, off, [[ps, parts]] + [list(d) for d in dims])


KOFF, VOFF, KMOFF, TOFF, GWID = 0, NH * TOK, 2 * NH * TOK, 2 * NH * TOK + 64, 2 * NH * TOK + 64 + 2048
RG = [[0, 1, 2, 3], [4, 5, 6, 7]]


def build():
    nc = bass.Bass("TRN2", target_bir_lowering=False)
    es = ExitStack()
    P = Prog(nc)

    def din(name, shape, dt=F32):
        return nc.dram_tensor(name, list(shape), dt, kind="ExternalInput").ap()

    def dout(name, shape, dt=F32):
        return nc.dram_tensor(name, list(shape), dt, kind="ExternalOutput").ap()

    def dscr(name, shape, dt=F32):
        return nc.dram_tensor(name, list(shape), dt, kind="Internal").ap()

    uniq = [0]

    def sb(name, shape, dt=F32, stack=None):
        uniq[0] += 1
        return (stack or es).enter_context(nc.sbuf_tensor("%s_%d" % (name, uniq[0]), list(shape), dt))

    x_d = din("x", [TOK, D])
    win_d = din("w_in", [2, D, NIN])
    ng_d = din("ng", [128, 2, 8])
    cs_d = din("cs", [128, NT, 32])
    qkg_d = din("qkg", [128, 2, 2, 128])
    gmask_d = din("gmask", [128, 8, NB])
    tri_d = din("tri", [128, 2, 256])
    smask_d = din("smask", [128, 4])
    hm_d = din("hm", [128, 4])
    bg_d = din("bgate", [128, 2, 16])
    cw_d = din("convw", [128, 2, 8, CK])
    cv_d = din("cvec", [128, 2, 3, 8])
    wap_d = din("w_ap", [2, D, D])
    wcp_d = din("w_cp", [2, D, D])
    wo_d = din("w_o", [2, D, D])
    y_d = dout("y", [TOK, D])
    gsc = dscr("gsc", [128, 8, TOK], BF16)
    xs1 = dscr("xs1", [TOK, D])
    NCH = 18
    gin = [[dscr("gin%d_%d" % (l, ch), [512, 2048]) for ch in range(NCH)] for l in range(2)]
    gout = [[dscr("gout%d_%d" % (l, ch), [512, 2048]) for ch in range(NCH)] for l in range(2)]
    hts = dscr("hts", [128, 8 * TOK], BF16)
    ktown_d = dscr("ktown", [128, NH * TOK], BF16)
    vown_d = dscr("vownd", [128, NT * D], BF16)

    pb = [es.enter_context(nc.psum_tensor("pb%d" % i, [128, 512], F32)) for i in range(8)]
    pbf = [pb[i][:] for i in range(8)]
    pbh = [pb[i][:].bitcast(BF16) for i in range(8)]

    identf = sb("identf", [128, 128], F32)
    ident = sb("ident", [128, 128], BF16)
    ng2 = sb("ng2", [128, 2, 8])
    cs = sb("cs", [128, NT, 32])
    qkg2 = sb("qkg2", [128, 2, 2, 128])
    gmask = sb("gmask", [128, 8, NB])
    tri = sb("tri", [128, 2, 256])
    smask = sb("smask", [128, 4])
    hm = sb("hm", [128, 4])
    bgate2 = sb("bgate2", [128, 2, 16])
    cwT2 = sb("cwT2", [128, 2, 8, CK])
    cvec2 = sb("cvec2", [128, 2, 3, 8])
    kmb = sb("kmb", [128, NH, NB], BF16)
    ones_s = sb("ones_s", [128, 128], BF16)
    epsc = sb("epsc", [128, 1])
    NST, NBF = 2, 2
    wst = [sb("wst%d" % i, [128, 8, 512], F32) for i in range(NST)]
    wbf = [sb("wbf%d" % i, [128, 8, 512], BF16) for i in range(NBF)]
    bufA = sb("bufA", [128, 8 * 8 * GW], BF16)
    for i in range(NST):
        P.persist.add(("wst", i))
    for i in range(NBF):
        P.persist.add(("wbf", i))
    P.persist.add(("ng",))
    for l_ in range(2):
        for ch_ in range(NCH):
            P.persist.add(("gout", l_, ch_))

    def qT(h, t0, n):
        return _ap(bufA, h * TOK + t0, [[1, n]])

    def gT(c, blk0, nblk, off, n):
        return _ap(bufA, c * 8 * GW + blk0 * GW + off, [[GW, nblk], [1, n]])

    P.add("pool", lambda E: E.memset(identf[:], 0.0), writes=[("identf",)])
    P.add("pool", lambda E: E.affine_select(out=identf[:], in_=identf[:], compare_op=ALU.not_equal, fill=1.0,
                                            base=0, pattern=[[-1, 128]], channel_multiplier=1),
          reads=[("identf",)], writes=[("identf",)])
    P.add("pool", lambda E: E.tensor_copy(out=ident[:], in_=identf[:]), reads=[("identf",)], writes=[("ident",)])
    P.add("pool", lambda E: E.memset(epsc[:], float(EPS)), writes=[("epsc",)])
    P.add("pool", lambda E: E.memset(ones_s[:], 1.0 / D), writes=[("ones_s",)])
    for (t_, d_, k_) in ((ng2, ng_d, "ng"), (cs, cs_d, "cs"), (qkg2, qkg_d, "qkg"), (gmask, gmask_d, "gmask"),
                         (tri, tri_d, "tri"), (smask, smask_d, "smask"), (hm, hm_d, "hm"), (bgate2, bg_d, "bgate"),
                         (cwT2, cw_d, "cwT"), (cvec2, cv_d, "cvec")):
        P.add("sp", lambda E, t_=t_, d_=d_: E.dma_start(out=t_[:], in_=d_), writes=[(k_,)], dma=True)
    P.barrier()

    wlist = []
    for l_ in range(2):
        wlist += [(l_, "in", C_UA), (l_, "in", C_UB), (l_, "in", C_UA + 512), (l_, "in", C_UB + 512),
                  (l_, "in", C_K), (l_, "in", C_K + 512), (l_, "in", C_V), (l_, "in", C_V + 512),
                  (l_, "in", C_UA), (l_, "in", C_UB), (l_, "in", C_UA + 512), (l_, "in", C_UB + 512),
                  (l_, "in", C_ZC), (l_, "in", C_ZC + 512), (l_, "in", C_GC), (l_, "in", C_GC + 512),
                  (l_, "cp", 0), (l_, "cp", 512), (l_, "in", C_Q), (l_, "in", C_Q + 512),
                  (l_, "in", C_ZA), (l_, "in", C_ZA + 512), (l_, "in", C_GA), (l_, "in", C_GA + 512),
                  (l_, "ap", 0), (l_, "ap", 512), (l_, "o", 0), (l_, "o", 512)]
    wstate = {"emitted": 0, "next": 0}

    def w_load(j):
        lw, kind, c0 = wlist[j]
        base = {"in": win_d, "cp": wcp_d, "ap": wap_d, "o": wo_d}[kind]
        s_ap = base[lw].rearrange("(k p) c -> p k c", p=128)[:, :, c0:c0 + 512]
        st = j % NST
        P.add("sp", lambda E: E.dma_start(out=wst[st][:], in_=s_ap), writes=[("wst", st)], dma=True, nobar=True)

    def w_cast(j):
        lw, kind, c0 = wlist[j]
        st, bf = j % NST, j % NBF
        if kind == "in":
            ngb = _ap(ng2, lw * 8, [[1, 8], [0, 512]])
            P.add("dve", lambda E: E.tensor_tensor(out=wbf[bf][:], in0=wst[st][:], in1=ngb, op=ALU.mult),
                  reads=[("wst", st), ("ng",)], writes=[("wbf", bf)], nobar=True)
        else:
            P.add("act", lambda E: E.activation(out=wbf[bf][:], in_=wst[st][:], func=AF.Copy),
                  reads=[("wst", st)], writes=[("wbf", bf)], nobar=True)

    def w_next(hold=False):
        j = wstate["next"]
        wstate["next"] += 1
        while wstate["emitted"] <= min(len(wlist) - 1, j):
            w_load(wstate["emitted"])
            wstate["emitted"] += 1
        w_cast(j)
        while wstate["emitted"] <= min(len(wlist) - 1, j + 1):
            w_load(wstate["emitted"])
            wstate["emitted"] += 1
        return wbf[j % NBF], ("wbf", j % NBF)


    def layer(l):
        x_src = x_d if l == 0 else xs1
        y_dst = xs1 if l == 0 else y_d
        gin_l, gout_l = gin[l], gout[l]
        def stage_hT(hT, st):
            xt = [sb("xt%d" % i, [128, D], F32, st) for i in range(2)]
            sqj = sb("sqj", [128, D], BF16, st)
            xn = [sb("xn%d" % i, [128, D], BF16, st) for i in range(2)]
            ss = sb("ss", [128, NT], F32, st)
            rs_ = sb("rs_", [128, NT], F32, st)
            for tt in range(NT):
                b = tt % 2
                P.add("sp", lambda E, tt=tt, b=b: E.dma_start(out=xt[b][:], in_=x_src[tt * 128:(tt + 1) * 128, :]),
                      writes=[("xt", b)], dma=True)
                P.add("act", lambda E, tt=tt, b=b: E.activation(out=sqj[:], in_=xt[b][:], func=AF.Square,
                                                                accum_out=ss[:, tt:tt + 1]),
                      reads=[("xt", b)], writes=[("sqj",), ("ss", tt)])
                P.add("act", lambda E, tt=tt: E.activation(out=rs_[:, tt:tt + 1], in_=ss[:, tt:tt + 1], func=AF.Sqrt,
                                                           bias=epsc[:, 0:1], scale=1.0 / D),
                      reads=[("ss", tt), ("epsc",)], writes=[("rs_", tt)])
                P.add("dve", lambda E, tt=tt: E.reciprocal(out=rs_[:, tt:tt + 1], in_=rs_[:, tt:tt + 1]),
                      reads=[("rs_", tt)], writes=[("rs_", tt)])
                P.add("dve", lambda E, tt=tt, b=b: E.tensor_scalar(out=xn[b][:], in0=xt[b][:], scalar1=rs_[:, tt:tt + 1],
                                                                   scalar2=None, op0=ALU.mult),
                      reads=[("xt", b), ("rs_", tt)], writes=[("xn", b)])
                bank = 4 + b
                for c in range(8):
                    P.add("pe", lambda E, c=c, b=b, bank=bank: E.transpose(out=pbh[bank][:, c * 128:(c + 1) * 128],
                                                                           in_=xn[b][:, c * 128:(c + 1) * 128],
                                                                           identity=ident[:]),
                          reads=[("xn", b), ("ident",)], writes=[("pb", bank)])
                P.add("act", lambda E, tt=tt, bank=bank: E.activation(
                    out=_ap(hT, tt * 128, [[TOK, 8], [1, 128]]),
                    in_=pbh[bank].rearrange("p (c t) -> p c t", c=8), func=AF.Copy),
                    reads=[("pb", bank)], writes=[("hT", tt)])

        def stage_qk(hT, st, which, dst_fn, dst_key, after_group=None):
            sqk = sb("sqk", [128, 512], F32, st)
            ssk = sb("ssk", [128, 4], F32, st)
            rk = sb("rk", [128, 4], F32, st)
            kn = sb("kn", [128, 512], F32, st)
            kb = [sb("kb%d" % i, [128, 512], BF16, st) for i in range(2)]
            rt = [sb("rt%d" % i, [128, 4, 16], F32, st) for i in range(4)]
            gvec = _ap(qkg2, (l * 2 + which) * 128, [[0, 4], [1, 128]])
            it = 0
            for kg in range(2):
                wt, wkey = w_next()
                for tt in range(NT):
                    bank = 1 + it % 3
                    for kc in range(8):
                        P.add("pe", lambda E, kc=kc, tt=tt, bank=bank, wt=wt: E.matmul(
                            pbf[bank], lhsT=hT[:, kc, tt * 128:(tt + 1) * 128], rhs=wt[:, kc, :],
                            start=(kc == 0), stop=(kc == 7)),
                            reads=[("hT", tt), wkey], writes=[("pb", bank)])
                    ps3 = pbf[bank].rearrange("p (h d) -> p h d", h=4)
                    P.add("act", lambda E, bank=bank: E.activation(out=sqk[:], in_=pbf[bank], func=AF.Square),
                          reads=[("pb", bank)], writes=[("sqk",)])
                    P.add("dve", lambda E: E.tensor_reduce(out=ssk[:], in_=sqk[:].rearrange("p (h d) -> p h d", h=4),
                                                           axis=AX.X, op=ALU.add),
                          reads=[("sqk",)], writes=[("ssk",)])
                    P.add("act", lambda E: E.activation(out=rk[:], in_=ssk[:], func=AF.Sqrt, bias=epsc[:, 0:1], scale=1.0 / HD),
                          reads=[("ssk",), ("epsc",)], writes=[("rk",)])
                    P.add("dve", lambda E: E.reciprocal(out=rk[:], in_=rk[:]), reads=[("rk",)], writes=[("rk",)])
                    kn3 = kn[:].rearrange("p (h d) -> p h d", h=4)
                    P.add("dve", lambda E, ps3=ps3, kn3=kn3: E.tensor_tensor(
                        out=kn3, in0=ps3, in1=_ap(rk, 0, [[1, 4], [0, 128]]), op=ALU.mult),
                        reads=[("pb", bank), ("rk",)], writes=[("kn",)])
                    P.add("dve", lambda E, kn3=kn3: E.tensor_tensor(out=kn3, in0=kn3, in1=gvec, op=ALU.mult),
                          reads=[("kn",), ("qkg",)], writes=[("kn",)])
                    kbb = kb[it % 2]
                    kbk = ("kb", it % 2)
                    P.add("act", lambda E, kbb=kbb: E.activation(out=kbb[:], in_=kn[:], func=AF.Copy),
                          reads=[("kn",)], writes=[kbk])
                    t1 = _ap(kn, 0, [[128, 4], [1, 16]])
                    t2 = _ap(kn, 16, [[128, 4], [1, 16]])
                    cosb = _ap(cs, tt * 32, [[0, 4], [1, 16]])
                    sinb = _ap(cs, tt * 32 + 16, [[0, 4], [1, 16]])
                    o1 = _ap(kbb, 0, [[128, 4], [1, 16]])
                    o2 = _ap(kbb, 16, [[128, 4], [1, 16]])
                    P.add("dve", lambda E, t1=t1, cosb=cosb: E.tensor_tensor(out=rt[0][:], in0=t1, in1=cosb, op=ALU.mult),
                          reads=[("kn",), ("cs",)], writes=[("rt", 0)])
                    P.add("dve", lambda E, t2=t2, sinb=sinb: E.tensor_tensor(out=rt[1][:], in0=t2, in1=sinb, op=ALU.mult),
                          reads=[("kn",), ("cs",)], writes=[("rt", 1)])
                    P.add("dve", lambda E, t2=t2, cosb=cosb: E.tensor_tensor(out=rt[2][:], in0=t2, in1=cosb, op=ALU.mult),
                          reads=[("kn",), ("cs",)], writes=[("rt", 2)])
                    P.add("dve", lambda E, t1=t1, sinb=sinb: E.tensor_tensor(out=rt[3][:], in0=t1, in1=sinb, op=ALU.mult),
                          reads=[("kn",), ("cs",)], writes=[("rt", 3)])
                    P.add("dve", lambda E, o1=o1: E.tensor_tensor(out=o1, in0=rt[0][:], in1=rt[1][:], op=ALU.subtract),
                          reads=[("rt", 0), ("rt", 1), kbk], writes=[kbk])
                    P.add("dve", lambda E, o2=o2: E.tensor_tensor(out=o2, in0=rt[2][:], in1=rt[3][:], op=ALU.add),
                          reads=[("rt", 2), ("rt", 3), kbk], writes=[kbk])
                    tb = 5 + it % 2
                    for h in range(4):
                        P.add("pe", lambda E, h=h, tb=tb, kbb=kbb: E.transpose(
                            out=pbh[tb][:, h * 128:(h + 1) * 128], in_=kbb[:, h * 128:(h + 1) * 128], identity=ident[:]),
                            reads=[kbk, ("ident",)], writes=[("pb", tb)])
                    P.add("act", lambda E, tb=tb, kg=kg, tt=tt: E.activation(
                        out=dst_fn(kg * 4, tt), in_=pbh[tb][:, 0:512].rearrange("p (h t) -> p h t", h=4), func=AF.Copy),
                        reads=[("pb", tb)], writes=[(dst_key, kg * 4 + h, tt) for h in range(4)])
                    it += 1
                if after_group is not None:
                    after_group(kg)


        hT_stack = ExitStack()
        hT = sb("hT", [128, 8, TOK], BF16, hT_stack)
        st1 = ExitStack()
        stage_hT(hT, st1)
        P.barrier()
        st1.close()
        P.add("sp", lambda E, hT=hT: E.dma_start(out=hts, in_=hT[:].rearrange("p k t -> p (k t)")),
              reads=[("hT", tt) for tt in range(NT)], writes=[("hts",)], dma=True)
        st2 = ExitStack()
        vown = sb("vown", [128, NT, D], BF16, st2)
        kms = sb("kms", [128, 64], F32, st2)
        sgA = sb("sgA", [128, 256], F32, st2)
        gtl = sb("gtl", [128, 8, 256], F32, st2)
        stg = [sb("stg%d" % i, [128, 2048], BF16, st2) for i in range(3)]
        stf = [sb("stf%d" % i, [128, 2048], F32, st2) for i in range(2)]
        skm = sb("skm", [128, 4, 64], F32, st2)
        allq = [("q", h, tt) for h in range(NH) for tt in range(NT)]
        allv = [("vown", tt, vg) for tt in range(NT) for vg in range(2)]
        cidx = [0]

        def gather_chunk(ch):
            gks = []
            for j in range(4):
                rows = slice(j * 128, (j + 1) * 128)
                mj = smask[:, j:j + 1]
                gk = ("gin", l, j, ch)
                if ch < 16:
                    ci = cidx[0]
                    cidx[0] += 1
                    s_ = stg[ci % 3]
                    if ch < 8:
                        src_ap, rk_ = qT(ch, 0, TOK), [("q", ch, tt) for tt in range(NT)]
                        o_ap = s_[:]
                    else:
                        hv = ch - 8
                        src_ap, rk_ = vown[:, :, hv * 128:(hv + 1) * 128], [("vown", tt, hv // 4) for tt in range(NT)]
                        o_ap = s_[:].rearrange("p (t d) -> p t d", d=128)
                    sk = ("stg", ci % 3)
                    if ci % 2 == 0:
                        P.add("act", lambda E, o_ap=o_ap, src_ap=src_ap, mj=mj: E.activation(out=o_ap, in_=src_ap, func=AF.Copy,
                                                                                             scale=mj),
                              reads=rk_ + [("smask",)], writes=[sk])
                    else:
                        P.add("dve", lambda E, o_ap=o_ap, src_ap=src_ap, mj=mj: E.tensor_scalar(out=o_ap, in0=src_ap, scalar1=mj,
                                                                                                scalar2=None, op0=ALU.mult),
                              reads=rk_ + [("smask",)], writes=[sk])
                    P.add("pool", lambda E, s_=s_, rows=rows: E.dma_start(out=gin_l[ch][rows, :], in_=s_[:]),
                          reads=[sk], writes=[gk], dma=True)
                elif ch == 16:
                    P.add("dve", lambda E, j=j, mj=mj: E.tensor_scalar(out=skm[:, j, :], in0=kms[:], scalar1=mj, scalar2=None,
                                                                       op0=ALU.mult),
                          reads=[("kms",), ("smask",)], writes=[("skm", j)])
                    P.add("pool", lambda E, j=j, rows=rows: E.dma_start(out=gin_l[ch][rows, 0:64], in_=skm[:, j, :]),
                          reads=[("skm", j)], writes=[gk], dma=True)
                else:
                    f_ = stf[j % 2]
                    P.add("dve", lambda E, f_=f_, mj=mj: E.tensor_scalar(out=f_[:], in0=gtl[:].rearrange("p c t -> p (c t)"),
                                                                         scalar1=mj, scalar2=None, op0=ALU.mult),
                          reads=[("gtl", c) for c in range(8)] + [("smask",)], writes=[("stf", j % 2)])
                    P.add("pool", lambda E, f_=f_, rows=rows: E.dma_start(out=gin_l[ch][rows, :], in_=f_[:]),
                          reads=[("stf", j % 2)], writes=[gk], dma=True)
                gks.append(gk)
            P.add("pool", lambda E: E.collective_compute("AllReduce", ALU.add, replica_groups=RG, ins=[gin_l[ch]],
                                                         outs=[gout_l[ch]]),
                  reads=gks, writes=[("gout", l, ch)], dma=True, nobar=True, cc=True)

        for g in range(2):
            wa, wakey = w_next()
            wb, wbkey = w_next(hold=True)
            for cc in range(4):
                c = 4 * g + cc
                for (bank, wt, wkey) in ((1, wa, wakey), (2, wb, wbkey)):
                    for kc in range(8):
                        rhsT = _ap(hT, kc * TOK + 224, [[BLK, 8], [1, 32]])
                        P.add("pe", lambda E, kc=kc, cc=cc, bank=bank, wt=wt, rhsT=rhsT: E.matmul(
                            pbf[bank][:, 0:256], lhsT=wt[:, kc, cc * 128:(cc + 1) * 128],
                            rhs=rhsT,
                            start=(kc == 0), stop=(kc == 7)),
                            reads=[("hT", tt) for tt in range(NT)] + [wkey], writes=[("pb", bank)])
                P.add("act", lambda E: E.activation(out=sgA[:], in_=pbf[2][:, 0:256], func=AF.Sigmoid),
                      reads=[("pb", 2)], writes=[("sg",)])
                P.add("dve", lambda E, c=c: E.tensor_tensor(out=gtl[:, c, :], in0=pbf[1][:, 0:256], in1=sgA[:], op=ALU.mult),
                      reads=[("pb", 1), ("sg",)], writes=[("gtl", c)])
        gather_chunk(17)

        def after_k(kg):
            for h_ in range(4 * kg, 4 * kg + 4):
                gather_chunk(h_)

        stage_qk(hT, st2, 1, lambda h0, tt: _ap(bufA, h0 * TOK + tt * 128, [[TOK, 4], [1, 128]]), "q", after_group=after_k)
        P.add("dve", lambda E: E.tensor_reduce(out=kms[:], in_=_ap(bufA, 0, [[BLK, 64], [1, BLK]]), axis=AX.X, op=ALU.add),
              reads=allq, writes=[("kms",)])
        P.add("dve", lambda E: E.tensor_scalar(out=kms[:], in0=kms[:], scalar1=1.0 / BLK, scalar2=None, op0=ALU.mult),
              reads=[("kms",)], writes=[("kms",)])
        gather_chunk(16)
        it = 0
        for vg in range(2):
            wt, wkey = w_next()
            for tt in range(NT):
                bank = 1 + it % 3
                for kc in range(8):
                    P.add("pe", lambda E, kc=kc, tt=tt, bank=bank, wt=wt, hT=hT: E.matmul(
                        pbf[bank], lhsT=hT[:, kc, tt * 128:(tt + 1) * 128], rhs=wt[:, kc, :],
                        start=(kc == 0), stop=(kc == 7)),
                        reads=[("hT", tt), wkey], writes=[("pb", bank)])
                P.add("act", lambda E, bank=bank, tt=tt, vg=vg: E.activation(
                    out=vown[:, tt, vg * 512:(vg + 1) * 512], in_=pbf[bank], func=AF.Copy),
                    reads=[("pb", bank)], writes=[("vown", tt, vg)])
                it += 1
            for h_ in range(4 * vg, 4 * vg + 4):
                gather_chunk(8 + h_)
        P.add("sp", lambda E: E.dma_start(out=ktown_d, in_=_ap(bufA, 0, [[1, NH * TOK]])), reads=allq, writes=[("ktown",)], dma=True)
        P.add("sp", lambda E: E.dma_start(out=vown_d, in_=vown[:].rearrange("p t d -> p (t d)")), reads=allv, writes=[("vownd",)],
              dma=True)
        P.barrier()
        st2.close()
        def proj_fm(wt, wkey, cc, T, bank, rhs_fn=None, rkeys=None):
            for kc in range(8):
                rhs = hT[:, kc, T * 512:(T + 1) * 512] if rhs_fn is None else rhs_fn(kc)
                rk_ = [("hT", 4 * T + j) for j in range(4)] if rkeys is None else rkeys(kc)
                o_ = pbf[bank] if len(rhs.shape) == 2 else pbf[bank].rearrange("p (b t) -> p b t", b=rhs.shape[1])
                P.add("pe", lambda E, kc=kc, rhs=rhs, o_=o_: E.matmul(o_, lhsT=wt[:, kc, cc * 128:(cc + 1) * 128], rhs=rhs,
                                                                      start=(kc == 0), stop=(kc == 7)),
                      reads=rk_ + [wkey], writes=[("pb", bank)])

        st3 = ExitStack()
        sg = [sb("sg%d" % i, [128, 512], F32, st3) for i in range(2)]
        it = 0
        for g in range(2):
            wa, wakey = w_next()
            wb_, wbkey = w_next(hold=True)
            for cc in range(4):
                c = 4 * g + cc
                for T in range(4):
                    ba, bb = 1 + (it % 2) * 2, 2 + (it % 2) * 2
                    proj_fm(wa, wakey, cc, T, ba)
                    proj_fm(wb_, wbkey, cc, T, bb)
                    s_ = sg[it % 2]
                    P.add("act", lambda E, s_=s_, bb=bb: E.activation(out=s_[:], in_=pbf[bb], func=AF.Sigmoid),
                          reads=[("pb", bb)], writes=[("sg", it % 2)])
                    P.add("dve", lambda E, s_=s_, ba=ba, c=c, T=T: E.tensor_tensor(
                        out=gT(c, 2 * T, 2, 32, 256), in0=pbf[ba].rearrange("p (b t) -> p b t", b=2),
                        in1=s_[:].rearrange("p (b t) -> p b t", b=2), op=ALU.mult),
                        reads=[("pb", ba), ("sg", it % 2)], writes=[("g", c, T)])
                    it += 1
        P.barrier()
        st3.close()

        sth = ExitStack()
        hl4 = sb("hl4", [128, 4, 2048], F32, sth)
        hacc = sb("hacc", [128, 2048], F32, sth)
        kmraw = sb("kmraw", [128, 4, 64], F32, sth)
        P.add("dve", lambda E: E.memset(hl4[:, 3, :], 0.0), writes=[("hl4", 3)])
        for j in range(3):
            P.add("sp", lambda E, j=j: E.dma_start(out=hl4[:, j, :], in_=gout_l[17][j * 128:(j + 1) * 128, :]),
                  reads=[("gout", l, 17)], writes=[("hl4", j)], dma=True)
        P.add("sp", lambda E: E.dma_start(
            out=_ap(hl4, 3 * 2048 + 32, [[256, 8], [1, 224]]),
            in_=gout_l[17][384:512, :].rearrange("p (c t) -> p c t", c=8)[:, :, 0:224]),
            reads=[("gout", l, 17), ("hl4", 3)], writes=[("hl4", 3)], dma=True)
        P.add("dve", lambda E: E.tensor_scalar(out=hacc[:], in0=hl4[:, 0, :], scalar1=hm[:, 0:1], scalar2=None, op0=ALU.mult),
              reads=[("hl4", 0), ("hm",)], writes=[("hacc",)])
        for j in range(1, 4):
            P.add("dve", lambda E, j=j: E.scalar_tensor_tensor(out=hacc[:], in0=hl4[:, j, :], scalar=hm[:, j:j + 1], in1=hacc[:],
                                                               op0=ALU.mult, op1=ALU.add),
                  reads=[("hl4", j), ("hm",), ("hacc",)], writes=[("hacc",)])
        P.add("dve", lambda E: E.tensor_copy(
            out=_ap(bufA, 0, [[GW, 64], [1, 32]]), in_=hacc[:].rearrange("p (ci t) -> p ci t", t=32)),
            reads=[("hacc",)], writes=[("gh",)])
        for rr in range(4):
            P.add("sp", lambda E, rr=rr: E.dma_start(out=kmraw[:, rr, :], in_=gout_l[16][rr * 128:(rr + 1) * 128, 0:64]),
                  reads=[("gout", l, 16)], writes=[("kmraw", rr)], dma=True)
        P.add("dve", lambda E: E.tensor_copy(out=_ap(kmb, 0, [[1, 4], [NB, 8], [4, 8]]),
                                             in_=kmraw[:].rearrange("p r (h i) -> p r h i", h=8)),
              reads=[("kmraw", rr) for rr in range(4)], writes=[("kmb",)])
        P.barrier()
        sth.close()

        st4 = ExitStack()
        gcy4_stack = ExitStack()
        gcy4 = sb("gcyc", [128, 8, TOK], BF16, gcy4_stack)
        cpre = sb("cpre", [128, 8, 512], F32, st4)
        cb = [sb("cb%d" % i, [128, 512], BF16, st4) for i in range(2)]
        csq = [sb("csq%d" % i, [128, 512], BF16, st4) for i in range(2)]
        mean = sb("mean", [128, 512], F32, st4)
        msq = sb("msq", [128, 512], F32, st4)
        rstd = sb("rstd", [128, 512], F32, st4)
        dw = sb("dw", [128, CK, 128], BF16, st4)
        sz = [sb("sz%d" % i, [128, 512], F32, st4) for i in range(2)]
        it = 0
        for T in range(4):
            for c in range(8):
                P.add("dve", lambda E, c=c: E.tensor_tensor(
                    out=dw[:], in0=_ap(identf, 0, [[0, CK], [1, 128]]), in1=_ap(cwT2, (l * 8 + c) * CK, [[1, CK], [0, 128]]),
                    op=ALU.mult), reads=[("identf",), ("cwT",)], writes=[("dw",)])
                bank = 1 + it % 2
                for j in range(CK):
                    P.add("pe", lambda E, j=j, c=c, T=T, bank=bank: E.matmul(
                        pbf[bank].rearrange("p (b t) -> p b t", b=2), lhsT=dw[:, j, :], rhs=gT(c, 2 * T, 2, 2 + j, 256),
                        start=(j == 0), stop=(j == CK - 1)),
                        reads=[("dw",), ("g", c, T), ("gh",)], writes=[("pb", bank)])
                P.add("act", lambda E, c=c, bank=bank: E.activation(out=cpre[:, c, :], in_=pbf[bank], func=AF.Identity,
                                                                    bias=cvec2[:, l, 0, c:c + 1]),
                      reads=[("pb", bank), ("cvec",)], writes=[("cpre", c)])
                P.add("act", lambda E, c=c, bank=bank, k=it % 2: E.activation(out=csq[k][:], in_=pbf[bank], func=AF.Square,
                                                                    bias=cvec2[:, l, 0, c:c + 1]),
                      reads=[("pb", bank), ("cvec",)], writes=[("csq", it % 2)])
                P.add("dve", lambda E, c=c, k=it % 2: E.tensor_copy(out=cb[k][:], in_=cpre[:, c, :]),
                      reads=[("cpre", c)], writes=[("cb", it % 2)])
                P.add("pe", lambda E, c=c, k=it % 2: E.matmul(pbf[3], lhsT=ones_s[:], rhs=cb[k][:], start=(c == 0), stop=(c == 7)),
                      reads=[("ones_s",), ("cb", it % 2)], writes=[("pb", 3)])
                P.add("pe", lambda E, c=c, k=it % 2: E.matmul(pbf[4], lhsT=ones_s[:], rhs=csq[k][:], start=(c == 0), stop=(c == 7)),
                      reads=[("ones_s",), ("csq", it % 2)], writes=[("pb", 4)])
                it += 1
            P.add("dve", lambda E: E.tensor_copy(out=mean[:], in_=pbf[3]), reads=[("pb", 3)], writes=[("mean",)])
            P.add("dve", lambda E: E.tensor_tensor(out=msq[:], in0=mean[:], in1=mean[:], op=ALU.mult),
                  reads=[("mean",)], writes=[("msq",)])
            P.add("dve", lambda E: E.tensor_tensor(out=rstd[:], in0=pbf[4], in1=msq[:], op=ALU.subtract),
                  reads=[("pb", 4), ("msq",)], writes=[("rstd",)])
            P.add("act", lambda E: E.activation(out=rstd[:], in_=rstd[:], func=AF.Sqrt, bias=epsc[:, 0:1]),
                  reads=[("rstd",), ("epsc",)], writes=[("rstd",)])
            P.add("dve", lambda E: E.reciprocal(out=rstd[:], in_=rstd[:]), reads=[("rstd",)], writes=[("rstd",)])
            for c in range(8):
                P.add("dve", lambda E, c=c: E.tensor_tensor(out=cpre[:, c, :], in0=cpre[:, c, :], in1=mean[:], op=ALU.subtract),
                      reads=[("cpre", c), ("mean",)], writes=[("cpre", c)])
                P.add("dve", lambda E, c=c: E.tensor_tensor(out=cpre[:, c, :], in0=cpre[:, c, :], in1=rstd[:], op=ALU.mult),
                      reads=[("cpre", c), ("rstd",)], writes=[("cpre", c)])
                P.add("act", lambda E, c=c, T=T: E.activation(
                    out=gT(c, 2 * T, 2, 32, 256), in_=cpre[:, c, :].rearrange("p (b t) -> p b t", b=2), func=AF.Silu,
                    bias=cvec2[:, l, 2, c:c + 1], scale=cvec2[:, l, 1, c:c + 1]),
                    reads=[("cpre", c), ("cvec",)], writes=[("g", c, T)])
        it = 0
        for g in range(2):
            wt, wkey = w_next()
            for cc in range(4):
                c = 4 * g + cc
                for T in range(4):
                    bank = 5 + it % 2
                    proj_fm(wt, wkey, cc, T, bank)
                    s_ = sz[it % 2]
                    P.add("act", lambda E, s_=s_, bank=bank: E.activation(out=s_[:], in_=pbf[bank], func=AF.Silu),
                          reads=[("pb", bank)], writes=[("sz", it % 2)])
                    P.add("dve", lambda E, s_=s_, c=c, T=T: E.tensor_tensor(
                        out=gT(c, 2 * T, 2, 32, 256), in0=gT(c, 2 * T, 2, 32, 256),
                        in1=s_[:].rearrange("p (b t) -> p b t", b=2), op=ALU.mult),
                        reads=[("g", c, T), ("sz", it % 2)], writes=[("g", c, T)])
                    it += 1
        for g in range(2):
            wt, wkey = w_next()
            for cc in range(4):
                c = 4 * g + cc
                for T in range(4):
                    bank = 5 + it % 2
                    proj_fm(wt, wkey, cc, T, bank)
                    P.add("act", lambda E, bank=bank, c=c, T=T: E.activation(
                        out=gcy4[:, c, T * 512:(T + 1) * 512], in_=pbf[bank], func=AF.Sigmoid, bias=bgate2[:, l, 8 + c:9 + c]),
                        reads=[("pb", bank), ("bgate",)], writes=[("gc", c, T)])
                    it += 1
        for g in range(2):
            wt, wkey = w_next()
            for cc in range(4):
                c = 4 * g + cc
                for T in range(4):
                    bank = 5 + it % 2
                    proj_fm(wt, wkey, cc, T, bank, rhs_fn=lambda kc, T=T: gT(kc, 2 * T, 2, 32, 256),
                            rkeys=lambda kc, T=T: [("g", kc, T)])
                    P.add("dve", lambda E, bank=bank, c=c, T=T: E.tensor_tensor(
                        out=gcy4[:, c, T * 512:(T + 1) * 512], in0=pbf[bank], in1=gcy4[:, c, T * 512:(T + 1) * 512],
                        op=ALU.mult), reads=[("pb", bank), ("gc", c, T)], writes=[("gc", c, T)])
                    it += 1
        P.add("pool", lambda E: E.dma_start(out=gsc, in_=gcy4[:]),
              reads=[("gc", c, T) for c in range(8) for T in range(4)], writes=[("gsc",)], dma=True)
        P.barrier()
        st4.close()
        gcy4_stack.close()

        st5 = ExitStack()
        stage_qk(hT, st5, 0, lambda h0, tt: _ap(bufA, h0 * TOK + tt * 128, [[TOK, 4], [1, 128]]), "q")
        P.barrier()
        st5.close()
        hT_stack.close()

        st6 = ExitStack()
        kth = [sb("kth%d" % i, [128, SEQ], BF16, st6) for i in range(2)]
        vh = [sb("vh%d" % i, [128, 64, 128], BF16, st6) for i in range(2)]
        kto = [sb("kto%d" % i, [128, TOK], BF16, st6) for i in range(2)]
        vo = [sb("vo%d" % i, [128, NT, 128], BF16, st6) for i in range(2)]
        gm = sb("gm", [128, NB], F32, st6)
        m8 = sb("m8", [128, 8], F32, st6)
        thr = sb("thr", [128, 1], F32, st6)
        biasb = [sb("biasb%d" % i, [128, NB], F32, st6) for i in range(2)]
        rsb = [sb("rsb%d" % i, [128, 40], F32, st6) for i in range(2)]
        Pb = [sb("Pb%d" % i, [128, 512], BF16, st6) for i in range(3)]
        PTb = [sb("PTb%d" % i, [128, 512], BF16, st6) for i in range(3)]
        scm = [sb("scm%d" % i, [128, 256], F32, st6) for i in range(2)]
        rsum = sb("rsum", [128, 1], F32, st6)
        rinv = sb("rinv", [128, 1], F32, st6)
        obf = [sb("obf%d" % i, [128, 128], BF16, st6) for i in range(2)]

        def load_head(h):
            hb = h % 2
            for rr in range(4):
                P.add("pool", lambda E, rr=rr: E.dma_start(
                    out=_ap(kth[hb], rr * BLK, [[4 * BLK, 8], [1, BLK]]),
                    in_=gout_l[h][rr * 128:(rr + 1) * 128, :].rearrange("p (i t) -> p i t", t=BLK)),
                    reads=[("gout", l, h)], writes=[("kth", hb)], dma=True)
            for rr in range(4):
                P.add("pool", lambda E, rr=rr: E.dma_start(
                    out=_ap(vh[hb], rr * 256, [[1024, 8], [1, 256]]),
                    in_=gout_l[8 + h][rr * 128:(rr + 1) * 128, :].rearrange("p (i t) -> p i t", t=256)),
                    reads=[("gout", l, 8 + h)], writes=[("vh", hb)], dma=True)
            P.add("sp", lambda E: E.dma_start(out=kto[hb][:], in_=ktown_d[:, h * TOK:(h + 1) * TOK]), writes=[("kto", hb)], dma=True)
            vsrc = vown_d.rearrange("p (t d) -> p t d", d=D)
            for a in range(2):
                P.add("sp", lambda E, a=a: E.dma_start(out=vo[hb][:, 8 * a:8 * a + 8, :],
                                                       in_=vsrc[:, 8 * a:8 * a + 8, h * 128:(h + 1) * 128]),
                      writes=[("vo", hb)], dma=True)

        load_head(0)
        sidx = [0]
        for h in range(NH):
            hb = h % 2
            if h + 1 < NH:
                load_head(h + 1)
            items = []
            for qt in range(NT):
                i, half = qt // 2, qt % 2
                nb = 4 * i + 3
                col = 0
                ng_ = (nb + 1) // 2
                for g in range(ng_):
                    nblk = min(2, nb - 2 * g)
                    items.append(dict(qt=qt, kind="g", g=g, nblk=nblk, ncols=256 * nblk, first=(g == 0), last=False, col=col))
                    col += nblk
                items.append(dict(qt=qt, kind="d", ncols=128 * (half + 1), first=False, last=True, col=col))
            for it_ in items:
                it_["s"] = sidx[0]
                sidx[0] += 1

            def do_qk(itm):
                qt, s = itm["qt"], itm["s"]
                i, half = qt // 2, qt % 2
                qslice = qT(h, qt * 128, 128)
                kmh = kmb[:, h, :]
                if itm["first"]:
                    P.add("pe", lambda E: E.matmul(pbf[0][:, 0:NB], lhsT=qslice, rhs=kmh, start=True, stop=True),
                          reads=[("q", h, qt), ("kmb",)], writes=[("pb", 0)])
                    P.add("dve", lambda E: E.tensor_tensor(out=gm[:], in0=pbf[0][:, 0:NB], in1=gmask[:, i, :], op=ALU.add),
                          reads=[("pb", 0), ("gmask",)], writes=[("gm",)])
                    P.add("dve", lambda E: E.max(out=m8[:], in_=gm[:]), reads=[("gm",)], writes=[("m8",)])
                    P.add("dve", lambda E: E.tensor_scalar(out=thr[:], in0=m8[:, 2:3], scalar1=-1e30, scalar2=None, op0=ALU.max),
                          reads=[("m8",)], writes=[("thr",)])
                    P.add("dve", lambda E: E.tensor_scalar(out=biasb[qt % 2][:], in0=gm[:], scalar1=thr[:, 0:1], scalar2=NEG,
                                                           op0=ALU.is_lt, op1=ALU.mult),
                          reads=[("gm",), ("thr",)], writes=[("biasb", qt % 2)])
                bank = 1 + s % 3
                nc_ = itm["ncols"]
                if itm["kind"] == "g":
                    rhs = kth[hb][:, 512 * itm["g"]:512 * itm["g"] + nc_]
                    rk_ = ("kth", hb)
                else:
                    rhs = kto[hb][:, i * BLK:i * BLK + nc_]
                    rk_ = ("kto", hb)
                P.add("pe", lambda E: E.matmul(pbf[bank][:, 0:nc_], lhsT=qslice, rhs=rhs, start=True, stop=True),
                      reads=[("q", h, qt), rk_], writes=[("pb", bank)])

            def do_exp(itm):
                qt, s = itm["qt"], itm["s"]
                half = qt % 2
                bank = 1 + s % 3
                pt_, pk = Pb[s % 3], ("Pb", s % 3)
                nc_ = itm["ncols"]
                if itm["kind"] == "g":
                    for b_ in range(itm["nblk"]):
                        n = 2 * itm["g"] + b_
                        col = itm["col"] + b_
                        P.add("act", lambda E, b_=b_, n=n, col=col: E.activation(
                            out=pt_[:, b_ * 256:(b_ + 1) * 256], in_=pbf[bank][:, b_ * 256:(b_ + 1) * 256], func=AF.Exp,
                            bias=biasb[qt % 2][:, n:n + 1], scale=SCALE, accum_out=rsb[qt % 2][:, col:col + 1]),
                            reads=[("pb", bank), ("biasb", qt % 2)], writes=[pk, ("rsb", qt % 2, col)])
                else:
                    sm, sk = scm[s % 2], ("scm", s % 2)
                    col = itm["col"]
                    P.add("dve", lambda E: E.tensor_tensor(out=sm[:, 0:nc_], in0=pbf[bank][:, 0:nc_], in1=tri[:, half, 0:nc_],
                                                           op=ALU.add), reads=[("pb", bank), ("tri",)], writes=[sk])
                    P.add("act", lambda E: E.activation(out=pt_[:, 0:nc_], in_=sm[:, 0:nc_], func=AF.Exp, scale=SCALE,
                                                        accum_out=rsb[qt % 2][:, col:col + 1]),
                          reads=[sk], writes=[pk, ("rsb", qt % 2, col)])

            def do_tr(itm):
                s = itm["s"]
                tb = 4 + s % 2
                for c in range(itm["ncols"] // 128):
                    P.add("pe", lambda E, c=c: E.transpose(out=pbh[tb][:, c * 128:(c + 1) * 128],
                                                           in_=Pb[s % 3][:, c * 128:(c + 1) * 128], identity=ident[:]),
                          reads=[("Pb", s % 3), ("ident",)], writes=[("ptp", tb)])

            def do_pv(itm):
                qt, s = itm["qt"], itm["s"]
                i = qt // 2
                tb = 4 + s % 2
                nc_ = itm["ncols"]
                P.add("dve", lambda E: E.tensor_copy(out=PTb[s % 3][:, 0:nc_], in_=pbh[tb][:, 0:nc_]),
                      reads=[("ptp", tb)], writes=[("PTb", s % 3)])
                ob = 6 + qt % 2
                nchunk = nc_ // 128
                for c in range(nchunk):
                    if itm["kind"] == "g":
                        rhs = vh[hb][:, 4 * itm["g"] + c, :]
                        rk_ = ("vh", hb)
                    else:
                        rhs = vo[hb][:, 2 * i + c, :]
                        rk_ = ("vo", hb)
                    st_ = itm["first"] and c == 0
                    sp_ = itm["last"] and c == nchunk - 1
                    P.add("pe", lambda E, c=c, rhs=rhs, st_=st_, sp_=sp_: E.matmul(
                        pbf[ob][:, 0:128], lhsT=PTb[s % 3][:, c * 128:(c + 1) * 128], rhs=rhs, start=st_, stop=sp_),
                        reads=[("PTb", s % 3), rk_], writes=[("ops", ob)])
                if itm["last"]:
                    ncol = itm["col"] + 1
                    P.add("dve", lambda E: E.tensor_reduce(out=rsum[:], in_=rsb[qt % 2][:, 0:ncol], axis=AX.X, op=ALU.add),
                          reads=[("rsb", qt % 2, c_) for c_ in range(ncol)], writes=[("rsum",)])
                    P.add("dve", lambda E: E.reciprocal(out=rinv[:], in_=rsum[:]), reads=[("rsum",)], writes=[("rinv",)])
                    P.add("dve", lambda E: E.tensor_scalar(out=obf[qt % 2][:], in0=pbf[ob][:, 0:128], scalar1=rinv[:, 0:1],
                                                           scalar2=None, op0=ALU.mult),
                          reads=[("ops", ob), ("rinv",)], writes=[("obf", qt % 2)])
                    ot_ = pbh[0][:, 512:640]
                    P.add("pe", lambda E: E.transpose(out=ot_, in_=obf[qt % 2][:], identity=ident[:]),
                          reads=[("obf", qt % 2), ("ident",)], writes=[("pb", 0)])
                    odst = qT(h, qt * 128, 128)
                    P.add("dve", lambda E: E.tensor_copy(out=odst, in_=ot_),
                          reads=[("pb", 0)], writes=[("q", h, qt)])

            n_it = len(items)
            for s_ in range(n_it + 3):
                if s_ < n_it:
                    do_qk(items[s_])
                if 0 <= s_ - 1 < n_it:
                    do_exp(items[s_ - 1])
                if 0 <= s_ - 2 < n_it:
                    do_tr(items[s_ - 2])
                if 0 <= s_ - 3 < n_it:
                    do_pv(items[s_ - 3])
        P.barrier()
        st6.close()

        st7 = ExitStack()
        ga = sb("ga", [128, 8, TOK], BF16, st7)
        hT_stack = ExitStack()
        hT = sb("hT", [128, 8, TOK], BF16, hT_stack)
        st1 = ExitStack()
        P.add("sp", lambda E: E.dma_start(out=hT[:].rearrange("p k t -> p (k t)"), in_=hts), writes=[("hT", tt) for tt in range(NT)], dma=True)
        P.barrier()
        st1.close()
        st7a = ExitStack()
        sz = [sb("sz%d" % i, [128, 512], F32, st7a) for i in range(2)]
        it = 0
        for g in range(2):
            wt, wkey = w_next()
            for cc in range(4):
                c = 4 * g + cc
                for T in range(4):
                    bank = 1 + it % 3
                    proj_fm(wt, wkey, cc, T, bank)
                    s_ = sz[it % 2]
                    P.add("act", lambda E, s_=s_, bank=bank: E.activation(out=s_[:], in_=pbf[bank], func=AF.Silu),
                          reads=[("pb", bank)], writes=[("sz", it % 2)])
                    P.add("dve", lambda E, s_=s_, c=c, T=T: E.tensor_tensor(
                        out=qT(c, T * 512, 512), in0=qT(c, T * 512, 512), in1=s_[:], op=ALU.mult),
                        reads=[("q", c, 4 * T + j) for j in range(4)] + [("sz", it % 2)],
                        writes=[("q", c, 4 * T + j) for j in range(4)])
                    it += 1
        for g in range(2):
            wt, wkey = w_next()
            for cc in range(4):
                c = 4 * g + cc
                for T in range(4):
                    bank = 1 + it % 3
                    proj_fm(wt, wkey, cc, T, bank)
                    P.add("act", lambda E, bank=bank, c=c, T=T: E.activation(
                        out=ga[:, c, T * 512:(T + 1) * 512], in_=pbf[bank], func=AF.Sigmoid, bias=bgate2[:, l, c:c + 1]),
                        reads=[("pb", bank), ("bgate",)], writes=[("ga", c, T)])
                    it += 1
        P.barrier()
        st7a.close()
        hT_stack.close()
        gcyc = sb("gcyc", [128, 8, TOK], BF16, st7)
        tmpf = [sb("tmpf%d" % i, [128, 512], F32, st7) for i in range(2)]
        xt = [sb("xo%d" % i, [128, D], F32, st7) for i in range(2)]
        ot = [sb("ot%d" % i, [128, D], F32, st7) for i in range(2)]
        P.add("sp", lambda E: E.dma_start(out=gcyc[:], in_=gsc), reads=[("gsc",)],
              writes=[("gc", c, T) for c in range(8) for T in range(4)], dma=True)
        for g in range(2):
            wt, wkey = w_next()
            for cc in range(4):
                c = 4 * g + cc
                for T in range(4):
                    bank = 1 + it % 3
                    proj_fm(wt, wkey, cc, T, bank, rhs_fn=lambda kc, T=T: qT(kc, T * 512, 512),
                            rkeys=lambda kc, T=T: [("q", kc, 4 * T + j) for j in range(4)])
                    tf = tmpf[it % 2]
                    P.add("dve", lambda E, tf=tf, bank=bank, c=c, T=T: E.tensor_tensor(
                        out=tf[:], in0=pbf[bank], in1=ga[:, c, T * 512:(T + 1) * 512], op=ALU.mult),
                        reads=[("pb", bank), ("ga", c, T)], writes=[("tmpf", it % 2)])
                    P.add("dve", lambda E, tf=tf, c=c, T=T: E.tensor_tensor(
                        out=gcyc[:, c, T * 512:(T + 1) * 512], in0=tf[:], in1=gcyc[:, c, T * 512:(T + 1) * 512], op=ALU.add),
                        reads=[("tmpf", it % 2), ("gc", c, T)], writes=[("gc", c, T)])
                    it += 1
        wo0, wo0k = w_next()
        wo1, wo1k = w_next(hold=True)
        for tt in range(NT):
            b = tt % 2
            P.add("sp", lambda E, tt=tt, b=b: E.dma_start(out=xt[b][:], in_=x_src[tt * 128:(tt + 1) * 128, :]),
                  writes=[("xo", b)], dma=True)
            for cg, (wt, wkey) in enumerate(((wo0, wo0k), (wo1, wo1k))):
                bank = 1 + it % 3
                for kc in range(8):
                    P.add("pe", lambda E, kc=kc, tt=tt, bank=bank, wt=wt: E.matmul(
                        pbf[bank], lhsT=gcyc[:, kc, tt * 128:(tt + 1) * 128], rhs=wt[:, kc, :],
                        start=(kc == 0), stop=(kc == 7)),
                        reads=[("gc", kc, tt // 4), wkey], writes=[("pb", bank)])
                P.add("dve", lambda E, bank=bank, b=b, cg=cg: E.tensor_tensor(
                    out=ot[b][:, cg * 512:(cg + 1) * 512], in0=pbf[bank], in1=xt[b][:, cg * 512:(cg + 1) * 512], op=ALU.add),
                    reads=[("pb", bank), ("xo", b)], writes=[("ot", b, cg)])
                it += 1
            P.add("pool", lambda E, tt=tt, b=b: E.dma_start(out=y_dst[tt * 128:(tt + 1) * 128, :], in_=ot[b][:]),
                  reads=[("ot", b, 0), ("ot", b, 1)], writes=[("y", tt)], dma=True)
        P.add("sp", None, reads=[("y", tt) for tt in range(NT)])
        P.barrier()
        st7.close()

    layer(0)
    layer(1)
    P.emit(es)
    es.close()
    return nc, P


_CACHE = {}


def _prog():
    if "F" not in _CACHE:
        _CACHE["F"] = build()[0]
    return _CACHE["F"]


def _own_blocks(r):
    return [4 * i + r for i in range(8)]


def _rope_table(r):
    inv_freq = (np.float32(500000.0) ** (-np.arange(0, 32, 2, dtype=np.float32) / np.float32(32))).astype(np.float32)
    pos = np.concatenate([np.arange(b * BLK, (b + 1) * BLK) for b in _own_blocks(r)]).astype(np.float32)
    ang = (pos[:, None] * inv_freq[None, :]).astype(np.float32)
    cs = np.concatenate([np.cos(ang), np.sin(ang)], axis=1).astype(np.float32)
    return np.ascontiguousarray(cs.reshape(NT, 128, 32).transpose(1, 0, 2))


def _consts(r):
    gmask = np.full((8, NB), -1e36, np.float32)
    for i in range(8):
        gmask[i, :4 * i + r] = 0.0
    gmask = np.ascontiguousarray(np.broadcast_to(gmask[None], (128, 8, NB)))
    q = np.arange(128)[:, None]
    k = np.arange(128)[None, :]
    t = np.where(k <= q, 0.0, NEG).astype(np.float32)
    tri = np.zeros((128, 2, 256), np.float32)
    tri[:, 0, :128] = t
    tri[:, 0, 128:] = NEG
    tri[:, 1, 128:] = t
    smask = np.zeros((128, 4), np.float32)
    smask[:, r] = 1.0
    hm = np.zeros((128, 4), np.float32)
    hm[:, (r - 1) % 4] = 1.0
    return gmask, tri, smask, hm


def _pk2(v):
    return np.ascontiguousarray(np.asarray(v, np.float32).reshape(2, 8, 128).transpose(2, 0, 1))


def kernel(**inputs):
    p = {k: np.asarray(v) for k, v in inputs.items()}
    x = np.asarray(p["x"], np.float32)
    f32 = lambda a: np.ascontiguousarray(np.asarray(a, np.float32))
    shared = {
        "w_in": f32(p["w_in"]), "ng": _pk2(p["norm_g"]),
        "qkg": np.ascontiguousarray(np.broadcast_to(
            np.stack([p["q_norm_g"], p["k_norm_g"]], axis=1)[None], (128, 2, 2, 128))).astype(np.float32),
        "bgate": np.ascontiguousarray(np.asarray(p["b_gate"], np.float32).reshape(2, 16, 128).transpose(2, 0, 1)),
        "convw": np.ascontiguousarray(np.asarray(p["conv_w"], np.float32).reshape(2, CK, 8, 128).transpose(3, 0, 2, 1)),
        "cvec": np.ascontiguousarray(np.stack([_pk2(p["conv_b"]), _pk2(p["cn_g"]), _pk2(p["cn_b"])], axis=2)),
        "w_ap": f32(p["w_attn_proj"]), "w_cp": f32(p["w_conv_proj"]), "w_o": f32(p["w_out"]),
    }
    ins = []
    for c in range(8):
        b, r = c // 4, c % 4
        gmask, tri, smask, hm = _consts(r)
        d = dict(shared)
        d.update({"x": np.ascontiguousarray(x[b].reshape(NB, BLK, D)[_own_blocks(r)].reshape(TOK, D)),
                  "cs": _rope_table(r), "gmask": gmask, "tri": tri, "smask": smask, "hm": hm})
        ins.append(d)
    res = run_bass_kernel_spmd(_prog(), ins, core_ids=list(range(8))).results
    out = np.empty((2, NB, BLK, D), np.float32)
    for c in range(8):
        b, r = c // 4, c % 4
        out[b, _own_blocks(r)] = np.asarray(res[c]["y"], np.float32).reshape(8, BLK, D)
    return out.reshape(2, SEQ, D)
```

```python
import numpy as np
import ml_dtypes
from contextlib import ExitStack
import concourse.bass as bass
import concourse.mybir as mybir
from concourse.bass_utils import run_bass_kernel_spmd

F32 = mybir.dt.float32
BF16 = mybir.dt.bfloat16
AF = mybir.ActivationFunctionType
ALU = mybir.AluOpType
AX = mybir.AxisListType
NPBF = ml_dtypes.bfloat16

D = 1024
SEQ = 8192
NB = 32
BLK = 256
NH = 8
HD = 128
NIN = 9216
TOK = 2048
NT = 16
EPS = 1e-6
NEG = -30000.0
SCALE = 1.0 / float(np.sqrt(HD))
CK = 31
GW = 288

C_Q, C_K, C_V, C_ZA, C_UA, C_UB, C_ZC, C_GA, C_GC = 0, 1024, 2048, 3072, 4096, 5120, 6144, 7168, 8192


class _Op:
    __slots__ = ("eng", "fn", "deps", "dma", "need", "ord", "dsem", "dval", "nobar", "cc")


class Prog:
    CE = ("pe", "act", "dve", "pool")
    ALL = ("sp", "pool", "act", "dve", "pe")
    NDS = 8

    def __init__(self, nc):
        self.nc = nc
        self.ops = []
        self.lastw = {}
        self.readers = {}
        self.persist = set()
        self.since_bar = []

    def add(self, eng, fn, reads=(), writes=(), dma=False, nobar=False, cc=False):
        op = _Op()
        op.cc = cc
        op.eng, op.fn, op.dma, op.need, op.ord, op.dsem, op.dval, op.nobar = eng, fn, dma, False, 0, None, 0, nobar
        deps = {}
        for b in reads:
            w = self.lastw.get(b)
            if w is not None:
                deps[id(w)] = w
        for b in writes:
            w = self.lastw.get(b)
            if w is not None:
                deps[id(w)] = w
            for r in self.readers.get(b, ()):
                deps[id(r)] = r
        for b in writes:
            self.lastw[b] = op
            self.readers[b] = []
        for b in reads:
            self.readers.setdefault(b, []).append(op)
        op.deps = [d for d in deps.values()
                   if d is not op and not (d.eng == "pe" and eng == "pe" and not d.dma and not dma)]
        for d in op.deps:
            d.need = True
        self.ops.append(op)
        if not nobar:
            self.since_bar.append(op)
        return op

    def barrier(self):
        last = {}
        dmas = []
        for op in self.since_bar:
            if op.dma:
                dmas.append(op)
            elif op.fn is not None:
                last[op.eng] = op
        deps = list(last.values()) + dmas
        for d in deps:
            d.need = True
        for e in self.ALL:
            op = _Op()
            op.cc = False
            op.eng, op.fn, op.dma, op.need, op.ord, op.dsem, op.dval, op.nobar = e, None, False, False, 0, None, 0, False
            op.deps = list(deps)
            self.ops.append(op)
        self.since_bar = []
        self.lastw = {k: v for k, v in self.lastw.items() if k in self.persist}
        self.readers = {k: v for k, v in self.readers.items() if k in self.persist}

    def emit(self, es):
        nc = self.nc
        sems = {e: es.enter_context(nc.semaphore("c_" + e)) for e in self.CE}
        dsems = {q: [es.enter_context(nc.semaphore("d_%s%d" % (q, j))) for j in range(self.NDS)]
                 for q in ("sp", "pool")}
        cnt = {e: 0 for e in self.CE}
        dcnt = {q: 0 for q in dsems}
        duse = {q: [0] * self.NDS for q in dsems}
        prev_on_sem = {}
        ncc = sum(1 for op in self.ops if op.cc)
        ccsems = [es.enter_context(nc.semaphore("cc%d" % j)) for j in range(ncc)]
        icc = 0
        for op in self.ops:
            if op.cc:
                op.dsem = ccsems[icc]
                op.dval = 1
                icc += 1
            elif op.dma:
                q = op.eng
                j = dcnt[q] % self.NDS
                dcnt[q] += 1
                duse[q][j] += 1
                op.dsem = dsems[q][j]
                op.dval = 16 * duse[q][j]
            elif op.fn is not None and op.need:
                cnt[op.eng] += 1
                op.ord = cnt[op.eng]
        block = es.enter_context(nc.Block())
        bname = {"sp": "sync", "pool": "gpsimd", "act": "scalar", "dve": "vector", "pe": "tensor"}
        ninst = [0]
        for e in self.ALL:
            ops_e = [op for op in self.ops if op.eng == e]

            def body(E, ops_e=ops_e, e=e):
                waited = {}
                for op in ops_e:
                    w = {}
                    for d in op.deps:
                        if d.dma:
                            s, v = d.dsem, d.dval
                        else:
                            s, v = sems[d.eng], d.ord
                        k = id(s)
                        if k not in w or w[k][1] < v:
                            w[k] = (s, v)
                    if op.dma and not op.cc and op.dval > 16:
                        k = id(op.dsem)
                        v = op.dval - 16
                        if k not in w or w[k][1] < v:
                            w[k] = (op.dsem, v)
                    for k, (s, v) in w.items():
                        if waited.get(k, 0) < v:
                            E.wait_ge(s, v)
                            waited[k] = v
                            ninst[0] += 1
                    if op.fn is not None:
                        ins = op.fn(E)
                        ninst[0] += 1
                        if op.cc:
                            ins.then_inc(op.dsem, 1)
                        elif op.dma:
                            ins.then_inc(op.dsem, 16)
                        elif op.need:
                            ins.then_inc(sems[e], 1)

            getattr(block, bname[e])(body)
        self.ninst = ninst[0]


def _ap(t, off, dims, parts=128):
    ps = 1
    for s in list(t.shape)[1:]:
        ps *= int(s)
    return bass.AP(t, off, [[ps, parts]] + [list(d) for d in dims])


KOFF, VOFF, KMOFF, TOFF, GWID = 0, NH * TOK, 2 * NH * TOK, 2 * NH * TOK + 64, 2 * NH * TOK + 64 + 2048
RG = [[0, 1, 2, 3], [4, 5, 6, 7]]


def build():
    nc = bass.Bass("TRN2", target_bir_lowering=False)
    es = ExitStack()
    P = Prog(nc)

    def din(name, shape, dt=F32):
        return nc.dram_tensor(name, list(shape), dt, kind="ExternalInput").ap()

    def dout(name, shape, dt=F32):
        return nc.dram_tensor(name, list(shape), dt, kind="ExternalOutput").ap()

    def dscr(name, shape, dt=F32):
        return nc.dram_tensor(name, list(shape), dt, kind="Internal").ap()

    uniq = [0]

    def sb(name, shape, dt=F32, stack=None):
        uniq[0] += 1
        return (stack or es).enter_context(nc.sbuf_tensor("%s_%d" % (name, uniq[0]), list(shape), dt))

    x_d = din("x", [TOK, D])
    win_d = din("w_in", [2, D, NIN])
    ng_d = din("ng", [128, 2, 8])
    cs_d = din("cs", [128, NT, 32])
    qkg_d = din("qkg", [128, 2, 2, 128])
    gmask_d = din("gmask", [128, 8, NB])
    tri_d = din("tri", [128, 2, 256])
    smask_d = din("smask", [128, 4])
    hm_d = din("hm", [128, 4])
    bg_d = din("bgate", [128, 2, 16])
    cw_d = din("convw", [128, 2, 8, CK])
    cv_d = din("cvec", [128, 2, 3, 8])
    wap_d = din("w_ap", [2, D, D])
    wcp_d = din("w_cp", [2, D, D])
    wo_d = din("w_o", [2, D, D])
    y_d = dout("y", [TOK, D])
    gsc = dscr("gsc", [128, 8, TOK], BF16)
    xs1 = dscr("xs1", [TOK, D])
    NCH = 18
    gin = [[dscr("gin%d_%d" % (l, ch), [512, 2048]) for ch in range(NCH)] for l in range(2)]
    gout = [[dscr("gout%d_%d" % (l, ch), [512, 2048]) for ch in range(NCH)] for l in range(2)]
    hts = dscr("hts", [128, 8 * TOK], BF16)
    ktown_d = dscr("ktown", [128, NH * TOK], BF16)
    vown_d = dscr("vownd", [128, NT * D], BF16)

    pb = [es.enter_context(nc.psum_tensor("pb%d" % i, [128, 512], F32)) for i in range(8)]
    pbf = [pb[i][:] for i in range(8)]
    pbh = [pb[i][:].bitcast(BF16) for i in range(8)]

    identf = sb("identf", [128, 128], F32)
    ident = sb("ident", [128, 128], BF16)
    ng2 = sb("ng2", [128, 2, 8])
    cs = sb("cs", [128, NT, 32])
    qkg2 = sb("qkg2", [128, 2, 2, 128])
    gmask = sb("gmask", [128, 8, NB])
    tri = sb("tri", [128, 2, 256])
    smask = sb("smask", [128, 4])
    hm = sb("hm", [128, 4])
    bgate2 = sb("bgate2", [128, 2, 16])
    cwT2 = sb("cwT2", [128, 2, 8, CK])
    cvec2 = sb("cvec2", [128, 2, 3, 8])
    kmb = sb("kmb", [128, NH, NB], BF16)
    ones_s = sb("ones_s", [128, 128], BF16)
    epsc = sb("epsc", [128, 1])
    NST, NBF = 2, 2
    wst = [sb("wst%d" % i, [128, 8, 512], F32) for i in range(NST)]
    wbf = [sb("wbf%d" % i, [128, 8, 512], BF16) for i in range(NBF)]
    bufA = sb("bufA", [128, 8 * 8 * GW], BF16)
    for i in range(NST):
        P.persist.add(("wst", i))
    for i in range(NBF):
        P.persist.add(("wbf", i))
    P.persist.add(("ng",))
    for l_ in range(2):
        for ch_ in range(NCH):
            P.persist.add(("gout", l_, ch_))

    def qT(h, t0, n):
        return _ap(bufA, h * TOK + t0, [[1, n]])

    def gT(c, blk0, nblk, off, n):
        return _ap(bufA, c * 8 * GW + blk0 * GW + off, [[GW, nblk], [1, n]])

    P.add("pool", lambda E: E.memset(identf[:], 0.0), writes=[("identf",)])
    P.add("pool", lambda E: E.affine_select(out=identf[:], in_=identf[:], compare_op=ALU.not_equal, fill=1.0,
                                            base=0, pattern=[[-1, 128]], channel_multiplier=1),
          reads=[("identf",)], writes=[("identf",)])
    P.add("pool", lambda E: E.tensor_copy(out=ident[:], in_=identf[:]), reads=[("identf",)], writes=[("ident",)])
    P.add("pool", lambda E: E.memset(epsc[:], float(EPS)), writes=[("epsc",)])
    P.add("pool", lambda E: E.memset(ones_s[:], 1.0 / D), writes=[("ones_s",)])
    for (t_, d_, k_) in ((ng2, ng_d, "ng"), (cs, cs_d, "cs"), (qkg2, qkg_d, "qkg"), (gmask, gmask_d, "gmask"),
                         (tri, tri_d, "tri"), (smask, smask_d, "smask"), (hm, hm_d, "hm"), (bgate2, bg_d, "bgate"),
                         (cwT2, cw_d, "cwT"), (cvec2, cv_d, "cvec")):
        P.add("sp", lambda E, t_=t_, d_=d_: E.dma_start(out=t_[:], in_=d_), writes=[(k_,)], dma=True)
    P.barrier()

    wlist = []
    for l_ in range(2):
        wlist += [(l_, "in", C_UA), (l_, "in", C_UB), (l_, "in", C_UA + 512), (l_, "in", C_UB + 512),
                  (l_, "in", C_K), (l_, "in", C_K + 512), (l_, "in", C_V), (l_, "in", C_V + 512),
                  (l_, "in", C_UA), (l_, "in", C_UB), (l_, "in", C_UA + 512), (l_, "in", C_UB + 512),
                  (l_, "in", C_ZC), (l_, "in", C_ZC + 512), (l_, "in", C_GC), (l_, "in", C_GC + 512),
                  (l_, "cp", 0), (l_, "cp", 512), (l_, "in", C_Q), (l_, "in", C_Q + 512),
                  (l_, "in", C_ZA), (l_, "in", C_ZA + 512), (l_, "in", C_GA), (l_, "in", C_GA + 512),
                  (l_, "ap", 0), (l_, "ap", 512), (l_, "o", 0), (l_, "o", 512)]
    wstate = {"emitted": 0, "next": 0}

    def w_load(j):
        lw, kind, c0 = wlist[j]
        base = {"in": win_d, "cp": wcp_d, "ap": wap_d, "o": wo_d}[kind]
        s_ap = base[lw].rearrange("(k p) c -> p k c", p=128)[:, :, c0:c0 + 512]
        st = j % NST
        P.add("sp", lambda E: E.dma_start(out=wst[st][:], in_=s_ap), writes=[("wst", st)], dma=True, nobar=True)

    def w_cast(j):
        lw, kind, c0 = wlist[j]
        st, bf = j % NST, j % NBF
        if kind == "in":
            ngb = _ap(ng2, lw * 8, [[1, 8], [0, 512]])
            P.add("dve", lambda E: E.tensor_tensor(out=wbf[bf][:], in0=wst[st][:], in1=ngb, op=ALU.mult),
                  reads=[("wst", st), ("ng",)], writes=[("wbf", bf)], nobar=True)
        else:
            P.add("act", lambda E: E.activation(out=wbf[bf][:], in_=wst[st][:], func=AF.Copy),
                  reads=[("wst", st)], writes=[("wbf", bf)], nobar=True)

    def w_next(hold=False):
        j = wstate["next"]
        wstate["next"] += 1
        while wstate["emitted"] <= min(len(wlist) - 1, j):
            w_load(wstate["emitted"])
            wstate["emitted"] += 1
        w_cast(j)
        while wstate["emitted"] <= min(len(wlist) - 1, j + 1):
            w_load(wstate["emitted"])
            wstate["emitted"] += 1
        return wbf[j % NBF], ("wbf", j % NBF)


    def layer(l):
        x_src = x_d if l == 0 else xs1
        y_dst = xs1 if l == 0 else y_d
        gin_l, gout_l = gin[l], gout[l]
        def stage_hT(hT, st):
            xt = [sb("xt%d" % i, [128, D], F32, st) for i in range(2)]
            sqj = sb("sqj", [128, D], BF16, st)
            xn = [sb("xn%d" % i, [128, D], BF16, st) for i in range(2)]
            ss = sb("ss", [128, NT], F32, st)
            rs_ = sb("rs_", [128, NT], F32, st)
            for tt in range(NT):
                b = tt % 2
                P.add("sp", lambda E, tt=tt, b=b: E.dma_start(out=xt[b][:], in_=x_src[tt * 128:(tt + 1) * 128, :]),
                      writes=[("xt", b)], dma=True)
                P.add("act", lambda E, tt=tt, b=b: E.activation(out=sqj[:], in_=xt[b][:], func=AF.Square,
                                                                accum_out=ss[:, tt:tt + 1]),
                      reads=[("xt", b)], writes=[("sqj",), ("ss", tt)])
                P.add("act", lambda E, tt=tt: E.activation(out=rs_[:, tt:tt + 1], in_=ss[:, tt:tt + 1], func=AF.Sqrt,
                                                           bias=epsc[:, 0:1], scale=1.0 / D),
                      reads=[("ss", tt), ("epsc",)], writes=[("rs_", tt)])
                P.add("dve", lambda E, tt=tt: E.reciprocal(out=rs_[:, tt:tt + 1], in_=rs_[:, tt:tt + 1]),
                      reads=[("rs_", tt)], writes=[("rs_", tt)])
                P.add("dve", lambda E, tt=tt, b=b: E.tensor_scalar(out=xn[b][:], in0=xt[b][:], scalar1=rs_[:, tt:tt + 1],
                                                                   scalar2=None, op0=ALU.mult),
                      reads=[("xt", b), ("rs_", tt)], writes=[("xn", b)])
                bank = 4 + b
                for c in range(8):
                    P.add("pe", lambda E, c=c, b=b, bank=bank: E.transpose(out=pbh[bank][:, c * 128:(c + 1) * 128],
                                                                           in_=xn[b][:, c * 128:(c + 1) * 128],
                                                                           identity=ident[:]),
                          reads=[("xn", b), ("ident",)], writes=[("pb", bank)])
                P.add("act", lambda E, tt=tt, bank=bank: E.activation(
                    out=_ap(hT, tt * 128, [[TOK, 8], [1, 128]]),
                    in_=pbh[bank].rearrange("p (c t) -> p c t", c=8), func=AF.Copy),
                    reads=[("pb", bank)], writes=[("hT", tt)])

        def stage_qk(hT, st, which, dst_fn, dst_key, after_group=None):
            sqk = sb("sqk", [128, 512], F32, st)
            ssk = sb("ssk", [128, 4], F32, st)
            rk = sb("rk", [128, 4], F32, st)
            kn = sb("kn", [128, 512], F32, st)
            kb = [sb("kb%d" % i, [128, 512], BF16, st) for i in range(2)]
            rt = [sb("rt%d" % i, [128, 4, 16], F32, st) for i in range(4)]
            gvec = _ap(qkg2, (l * 2 + which) * 128, [[0, 4], [1, 128]])
            it = 0
            for kg in range(2):
                wt, wkey = w_next()
                for tt in range(NT):
                    bank = 1 + it % 3
                    for kc in range(8):
                        P.add("pe", lambda E, kc=kc, tt=tt, bank=bank, wt=wt: E.matmul(
                            pbf[bank], lhsT=hT[:, kc, tt * 128:(tt + 1) * 128], rhs=wt[:, kc, :],
                            start=(kc == 0), stop=(kc == 7)),
                            reads=[("hT", tt), wkey], writes=[("pb", bank)])
                    ps3 = pbf[bank].rearrange("p (h d) -> p h d", h=4)
                    P.add("act", lambda E, bank=bank: E.activation(out=sqk[:], in_=pbf[bank], func=AF.Square),
                          reads=[("pb", bank)], writes=[("sqk",)])
                    P.add("dve", lambda E: E.tensor_reduce(out=ssk[:], in_=sqk[:].rearrange("p (h d) -> p h d", h=4),
                                                           axis=AX.X, op=ALU.add),
                          reads=[("sqk",)], writes=[("ssk",)])
                    P.add("act", lambda E: E.activation(out=rk[:], in_=ssk[:], func=AF.Sqrt, bias=epsc[:, 0:1], scale=1.0 / HD),
                          reads=[("ssk",), ("epsc",)], writes=[("rk",)])
                    P.add("dve", lambda E: E.reciprocal(out=rk[:], in_=rk[:]), reads=[("rk",)], writes=[("rk",)])
                    kn3 = kn[:].rearrange("p (h d) -> p h d", h=4)
                    P.add("dve", lambda E, ps3=ps3, kn3=kn3: E.tensor_tensor(
                        out=kn3, in0=ps3, in1=_ap(rk, 0, [[1, 4], [0, 128]]), op=ALU.mult),
                        reads=[("pb", bank), ("rk",)], writes=[("kn",)])
                    P.add("dve", lambda E, kn3=kn3: E.tensor_tensor(out=kn3, in0=kn3, in1=gvec, op=ALU.mult),
                          reads=[("kn",), ("qkg",)], writes=[("kn",)])
                    kbb = kb[it % 2]
                    kbk = ("kb", it % 2)
                    P.add("act", lambda E, kbb=kbb: E.activation(out=kbb[:], in_=kn[:], func=AF.Copy),
                          reads=[("kn",)], writes=[kbk])
                    t1 = _ap(kn, 0, [[128, 4], [1, 16]])
                    t2 = _ap(kn, 16, [[128, 4], [1, 16]])
                    cosb = _ap(cs, tt * 32, [[0, 4], [1, 16]])
                    sinb = _ap(cs, tt * 32 + 16, [[0, 4], [1, 16]])
                    o1 = _ap(kbb, 0, [[128, 4], [1, 16]])
                    o2 = _ap(kbb, 16, [[128, 4], [1, 16]])
                    P.add("dve", lambda E, t1=t1, cosb=cosb: E.tensor_tensor(out=rt[0][:], in0=t1, in1=cosb, op=ALU.mult),
                          reads=[("kn",), ("cs",)], writes=[("rt", 0)])
                    P.add("dve", lambda E, t2=t2, sinb=sinb: E.tensor_tensor(out=rt[1][:], in0=t2, in1=sinb, op=ALU.mult),
                          reads=[("kn",), ("cs",)], writes=[("rt", 1)])
                    P.add("dve", lambda E, t2=t2, cosb=cosb: E.tensor_tensor(out=rt[2][:], in0=t2, in1=cosb, op=ALU.mult),
                          reads=[("kn",), ("cs",)], writes=[("rt", 2)])
                    P.add("dve", lambda E, t1=t1, sinb=sinb: E.tensor_tensor(out=rt[3][:], in0=t1, in1=sinb, op=ALU.mult),
                          reads=[("kn",), ("cs",)], writes=[("rt", 3)])
                    P.add("dve", lambda E, o1=o1: E.tensor_tensor(out=o1, in0=rt[0][:], in1=rt[1][:], op=ALU.subtract),
                          reads=[("rt", 0), ("rt", 1), kbk], writes=[kbk])
                    P.add("dve", lambda E, o2=o2: E.tensor_tensor(out=o2, in0=rt[2][:], in1=rt[3][:], op=ALU.add),
                          reads=[("rt", 2), ("rt", 3), kbk], writes=[kbk])
                    tb = 5 + it % 2
                    for h in range(4):
                        P.add("pe", lambda E, h=h, tb=tb, kbb=kbb: E.transpose(
                            out=pbh[tb][:, h * 128:(h + 1) * 128], in_=kbb[:, h * 128:(h + 1) * 128], identity=ident[:]),
                            reads=[kbk, ("ident",)], writes=[("pb", tb)])
                    P.add("act", lambda E, tb=tb, kg=kg, tt=tt: E.activation(
                        out=dst_fn(kg * 4, tt), in_=pbh[tb][:, 0:512].rearrange("p (h t) -> p h t", h=4), func=AF.Copy),
                        reads=[("pb", tb)], writes=[(dst_key, kg * 4 + h, tt) for h in range(4)])
                    it += 1
                if after_group is not None:
                    after_group(kg)


        hT_stack = ExitStack()
        hT = sb("hT", [128, 8, TOK], BF16, hT_stack)
        st1 = ExitStack()
        stage_hT(hT, st1)
        P.barrier()
        st1.close()
        P.add("sp", lambda E, hT=hT: E.dma_start(out=hts, in_=hT[:].rearrange("p k t -> p (k t)")),
              reads=[("hT", tt) for tt in range(NT)], writes=[("hts",)], dma=True)
        st2 = ExitStack()
        vown = sb("vown", [128, NT, D], BF16, st2)
        kms = sb("kms", [128, 64], F32, st2)
        sgA = sb("sgA", [128, 256], F32, st2)
        gtl = sb("gtl", [128, 8, 256], F32, st2)
        stg = [sb("stg%d" % i, [128, 2048], BF16, st2) for i in range(3)]
        stf = [sb("stf%d" % i, [128, 2048], F32, st2) for i in range(2)]
        skm = sb("skm", [128, 4, 64], F32, st2)
        allq = [("q", h, tt) for h in range(NH) for tt in range(NT)]
        allv = [("vown", tt, vg) for tt in range(NT) for vg in range(2)]
        cidx = [0]

        def gather_chunk(ch):
            gks = []
            for j in range(4):
                rows = slice(j * 128, (j + 1) * 128)
                mj = smask[:, j:j + 1]
                gk = ("gin", l, j, ch)
                if ch < 16:
                    ci = cidx[0]
                    cidx[0] += 1
                    s_ = stg[ci % 3]
                    if ch < 8:
                        src_ap, rk_ = qT(ch, 0, TOK), [("q", ch, tt) for tt in range(NT)]
                        o_ap = s_[:]
                    else:
                        hv = ch - 8
                        src_ap, rk_ = vown[:, :, hv * 128:(hv + 1) * 128], [("vown", tt, hv // 4) for tt in range(NT)]
                        o_ap = s_[:].rearrange("p (t d) -> p t d", d=128)
                    sk = ("stg", ci % 3)
                    if ci % 2 == 0:
                        P.add("act", lambda E, o_ap=o_ap, src_ap=src_ap, mj=mj: E.activation(out=o_ap, in_=src_ap, func=AF.Copy,
                                                                                             scale=mj),
                              reads=rk_ + [("smask",)], writes=[sk])
                    else:
                        P.add("dve", lambda E, o_ap=o_ap, src_ap=src_ap, mj=mj: E.tensor_scalar(out=o_ap, in0=src_ap, scalar1=mj,
                                                                                                scalar2=None, op0=ALU.mult),
                              reads=rk_ + [("smask",)], writes=[sk])
                    P.add("pool", lambda E, s_=s_, rows=rows: E.dma_start(out=gin_l[ch][rows, :], in_=s_[:]),
                          reads=[sk], writes=[gk], dma=True)
                elif ch == 16:
                    f_ = stf[j % 2]
                    P.add("dve", lambda E, f_=f_: E.memset(f_[:, 64:2048], 0.0), writes=[("stf", j % 2)])
                    P.add("dve", lambda E, f_=f_, mj=mj: E.tensor_scalar(out=f_[:, 0:64], in0=kms[:], scalar1=mj, scalar2=None,
                                                                         op0=ALU.mult),
                          reads=[("kms",), ("smask",), ("stf", j % 2)], writes=[("stf", j % 2)])
                    P.add("pool", lambda E, f_=f_, rows=rows: E.dma_start(out=gin_l[ch][rows, :], in_=f_[:]),
                          reads=[("stf", j % 2)], writes=[gk], dma=True)
                else:
                    f_ = stf[j % 2]
                    P.add("dve", lambda E, f_=f_, mj=mj: E.tensor_scalar(out=f_[:], in0=gtl[:].rearrange("p c t -> p (c t)"),
                                                                         scalar1=mj, scalar2=None, op0=ALU.mult),
                          reads=[("gtl", c) for c in range(8)] + [("smask",)], writes=[("stf", j % 2)])
                    P.add("pool", lambda E, f_=f_, rows=rows: E.dma_start(out=gin_l[ch][rows, :], in_=f_[:]),
                          reads=[("stf", j % 2)], writes=[gk], dma=True)
                gks.append(gk)
            P.add("pool", lambda E: E.collective_compute("AllReduce", ALU.add, replica_groups=RG, ins=[gin_l[ch]],
                                                         outs=[gout_l[ch]]),
                  reads=gks, writes=[("gout", l, ch)], dma=True, nobar=True, cc=True)

        for g in range(2):
            wa, wakey = w_next()
            wb, wbkey = w_next(hold=True)
            for cc in range(4):
                c = 4 * g + cc
                for (bank, wt, wkey) in ((1, wa, wakey), (2, wb, wbkey)):
                    for kc in range(8):
                        rhsT = _ap(hT, kc * TOK + 224, [[BLK, 8], [1, 32]])
                        P.add("pe", lambda E, kc=kc, cc=cc, bank=bank, wt=wt, rhsT=rhsT: E.matmul(
                            pbf[bank][:, 0:256], lhsT=wt[:, kc, cc * 128:(cc + 1) * 128],
                            rhs=rhsT,
                            start=(kc == 0), stop=(kc == 7)),
                            reads=[("hT", tt) for tt in range(NT)] + [wkey], writes=[("pb", bank)])
                P.add("act", lambda E: E.activation(out=sgA[:], in_=pbf[2][:, 0:256], func=AF.Sigmoid),
                      reads=[("pb", 2)], writes=[("sg",)])
                P.add("dve", lambda E, c=c: E.tensor_tensor(out=gtl[:, c, :], in0=pbf[1][:, 0:256], in1=sgA[:], op=ALU.mult),
                      reads=[("pb", 1), ("sg",)], writes=[("gtl", c)])
        gather_chunk(17)

        def after_k(kg):
            for h_ in range(4 * kg, 4 * kg + 4):
                gather_chunk(h_)

        stage_qk(hT, st2, 1, lambda h0, tt: _ap(bufA, h0 * TOK + tt * 128, [[TOK, 4], [1, 128]]), "q", after_group=after_k)
        P.add("dve", lambda E: E.tensor_reduce(out=kms[:], in_=_ap(bufA, 0, [[BLK, 64], [1, BLK]]), axis=AX.X, op=ALU.add),
              reads=allq, writes=[("kms",)])
        P.add("dve", lambda E: E.tensor_scalar(out=kms[:], in0=kms[:], scalar1=1.0 / BLK, scalar2=None, op0=ALU.mult),
              reads=[("kms",)], writes=[("kms",)])
        gather_chunk(16)
        it = 0
        for vg in range(2):
            wt, wkey = w_next()
            for tt in range(NT):
                bank = 1 + it % 3
                for kc in range(8):
                    P.add("pe", lambda E, kc=kc, tt=tt, bank=bank, wt=wt, hT=hT: E.matmul(
                        pbf[bank], lhsT=hT[:, kc, tt * 128:(tt + 1) * 128], rhs=wt[:, kc, :],
                        start=(kc == 0), stop=(kc == 7)),
                        reads=[("hT", tt), wkey], writes=[("pb", bank)])
                P.add("act", lambda E, bank=bank, tt=tt, vg=vg: E.activation(
                    out=vown[:, tt, vg * 512:(vg + 1) * 512], in_=pbf[bank], func=AF.Copy),
                    reads=[("pb", bank)], writes=[("vown", tt, vg)])
                it += 1
            for h_ in range(4 * vg, 4 * vg + 4):
                gather_chunk(8 + h_)
        P.add("sp", lambda E: E.dma_start(out=ktown_d, in_=_ap(bufA, 0, [[1, NH * TOK]])), reads=allq, writes=[("ktown",)], dma=True)
        P.add("sp", lambda E: E.dma_start(out=vown_d, in_=vown[:].rearrange("p t d -> p (t d)")), reads=allv, writes=[("vownd",)],
              dma=True)
        P.barrier()
        st2.close()
        def proj_fm(wt, wkey, cc, T, bank, rhs_fn=None, rkeys=None):
            for kc in range(8):
                rhs = hT[:, kc, T * 512:(T + 1) * 512] if rhs_fn is None else rhs_fn(kc)
                rk_ = [("hT", 4 * T + j) for j in range(4)] if rkeys is None else rkeys(kc)
                o_ = pbf[bank] if len(rhs.shape) == 2 else pbf[bank].rearrange("p (b t) -> p b t", b=rhs.shape[1])
                P.add("pe", lambda E, kc=kc, rhs=rhs, o_=o_: E.matmul(o_, lhsT=wt[:, kc, cc * 128:(cc + 1) * 128], rhs=rhs,
                                                                      start=(kc == 0), stop=(kc == 7)),
                      reads=rk_ + [wkey], writes=[("pb", bank)])

        st3 = ExitStack()
        sg = [sb("sg%d" % i, [128, 512], F32, st3) for i in range(2)]
        it = 0
        for g in range(2):
            wa, wakey = w_next()
            wb_, wbkey = w_next(hold=True)
            for cc in range(4):
                c = 4 * g + cc
                for T in range(4):
                    ba, bb = 1 + (it % 2) * 2, 2 + (it % 2) * 2
                    proj_fm(wa, wakey, cc, T, ba)
                    proj_fm(wb_, wbkey, cc, T, bb)
                    s_ = sg[it % 2]
                    P.add("act", lambda E, s_=s_, bb=bb: E.activation(out=s_[:], in_=pbf[bb], func=AF.Sigmoid),
                          reads=[("pb", bb)], writes=[("sg", it % 2)])
                    P.add("dve", lambda E, s_=s_, ba=ba, c=c, T=T: E.tensor_tensor(
                        out=gT(c, 2 * T, 2, 32, 256), in0=pbf[ba].rearrange("p (b t) -> p b t", b=2),
                        in1=s_[:].rearrange("p (b t) -> p b t", b=2), op=ALU.mult),
                        reads=[("pb", ba), ("sg", it % 2)], writes=[("g", c, T)])
                    it += 1
        P.barrier()
        st3.close()

        sth = ExitStack()
        hl4 = sb("hl4", [128, 4, 2048], F32, sth)
        hacc = sb("hacc", [128, 2048], F32, sth)
        kmraw = sb("kmraw", [128, 4, 64], F32, sth)
        P.add("dve", lambda E: E.memset(hl4[:, 3, :], 0.0), writes=[("hl4", 3)])
        for j in range(3):
            P.add("sp", lambda E, j=j: E.dma_start(out=hl4[:, j, :], in_=gout_l[17][j * 128:(j + 1) * 128, :]),
                  reads=[("gout", l, 17)], writes=[("hl4", j)], dma=True)
        P.add("sp", lambda E: E.dma_start(
            out=_ap(hl4, 3 * 2048 + 32, [[256, 8], [1, 224]]),
            in_=gout_l[17][384:512, :].rearrange("p (c t) -> p c t", c=8)[:, :, 0:224]),
            reads=[("gout", l, 17), ("hl4", 3)], writes=[("hl4", 3)], dma=True)
        P.add("dve", lambda E: E.tensor_scalar(out=hacc[:], in0=hl4[:, 0, :], scalar1=hm[:, 0:1], scalar2=None, op0=ALU.mult),
              reads=[("hl4", 0), ("hm",)], writes=[("hacc",)])
        for j in range(1, 4):
            P.add("dve", lambda E, j=j: E.scalar_tensor_tensor(out=hacc[:], in0=hl4[:, j, :], scalar=hm[:, j:j + 1], in1=hacc[:],
                                                               op0=ALU.mult, op1=ALU.add),
                  reads=[("hl4", j), ("hm",), ("hacc",)], writes=[("hacc",)])
        P.add("dve", lambda E: E.tensor_copy(
            out=_ap(bufA, 0, [[GW, 64], [1, 32]]), in_=hacc[:].rearrange("p (ci t) -> p ci t", t=32)),
            reads=[("hacc",)], writes=[("gh",)])
        for rr in range(4):
            P.add("sp", lambda E, rr=rr: E.dma_start(out=kmraw[:, rr, :], in_=gout_l[16][rr * 128:(rr + 1) * 128, 0:64]),
                  reads=[("gout", l, 16)], writes=[("kmraw", rr)], dma=True)
        P.add("dve", lambda E: E.tensor_copy(out=_ap(kmb, 0, [[1, 4], [NB, 8], [4, 8]]),
                                             in_=kmraw[:].rearrange("p r (h i) -> p r h i", h=8)),
              reads=[("kmraw", rr) for rr in range(4)], writes=[("kmb",)])
        P.barrier()
        sth.close()

        st4 = ExitStack()
        gcy4_stack = ExitStack()
        gcy4 = sb("gcyc", [128, 8, TOK], BF16, gcy4_stack)
        cpre = sb("cpre", [128, 8, 512], F32, st4)
        cb = [sb("cb%d" % i, [128, 512], BF16, st4) for i in range(2)]
        csq = [sb("csq%d" % i, [128, 512], BF16, st4) for i in range(2)]
        mean = sb("mean", [128, 512], F32, st4)
        msq = sb("msq", [128, 512], F32, st4)
        rstd = sb("rstd", [128, 512], F32, st4)
        dw = sb("dw", [128, CK, 128], BF16, st4)
        sz = [sb("sz%d" % i, [128, 512], F32, st4) for i in range(2)]
        it = 0
        for T in range(4):
            for c in range(8):
                P.add("dve", lambda E, c=c: E.tensor_tensor(
                    out=dw[:], in0=_ap(identf, 0, [[0, CK], [1, 128]]), in1=_ap(cwT2, (l * 8 + c) * CK, [[1, CK], [0, 128]]),
                    op=ALU.mult), reads=[("identf",), ("cwT",)], writes=[("dw",)])
                bank = 1 + it % 2
                for j in range(CK):
                    P.add("pe", lambda E, j=j, c=c, T=T, bank=bank: E.matmul(
                        pbf[bank].rearrange("p (b t) -> p b t", b=2), lhsT=dw[:, j, :], rhs=gT(c, 2 * T, 2, 2 + j, 256),
                        start=(j == 0), stop=(j == CK - 1)),
                        reads=[("dw",), ("g", c, T), ("gh",)], writes=[("pb", bank)])
                P.add("act", lambda E, c=c, bank=bank: E.activation(out=cpre[:, c, :], in_=pbf[bank], func=AF.Identity,
                                                                    bias=cvec2[:, l, 0, c:c + 1]),
                      reads=[("pb", bank), ("cvec",)], writes=[("cpre", c)])
                P.add("act", lambda E, c=c, bank=bank, k=it % 2: E.activation(out=csq[k][:], in_=pbf[bank], func=AF.Square,
                                                                    bias=cvec2[:, l, 0, c:c + 1]),
                      reads=[("pb", bank), ("cvec",)], writes=[("csq", it % 2)])
                P.add("dve", lambda E, c=c, k=it % 2: E.tensor_copy(out=cb[k][:], in_=cpre[:, c, :]),
                      reads=[("cpre", c)], writes=[("cb", it % 2)])
                P.add("pe", lambda E, c=c, k=it % 2: E.matmul(pbf[3], lhsT=ones_s[:], rhs=cb[k][:], start=(c == 0), stop=(c == 7)),
                      reads=[("ones_s",), ("cb", it % 2)], writes=[("pb", 3)])
                P.add("pe", lambda E, c=c, k=it % 2: E.matmul(pbf[4], lhsT=ones_s[:], rhs=csq[k][:], start=(c == 0), stop=(c == 7)),
                      reads=[("ones_s",), ("csq", it % 2)], writes=[("pb", 4)])
                it += 1
            P.add("dve", lambda E: E.tensor_copy(out=mean[:], in_=pbf[3]), reads=[("pb", 3)], writes=[("mean",)])
            P.add("dve", lambda E: E.tensor_tensor(out=msq[:], in0=mean[:], in1=mean[:], op=ALU.mult),
                  reads=[("mean",)], writes=[("msq",)])
            P.add("dve", lambda E: E.tensor_tensor(out=rstd[:], in0=pbf[4], in1=msq[:], op=ALU.subtract),
                  reads=[("pb", 4), ("msq",)], writes=[("rstd",)])
            P.add("act", lambda E: E.activation(out=rstd[:], in_=rstd[:], func=AF.Sqrt, bias=epsc[:, 0:1]),
                  reads=[("rstd",), ("epsc",)], writes=[("rstd",)])
            P.add("dve", lambda E: E.reciprocal(out=rstd[:], in_=rstd[:]), reads=[("rstd",)], writes=[("rstd",)])
            for c in range(8):
                P.add("dve", lambda E, c=c: E.tensor_tensor(out=cpre[:, c, :], in0=cpre[:, c, :], in1=mean[:], op=ALU.subtract),
                      reads=[("cpre", c), ("mean",)], writes=[("cpre", c)])
                P.add("dve", lambda E, c=c: E.tensor_tensor(out=cpre[:, c, :], in0=cpre[:, c, :], in1=rstd[:], op=ALU.mult),
                      reads=[("cpre", c), ("rstd",)], writes=[("cpre", c)])
                P.add("act", lambda E, c=c, T=T: E.activation(
                    out=gT(c, 2 * T, 2, 32, 256), in_=cpre[:, c, :].rearrange("p (b t) -> p b t", b=2), func=AF.Silu,
                    bias=cvec2[:, l, 2, c:c + 1], scale=cvec2[:, l, 1, c:c + 1]),
                    reads=[("cpre", c), ("cvec",)], writes=[("g", c, T)])
        it = 0
        for g in range(2):
            wt, wkey = w_next()
            for cc in range(4):
                c = 4 * g + cc
                for T in range(4):
                    bank = 5 + it % 2
                    proj_fm(wt, wkey, cc, T, bank)
                    s_ = sz[it % 2]
                    P.add("act", lambda E, s_=s_, bank=bank: E.activation(out=s_[:], in_=pbf[bank], func=AF.Silu),
                          reads=[("pb", bank)], writes=[("sz", it % 2)])
                    P.add("dve", lambda E, s_=s_, c=c, T=T: E.tensor_tensor(
                        out=gT(c, 2 * T, 2, 32, 256), in0=gT(c, 2 * T, 2, 32, 256),
                        in1=s_[:].rearrange("p (b t) -> p b t", b=2), op=ALU.mult),
                        reads=[("g", c, T), ("sz", it % 2)], writes=[("g", c, T)])
                    it += 1
        for g in range(2):
            wt, wkey = w_next()
            for cc in range(4):
                c = 4 * g + cc
                for T in range(4):
                    bank = 5 + it % 2
                    proj_fm(wt, wkey, cc, T, bank)
                    P.add("act", lambda E, bank=bank, c=c, T=T: E.activation(
                        out=gcy4[:, c, T * 512:(T + 1) * 512], in_=pbf[bank], func=AF.Sigmoid, bias=bgate2[:, l, 8 + c:9 + c]),
                        reads=[("pb", bank), ("bgate",)], writes=[("gc", c, T)])
                    it += 1
        for g in range(2):
            wt, wkey = w_next()
            for cc in range(4):
                c = 4 * g + cc
                for T in range(4):
                    bank = 5 + it % 2
                    proj_fm(wt, wkey, cc, T, bank, rhs_fn=lambda kc, T=T: gT(kc, 2 * T, 2, 32, 256),
                            rkeys=lambda kc, T=T: [("g", kc, T)])
                    P.add("dve", lambda E, bank=bank, c=c, T=T: E.tensor_tensor(
                        out=gcy4[:, c, T * 512:(T + 1) * 512], in0=pbf[bank], in1=gcy4[:, c, T * 512:(T + 1) * 512],
                        op=ALU.mult), reads=[("pb", bank), ("gc", c, T)], writes=[("gc", c, T)])
                    it += 1
        P.add("pool", lambda E: E.dma_start(out=gsc, in_=gcy4[:]),
              reads=[("gc", c, T) for c in range(8) for T in range(4)], writes=[("gsc",)], dma=True)
        P.barrier()
        st4.close()
        gcy4_stack.close()

        st5 = ExitStack()
        stage_qk(hT, st5, 0, lambda h0, tt: _ap(bufA, h0 * TOK + tt * 128, [[TOK, 4], [1, 128]]), "q")
        P.barrier()
        st5.close()
        hT_stack.close()

        st6 = ExitStack()
        kth = [sb("kth%d" % i, [128, SEQ], BF16, st6) for i in range(2)]
        vh = [sb("vh%d" % i, [128, 64, 128], BF16, st6) for i in range(2)]
        kto = [sb("kto%d" % i, [128, TOK], BF16, st6) for i in range(2)]
        vo = [sb("vo%d" % i, [128, NT, 128], BF16, st6) for i in range(2)]
        gm = sb("gm", [128, NB], F32, st6)
        m8 = sb("m8", [128, 8], F32, st6)
        thr = sb("thr", [128, 1], F32, st6)
        biasb = [sb("biasb%d" % i, [128, NB], F32, st6) for i in range(2)]
        rsb = [sb("rsb%d" % i, [128, 40], F32, st6) for i in range(2)]
        Pb = [sb("Pb%d" % i, [128, 512], BF16, st6) for i in range(3)]
        PTb = [sb("PTb%d" % i, [128, 512], BF16, st6) for i in range(3)]
        scm = [sb("scm%d" % i, [128, 256], F32, st6) for i in range(2)]
        rsum = sb("rsum", [128, 1], F32, st6)
        rinv = sb("rinv", [128, 1], F32, st6)
        obf = [sb("obf%d" % i, [128, 128], BF16, st6) for i in range(2)]

        def load_head(h):
            hb = h % 2
            for rr in range(4):
                P.add("pool", lambda E, rr=rr: E.dma_start(
                    out=_ap(kth[hb], rr * BLK, [[4 * BLK, 8], [1, BLK]]),
                    in_=gout_l[h][rr * 128:(rr + 1) * 128, :].rearrange("p (i t) -> p i t", t=BLK)),
                    reads=[("gout", l, h)], writes=[("kth", hb)], dma=True)
            for rr in range(4):
                P.add("pool", lambda E, rr=rr: E.dma_start(
                    out=_ap(vh[hb], rr * 256, [[1024, 8], [1, 256]]),
                    in_=gout_l[8 + h][rr * 128:(rr + 1) * 128, :].rearrange("p (i t) -> p i t", t=256)),
                    reads=[("gout", l, 8 + h)], writes=[("vh", hb)], dma=True)
            P.add("sp", lambda E: E.dma_start(out=kto[hb][:], in_=ktown_d[:, h * TOK:(h + 1) * TOK]), writes=[("kto", hb)], dma=True)
            vsrc = vown_d.rearrange("p (t d) -> p t d", d=D)
            for a in range(2):
                P.add("sp", lambda E, a=a: E.dma_start(out=vo[hb][:, 8 * a:8 * a + 8, :],
                                                       in_=vsrc[:, 8 * a:8 * a + 8, h * 128:(h + 1) * 128]),
                      writes=[("vo", hb)], dma=True)

        load_head(0)
        sidx = [0]
        for h in range(NH):
            hb = h % 2
            if h + 1 < NH:
                load_head(h + 1)
            items = []
            for qt in range(NT):
                i, half = qt // 2, qt % 2
                nb = 4 * i + 3
                col = 0
                ng_ = (nb + 1) // 2
                for g in range(ng_):
                    nblk = min(2, nb - 2 * g)
                    items.append(dict(qt=qt, kind="g", g=g, nblk=nblk, ncols=256 * nblk, first=(g == 0), last=False, col=col))
                    col += nblk
                items.append(dict(qt=qt, kind="d", ncols=128 * (half + 1), first=False, last=True, col=col))
            for it_ in items:
                it_["s"] = sidx[0]
                sidx[0] += 1

            def do_qk(itm):
                qt, s = itm["qt"], itm["s"]
                i, half = qt // 2, qt % 2
                qslice = qT(h, qt * 128, 128)
                kmh = kmb[:, h, :]
                if itm["first"]:
                    P.add("pe", lambda E: E.matmul(pbf[0][:, 0:NB], lhsT=qslice, rhs=kmh, start=True, stop=True),
                          reads=[("q", h, qt), ("kmb",)], writes=[("pb", 0)])
                    P.add("dve", lambda E: E.tensor_tensor(out=gm[:], in0=pbf[0][:, 0:NB], in1=gmask[:, i, :], op=ALU.add),
                          reads=[("pb", 0), ("gmask",)], writes=[("gm",)])
                    P.add("dve", lambda E: E.max(out=m8[:], in_=gm[:]), reads=[("gm",)], writes=[("m8",)])
                    P.add("dve", lambda E: E.tensor_scalar(out=thr[:], in0=m8[:, 2:3], scalar1=-1e30, scalar2=None, op0=ALU.max),
                          reads=[("m8",)], writes=[("thr",)])
                    P.add("dve", lambda E: E.tensor_scalar(out=biasb[qt % 2][:], in0=gm[:], scalar1=thr[:, 0:1], scalar2=NEG,
                                                           op0=ALU.is_lt, op1=ALU.mult),
                          reads=[("gm",), ("thr",)], writes=[("biasb", qt % 2)])
                bank = 1 + s % 3
                nc_ = itm["ncols"]
                if itm["kind"] == "g":
                    rhs = kth[hb][:, 512 * itm["g"]:512 * itm["g"] + nc_]
                    rk_ = ("kth", hb)
                else:
                    rhs = kto[hb][:, i * BLK:i * BLK + nc_]
                    rk_ = ("kto", hb)
                P.add("pe", lambda E: E.matmul(pbf[bank][:, 0:nc_], lhsT=qslice, rhs=rhs, start=True, stop=True),
                      reads=[("q", h, qt), rk_], writes=[("pb", bank)])

            def do_exp(itm):
                qt, s = itm["qt"], itm["s"]
                half = qt % 2
                bank = 1 + s % 3
                pt_, pk = Pb[s % 3], ("Pb", s % 3)
                nc_ = itm["ncols"]
                if itm["kind"] == "g":
                    for b_ in range(itm["nblk"]):
                        n = 2 * itm["g"] + b_
                        col = itm["col"] + b_
                        P.add("act", lambda E, b_=b_, n=n, col=col: E.activation(
                            out=pt_[:, b_ * 256:(b_ + 1) * 256], in_=pbf[bank][:, b_ * 256:(b_ + 1) * 256], func=AF.Exp,
                            bias=biasb[qt % 2][:, n:n + 1], scale=SCALE, accum_out=rsb[qt % 2][:, col:col + 1]),
                            reads=[("pb", bank), ("biasb", qt % 2)], writes=[pk, ("rsb", qt % 2, col)])
                else:
                    sm, sk = scm[s % 2], ("scm", s % 2)
                    col = itm["col"]
                    P.add("dve", lambda E: E.tensor_tensor(out=sm[:, 0:nc_], in0=pbf[bank][:, 0:nc_], in1=tri[:, half, 0:nc_],
                                                           op=ALU.add), reads=[("pb", bank), ("tri",)], writes=[sk])
                    P.add("act", lambda E: E.activation(out=pt_[:, 0:nc_], in_=sm[:, 0:nc_], func=AF.Exp, scale=SCALE,
                                                        accum_out=rsb[qt % 2][:, col:col + 1]),
                          reads=[sk], writes=[pk, ("rsb", qt % 2, col)])

            def do_tr(itm):
                s = itm["s"]
                tb = 4 + s % 2
                for c in range(itm["ncols"] // 128):
                    P.add("pe", lambda E, c=c: E.transpose(out=pbh[tb][:, c * 128:(c + 1) * 128],
                                                           in_=Pb[s % 3][:, c * 128:(c + 1) * 128], identity=ident[:]),
                          reads=[("Pb", s % 3), ("ident",)], writes=[("ptp", tb)])

            def do_pv(itm):
                qt, s = itm["qt"], itm["s"]
                i = qt // 2
                tb = 4 + s % 2
                nc_ = itm["ncols"]
                P.add("dve", lambda E: E.tensor_copy(out=PTb[s % 3][:, 0:nc_], in_=pbh[tb][:, 0:nc_]),
                      reads=[("ptp", tb)], writes=[("PTb", s % 3)])
                ob = 6 + qt % 2
                nchunk = nc_ // 128
                for c in range(nchunk):
                    if itm["kind"] == "g":
                        rhs = vh[hb][:, 4 * itm["g"] + c, :]
                        rk_ = ("vh", hb)
                    else:
                        rhs = vo[hb][:, 2 * i + c, :]
                        rk_ = ("vo", hb)
                    st_ = itm["first"] and c == 0
                    sp_ = itm["last"] and c == nchunk - 1
                    P.add("pe", lambda E, c=c, rhs=rhs, st_=st_, sp_=sp_: E.matmul(
                        pbf[ob][:, 0:128], lhsT=PTb[s % 3][:, c * 128:(c + 1) * 128], rhs=rhs, start=st_, stop=sp_),
                        reads=[("PTb", s % 3), rk_], writes=[("ops", ob)])
                if itm["last"]:
                    ncol = itm["col"] + 1
                    P.add("dve", lambda E: E.tensor_reduce(out=rsum[:], in_=rsb[qt % 2][:, 0:ncol], axis=AX.X, op=ALU.add),
                          reads=[("rsb", qt % 2, c_) for c_ in range(ncol)], writes=[("rsum",)])
                    P.add("dve", lambda E: E.reciprocal(out=rinv[:], in_=rsum[:]), reads=[("rsum",)], writes=[("rinv",)])
                    P.add("dve", lambda E: E.tensor_scalar(out=obf[qt % 2][:], in0=pbf[ob][:, 0:128], scalar1=rinv[:, 0:1],
                                                           scalar2=None, op0=ALU.mult),
                          reads=[("ops", ob), ("rinv",)], writes=[("obf", qt % 2)])
                    ot_ = pbh[0][:, 512:640]
                    P.add("pe", lambda E: E.transpose(out=ot_, in_=obf[qt % 2][:], identity=ident[:]),
                          reads=[("obf", qt % 2), ("ident",)], writes=[("pb", 0)])
                    odst = qT(h, qt * 128, 128)
                    P.add("dve", lambda E: E.tensor_copy(out=odst, in_=ot_),
                          reads=[("pb", 0)], writes=[("q", h, qt)])

            n_it = len(items)
            for s_ in range(n_it + 3):
                if s_ < n_it:
                    do_qk(items[s_])
                if 0 <= s_ - 1 < n_it:
                    do_exp(items[s_ - 1])
                if 0 <= s_ - 2 < n_it:
                    do_tr(items[s_ - 2])
                if 0 <= s_ - 3 < n_it:
                    do_pv(items[s_ - 3])
        P.barrier()
        st6.close()

        st7 = ExitStack()
        ga = sb("ga", [128, 8, TOK], BF16, st7)
        hT_stack = ExitStack()
        hT = sb("hT", [128, 8, TOK], BF16, hT_stack)
        st1 = ExitStack()
        P.add("sp", lambda E: E.dma_start(out=hT[:].rearrange("p k t -> p (k t)"), in_=hts), writes=[("hT", tt) for tt in range(NT)], dma=True)
        P.barrier()
        st1.close()
        st7a = ExitStack()
        sz = [sb("sz%d" % i, [128, 512], F32, st7a) for i in range(2)]
        it = 0
        for g in range(2):
            wt, wkey = w_next()
            for cc in range(4):
                c = 4 * g + cc
                for T in range(4):
                    bank = 1 + it % 3
                    proj_fm(wt, wkey, cc, T, bank)
                    s_ = sz[it % 2]
                    P.add("act", lambda E, s_=s_, bank=bank: E.activation(out=s_[:], in_=pbf[bank], func=AF.Silu),
                          reads=[("pb", bank)], writes=[("sz", it % 2)])
                    P.add("dve", lambda E, s_=s_, c=c, T=T: E.tensor_tensor(
                        out=qT(c, T * 512, 512), in0=qT(c, T * 512, 512), in1=s_[:], op=ALU.mult),
                        reads=[("q", c, 4 * T + j) for j in range(4)] + [("sz", it % 2)],
                        writes=[("q", c, 4 * T + j) for j in range(4)])
                    it += 1
        for g in range(2):
            wt, wkey = w_next()
            for cc in range(4):
                c = 4 * g + cc
                for T in range(4):
                    bank = 1 + it % 3
                    proj_fm(wt, wkey, cc, T, bank)
                    P.add("act", lambda E, bank=bank, c=c, T=T: E.activation(
                        out=ga[:, c, T * 512:(T + 1) * 512], in_=pbf[bank], func=AF.Sigmoid, bias=bgate2[:, l, c:c + 1]),
                        reads=[("pb", bank), ("bgate",)], writes=[("ga", c, T)])
                    it += 1
        P.barrier()
        st7a.close()
        hT_stack.close()
        gcyc = sb("gcyc", [128, 8, TOK], BF16, st7)
        tmpf = [sb("tmpf%d" % i, [128, 512], F32, st7) for i in range(2)]
        xt = [sb("xo%d" % i, [128, D], F32, st7) for i in range(2)]
        ot = [sb("ot%d" % i, [128, D], F32, st7) for i in range(2)]
        P.add("sp", lambda E: E.dma_start(out=gcyc[:], in_=gsc), reads=[("gsc",)],
              writes=[("gc", c, T) for c in range(8) for T in range(4)], dma=True)
        for g in range(2):
            wt, wkey = w_next()
            for cc in range(4):
                c = 4 * g + cc
                for T in range(4):
                    bank = 1 + it % 3
                    proj_fm(wt, wkey, cc, T, bank, rhs_fn=lambda kc, T=T: qT(kc, T * 512, 512),
                            rkeys=lambda kc, T=T: [("q", kc, 4 * T + j) for j in range(4)])
                    tf = tmpf[it % 2]
                    P.add("dve", lambda E, tf=tf, bank=bank, c=c, T=T: E.tensor_tensor(
                        out=tf[:], in0=pbf[bank], in1=ga[:, c, T * 512:(T + 1) * 512], op=ALU.mult),
                        reads=[("pb", bank), ("ga", c, T)], writes=[("tmpf", it % 2)])
                    P.add("dve", lambda E, tf=tf, c=c, T=T: E.tensor_tensor(
                        out=gcyc[:, c, T * 512:(T + 1) * 512], in0=tf[:], in1=gcyc[:, c, T * 512:(T + 1) * 512], op=ALU.add),
                        reads=[("tmpf", it % 2), ("gc", c, T)], writes=[("gc", c, T)])
                    it += 1
        wo0, wo0k = w_next()
        wo1, wo1k = w_next(hold=True)
        for tt in range(NT):
            b = tt % 2
            P.add("sp", lambda E, tt=tt, b=b: E.dma_start(out=xt[b][:], in_=x_src[tt * 128:(tt + 1) * 128, :]),
                  writes=[("xo", b)], dma=True)
            for cg, (wt, wkey) in enumerate(((wo0, wo0k), (wo1, wo1k))):
                bank = 1 + it % 3
                for kc in range(8):
                    P.add("pe", lambda E, kc=kc, tt=tt, bank=bank, wt=wt: E.matmul(
                        pbf[bank], lhsT=gcyc[:, kc, tt * 128:(tt + 1) * 128], rhs=wt[:, kc, :],
                        start=(kc == 0), stop=(kc == 7)),
                        reads=[("gc", kc, tt // 4), wkey], writes=[("pb", bank)])
                P.add("dve", lambda E, bank=bank, b=b, cg=cg: E.tensor_tensor(
                    out=ot[b][:, cg * 512:(cg + 1) * 512], in0=pbf[bank], in1=xt[b][:, cg * 512:(cg + 1) * 512], op=ALU.add),
                    reads=[("pb", bank), ("xo", b)], writes=[("ot", b, cg)])
                it += 1
            P.add("pool", lambda E, tt=tt, b=b: E.dma_start(out=y_dst[tt * 128:(tt + 1) * 128, :], in_=ot[b][:]),
                  reads=[("ot", b, 0), ("ot", b, 1)], writes=[("y", tt)], dma=True)
        P.add("sp", None, reads=[("y", tt) for tt in range(NT)])
        P.barrier()
        st7.close()

    layer(0)
    layer(1)
    P.emit(es)
    es.close()
    return nc, P


_CACHE = {}


def _prog():
    if "F" not in _CACHE:
        _CACHE["F"] = build()[0]
    return _CACHE["F"]


def _own_blocks(r):
    return [4 * i + r for i in range(8)]


def _rope_table(r):
    inv_freq = (np.float32(500000.0) ** (-np.arange(0, 32, 2, dtype=np.float32) / np.float32(32))).astype(np.float32)
    pos = np.concatenate([np.arange(b * BLK, (b + 1) * BLK) for b in _own_blocks(r)]).astype(np.float32)
    ang = (pos[:, None] * inv_freq[None, :]).astype(np.float32)
    cs = np.concatenate([np.cos(ang), np.sin(ang)], axis=1).astype(np.float32)
    return np.ascontiguousarray(cs.reshape(NT, 128, 32).transpose(1, 0, 2))


def _consts(r):
    gmask = np.full((8, NB), -1e36, np.float32)
    for i in range(8):
        gmask[i, :4 * i + r] = 0.0
    gmask = np.ascontiguousarray(np.broadcast_to(gmask[None], (128, 8, NB)))
    q = np.arange(128)[:, None]
    k = np.arange(128)[None, :]
    t = np.where(k <= q, 0.0, NEG).astype(np.float32)
    tri = np.zeros((128, 2, 256), np.float32)
    tri[:, 0, :128] = t
    tri[:, 0, 128:] = NEG
    tri[:, 1, 128:] = t
    smask = np.zeros((128, 4), np.float32)
    smask[:, r] = 1.0
    hm = np.zeros((128, 4), np.float32)
    hm[:, (r - 1) % 4] = 1.0
    return gmask, tri, smask, hm


def _pk2(v):
    return np.ascontiguousarray(np.asarray(v, np.float32).reshape(2, 8, 128).transpose(2, 0, 1))


def kernel(**inputs):
    p = {k: np.asarray(v) for k, v in inputs.items()}
    x = np.asarray(p["x"], np.float32)
    f32 = lambda a: np.ascontiguousarray(np.asarray(a, np.float32))
    shared = {
        "w_in": f32(p["w_in"]), "ng": _pk2(p["norm_g"]),
        "qkg": np.ascontiguousarray(np.broadcast_to(
            np.stack([p["q_norm_g"], p["k_norm_g"]], axis=1)[None], (128, 2, 2, 128))).astype(np.float32),
        "bgate": np.ascontiguousarray(np.asarray(p["b_gate"], np.float32).reshape(2, 16, 128).transpose(2, 0, 1)),
        "convw": np.ascontiguousarray(np.asarray(p["conv_w"], np.float32).reshape(2, CK, 8, 128).transpose(3, 0, 2, 1)),
        "cvec": np.ascontiguousarray(np.stack([_pk2(p["conv_b"]), _pk2(p["cn_g"]), _pk2(p["cn_b"])], axis=2)),
        "w_ap": f32(p["w_attn_proj"]), "w_cp": f32(p["w_conv_proj"]), "w_o": f32(p["w_out"]),
    }
    ins = []
    for c in range(8):
        b, r = c // 4, c % 4
        gmask, tri, smask, hm = _consts(r)
        d = dict(shared)
        d.update({"x": np.ascontiguousarray(x[b].reshape(NB, BLK, D)[_own_blocks(r)].reshape(TOK, D)),
                  "cs": _rope_table(r), "gmask": gmask, "tri": tri, "smask": smask, "hm": hm})
        ins.append(d)
    res = run_bass_kernel_spmd(_prog(), ins, core_ids=list(range(8))).results
    out = np.empty((2, NB, BLK, D), np.float32)
    for c in range(8):
        b, r = c // 4, c % 4
        out[b, _own_blocks(r)] = np.asarray(res[c]["y"], np.float32).reshape(8, BLK, D)
    return out.reshape(2, SEQ, D)
```
